# Optimizing a Trainium2 kernel written in Bass

```python
import math
import jax, jax.numpy as jnp
from jax import lax
import numpy as np

D_MODEL = 1024
BATCH = 8
SEQ = 2048
DEPTH = 4

CHUNK = 64
N_MIXERS = 3
A_HEADS = 16
A_HEAD_DIM = D_MODEL // A_HEADS
A_LEFT_CHUNKS = 8
A_BAND = (A_LEFT_CHUNKS + 1) * CHUNK
A_REL_CLIP = 128
B_HEADS = 4
B_KEY_DIM = D_MODEL // 2
B_VAL_DIM = D_MODEL
B_HK = B_KEY_DIM // B_HEADS
B_HV = B_VAL_DIM // B_HEADS
B_GATE_RANK = 16
B_GATE_TAU = 16.0
C_HEADS = 8
C_HEAD_DIM = D_MODEL // (2 * C_HEADS)
C_Q_BLOCK = 128
T5_BUCKETS = 32
T5_MAX_DIST = 128
FFN_HIDDEN = ((8 * D_MODEL + 3 * 256 - 1) // (3 * 256)) * 256
DN_ALPHA = (2.0 * DEPTH) ** 0.25
DN_BETA = (8.0 * DEPTH) ** -0.25
N_A = (DEPTH + 2) // 3
N_B = (DEPTH + 1) // 3
N_C = DEPTH // 3
LN_EPS = 1e-5
RMS_EPS = 1e-6

kernel_name = "hybrid_chunk_causal_deepnorm_trunk"


def layer_norm(x, g, b):
    xf = x.astype(jnp.float32)
    mu = jnp.mean(xf, axis=-1, keepdims=True)
    var = jnp.mean(jnp.square(xf - mu), axis=-1, keepdims=True)
    return ((xf - mu) * lax.rsqrt(var + LN_EPS) * g.astype(jnp.float32) + b.astype(jnp.float32)).astype(x.dtype)


def rms_norm(x, g):
    xf = x.astype(jnp.float32)
    ms = jnp.mean(jnp.square(xf), axis=-1, keepdims=True)
    return (xf * lax.rsqrt(ms + RMS_EPS) * g.astype(jnp.float32)).astype(x.dtype)


def t5_bucket(rel):
    nb = T5_BUCKETS // 2
    max_exact = nb // 2
    ret = (rel > 0).astype(jnp.int32) * nb
    n = jnp.abs(rel)
    is_small = n < max_exact
    n_f = jnp.maximum(n, 1).astype(jnp.float32)
    large = max_exact + (jnp.log(n_f / max_exact) / math.log(T5_MAX_DIST / max_exact) * (nb - max_exact)).astype(jnp.int32)
    large = jnp.minimum(large, nb - 1)
    return ret + jnp.where(is_small, n, large)


def chunk_band_attention(x, w_qkv, rel_bias, w_o):
    b, s, _ = x.shape
    nc = s // CHUNK
    q, k, v = jnp.split(x @ w_qkv, 3, axis=-1)
    q = q.reshape(b, s, A_HEADS, A_HEAD_DIM) * (A_HEAD_DIM ** -0.5)
    k = k.reshape(b, s, A_HEADS, A_HEAD_DIM)
    v = v.reshape(b, s, A_HEADS, A_HEAD_DIM)
    pad = A_LEFT_CHUNKS * CHUNK
    kp = jnp.pad(k, ((0, 0), (pad, 0), (0, 0), (0, 0)))
    vp = jnp.pad(v, ((0, 0), (pad, 0), (0, 0), (0, 0)))
    q_off = jnp.arange(CHUNK)
    k_off = jnp.arange(A_BAND)
    rel = (q_off[:, None] + pad) - k_off[None, :]
    bias = rel_bias[jnp.clip(rel, -A_REL_CLIP, A_REL_CLIP) + A_REL_CLIP]
    bias = jnp.transpose(bias, (2, 0, 1)).astype(jnp.float32)

    def one_chunk(c):
        start = c * CHUNK
        qc = lax.dynamic_slice_in_dim(q, start, CHUNK, axis=1)
        kc = lax.dynamic_slice_in_dim(kp, start, A_BAND, axis=1)
        vc = lax.dynamic_slice_in_dim(vp, start, A_BAND, axis=1)
        logits = jnp.einsum('bqhd,bkhd->bhqk', qc, kc).astype(jnp.float32) + bias[None]
        valid = (start - pad + k_off) >= 0
        logits = jnp.where(valid[None, None, None, :], logits, -jnp.inf)
        p = jax.nn.softmax(logits, axis=-1).astype(vc.dtype)
        return jnp.einsum('bhqk,bkhd->bqhd', p, vc)

    out = lax.map(one_chunk, jnp.arange(nc))
    out = jnp.transpose(out, (1, 0, 2, 3, 4)).reshape(b, s, D_MODEL)
    return out @ w_o


def gated_linear_attention(x, w_in, w_g1, w_g2, b_g, g_norm, w_o):
    b, s, _ = x.shape
    nc = s // CHUNK
    q, k, v, r = jnp.split(x @ w_in, [B_KEY_DIM, 2 * B_KEY_DIM, 2 * B_KEY_DIM + B_VAL_DIM], axis=-1)
    log_a = jax.nn.log_sigmoid(((x @ w_g1) @ w_g2 + b_g).astype(jnp.float32)) / B_GATE_TAU
    q = q.reshape(b, nc, CHUNK, B_HEADS, B_HK) * (B_HK ** -0.5)
    k = k.reshape(b, nc, CHUNK, B_HEADS, B_HK)
    v = v.reshape(b, nc, CHUNK, B_HEADS, B_HV)
    log_a = log_a.reshape(b, nc, CHUNK, B_HEADS, B_HK)
    cum = jnp.cumsum(log_a, axis=2)
    total = cum[:, :, -1:]
    k_dec = k * jnp.exp(total - cum).astype(k.dtype)
    chunk_kv = jnp.einsum('bnchk,bnchv->bnhkv', k_dec, v)
    chunk_decay = jnp.exp(total[:, :, 0]).astype(chunk_kv.dtype)

    def step(state, inp):
        dec, kv = inp
        state = dec[..., None] * state + kv
        return state, state

    init = jnp.zeros((b, B_HEADS, B_HK, B_HV), chunk_kv.dtype)
    _, states = lax.scan(step, init, (jnp.moveaxis(chunk_decay, 1, 0), jnp.moveaxis(chunk_kv, 1, 0)))
    o = jnp.einsum('bnchk,nbhkv->bnchv', q, states)
    o = rms_norm(o, g_norm).reshape(b, s, B_VAL_DIM) * jax.nn.silu(r)
    return o @ w_o


def differential_attention(x, w_qkv, lam_q1, lam_k1, lam_q2, lam_k2, g_norm, w_o, t5_table, lambda_init):
    b, s, _ = x.shape
    q, k, v = jnp.split(x @ w_qkv, 3, axis=-1)
    q = q.reshape(b, s, C_HEADS, 2, C_HEAD_DIM) * (C_HEAD_DIM ** -0.5)
    k = k.reshape(b, s, C_HEADS, 2, C_HEAD_DIM)
    v = v.reshape(b, s, C_HEADS, 2 * C_HEAD_DIM)
    lam = (jnp.exp(jnp.sum(lam_q1.astype(jnp.float32) * lam_k1.astype(jnp.float32)))
           - jnp.exp(jnp.sum(lam_q2.astype(jnp.float32) * lam_k2.astype(jnp.float32))) + lambda_init)
    k_pos = jnp.arange(s)
    k_chunk = k_pos // CHUNK

    def one_block(blk):
        start = blk * C_Q_BLOCK
        qb = lax.dynamic_slice_in_dim(q, start, C_Q_BLOCK, axis=1)
        q_pos = start + jnp.arange(C_Q_BLOCK)
        logits = jnp.einsum('bqhmd,bkhmd->bhmqk', qb, k).astype(jnp.float32)
        bias = t5_table[t5_bucket(k_pos[None, :] - q_pos[:, None])]
        logits = logits + jnp.transpose(bias, (2, 0, 1)).astype(jnp.float32)[None, :, None]
        allowed = k_chunk[None, :] <= (q_pos // CHUNK)[:, None]
        logits = jnp.where(allowed[None, None, None], logits, -jnp.inf)
        p = jax.nn.softmax(logits, axis=-1)
        diff = (p[:, :, 0] - lam * p[:, :, 1]).astype(v.dtype)
        return jnp.einsum('bhqk,bkhe->bqhe', diff, v)

    out = lax.map(one_block, jnp.arange(s // C_Q_BLOCK))
    out = jnp.transpose(out, (1, 0, 2, 3, 4)).reshape(b, s, C_HEADS, 2 * C_HEAD_DIM)
    out = rms_norm(out, g_norm) * (1.0 - lambda_init)
    return out.reshape(b, s, D_MODEL) @ w_o


def swiglu(x, w_gate_up, w_down):
    g, u = jnp.split(x @ w_gate_up, 2, axis=-1)
    return (jax.nn.silu(g) * u) @ w_down


def setup_inputs(seed: int = 0) -> dict:
    key = jax.random.key(seed)
    ks = jax.random.split(key, 32)
    f32 = jnp.float32
    d = D_MODEL

    def nrm(k, shape, scale):
        return jax.random.normal(k, shape, f32) * scale

    return {
        "x": nrm(ks[0], (BATCH, SEQ, d), 1.0),
        "ln1_g": 1.0 + nrm(ks[1], (DEPTH, d), 0.02),
        "ln1_b": nrm(ks[2], (DEPTH, d), 0.02),
        "ln2_g": 1.0 + nrm(ks[3], (DEPTH, d), 0.02),
        "ln2_b": nrm(ks[4], (DEPTH, d), 0.02),
        "ffn_w_gate_up": nrm(ks[5], (DEPTH, d, 2 * FFN_HIDDEN), d ** -0.5),
        "ffn_w_down": nrm(ks[6], (DEPTH, FFN_HIDDEN, d), DN_BETA * FFN_HIDDEN ** -0.5),
        "t5_table": nrm(ks[7], (T5_BUCKETS, C_HEADS), 0.1),
        "a_w_qkv": nrm(ks[8], (N_A, d, 3 * d), d ** -0.5),
        "a_rel_bias": nrm(ks[9], (N_A, 2 * A_REL_CLIP + 1, A_HEADS), 0.1),
        "a_w_o": nrm(ks[10], (N_A, d, d), DN_BETA * d ** -0.5),
        "b_w_in": nrm(ks[11], (N_B, d, 2 * B_KEY_DIM + 2 * B_VAL_DIM), d ** -0.5),
        "b_w_g1": nrm(ks[12], (N_B, d, B_GATE_RANK), d ** -0.5),
        "b_w_g2": nrm(ks[13], (N_B, B_GATE_RANK, B_KEY_DIM), B_GATE_RANK ** -0.5),
        "b_b_g": nrm(ks[14], (N_B, B_KEY_DIM), 0.1),
        "b_g_norm": 1.0 + nrm(ks[15], (N_B, B_HV), 0.02),
        "b_w_o": nrm(ks[16], (N_B, B_VAL_DIM, d), DN_BETA * B_VAL_DIM ** -0.5),
        "c_w_qkv": nrm(ks[17], (N_C, d, 3 * d), d ** -0.5),
        "c_lam_q1": nrm(ks[18], (N_C, C_HEAD_DIM), 0.1),
        "c_lam_k1": nrm(ks[19], (N_C, C_HEAD_DIM), 0.1),
        "c_lam_q2": nrm(ks[20], (N_C, C_HEAD_DIM), 0.1),
        "c_lam_k2": nrm(ks[21], (N_C, C_HEAD_DIM), 0.1),
        "c_g_norm": 1.0 + nrm(ks[22], (N_C, 2 * C_HEAD_DIM), 0.02),
        "c_w_o": nrm(ks[23], (N_C, d, d), DN_BETA * d ** -0.5),
    }


def reference(x, ln1_g, ln1_b, ln2_g, ln2_b, ffn_w_gate_up, ffn_w_down, t5_table,
              a_w_qkv, a_rel_bias, a_w_o,
              b_w_in, b_w_g1, b_w_g2, b_b_g, b_g_norm, b_w_o,
              c_w_qkv, c_lam_q1, c_lam_k1, c_lam_q2, c_lam_k2, c_g_norm, c_w_o):
    h = x
    for i in range(DEPTH):
        kind = i % N_MIXERS
        j = i // N_MIXERS
        if kind == 0:
            y = chunk_band_attention(h, a_w_qkv[j], a_rel_bias[j], a_w_o[j])
        elif kind == 1:
            y = gated_linear_attention(h, b_w_in[j], b_w_g1[j], b_w_g2[j], b_b_g[j], b_g_norm[j], b_w_o[j])
        else:
            lambda_init = 0.8 - 0.6 * math.exp(-0.3 * i)
            y = differential_attention(h, c_w_qkv[j], c_lam_q1[j], c_lam_k1[j], c_lam_q2[j], c_lam_k2[j],
                                       c_g_norm[j], c_w_o[j], t5_table, lambda_init)
        h = layer_norm(DN_ALPHA * h + y, ln1_g[i], ln1_b[i])
        h = layer_norm(DN_ALPHA * h + swiglu(h, ffn_w_gate_up[i], ffn_w_down[i]), ln2_g[i], ln2_b[i])
    return h
```

```python
import math
import bisect
import contextlib
import numpy as np
import concourse.bass as bass
import concourse.mybir as mybir
from concourse.bass_utils import run_bass_kernel_spmd

F32 = mybir.dt.float32
BF16 = mybir.dt.bfloat16
AF = mybir.ActivationFunctionType
ALU = mybir.AluOpType

D = 1024
T = 2048
NT = 16
KC = 8
DEPTH = 4
FF = 2816
DN_ALPHA = (2.0 * DEPTH) ** 0.25
LN_EPS = 1e-5
RMS_EPS = 1e-6
NEG = -30000.0
DBG = set()

ENGS = ("pe", "act", "dve", "pool", "sp")


class Op:
    __slots__ = ("eng", "fn", "deps", "idx", "gidx", "signal", "cnt", "dma", "waits", "tag")


class Sched:
    def __init__(self):
        self.q = {e: [] for e in ENGS}
        self.lastw = {}
        self.readers = {}
        self.g = 0
        self.dma_gidx = {}
        self.tag = ""
        self.guard = None

    def last(self, eng):
        for op in reversed(self.q[eng]):
            if op.fn is not None and op.dma is None:
                return op
        return None

    def collect(self, keys):
        deps = set()
        for k in keys:
            d = self.lastw.get(k)
            if d is not None:
                deps.add(d)
            rd = self.readers.get(k)
            if rd:
                for x in rd.values():
                    if isinstance(x, list):
                        deps.update(x)
                    else:
                        deps.add(x)
        return deps

    def add(self, eng, fn, r=(), w=(), dma=None, extra=None):
        op = Op()
        op.eng = eng
        op.fn = fn
        op.idx = len(self.q[eng])
        op.gidx = self.g
        self.g += 1
        op.dma = dma
        op.signal = False
        op.cnt = 0
        op.tag = self.tag
        deps = set()
        for k in r:
            d = self.lastw.get(k)
            if d is not None:
                deps.add(d)
        for k in w:
            d = self.lastw.get(k)
            if d is not None:
                deps.add(d)
            rd = self.readers.get(k)
            if rd:
                for x in rd.values():
                    if isinstance(x, list):
                        deps.update(x)
                    else:
                        deps.add(x)
        if extra:
            deps |= extra
        if self.guard:
            deps |= self.guard
        op.deps = deps
        for k in w:
            self.lastw[k] = op
            self.readers[k] = {}
        for k in r:
            rd = self.readers.setdefault(k, {})
            if dma is not None:
                rd.setdefault("dma", []).append(op)
            else:
                rd[eng] = op
        self.q[eng].append(op)
        if dma is not None:
            self.dma_gidx.setdefault(dma, []).append(op.gidx)
        return op

    def fence(self, eng, keys):
        return self.add(eng, None, r=keys)

    def barrier(self, engs=("pe", "act", "dve")):
        lasts = []
        for e in engs:
            for op in reversed(self.q[e]):
                if op.fn is not None and op.dma is None:
                    lasts.append(op)
                    break
        for e in engs:
            op = self.add(e, None)
            op.deps = set(x for x in lasts if x.eng != e)

    def resolve(self):
        for e in ENGS:
            for op in self.q[e]:
                keep = []
                for d in op.deps:
                    if d.dma is not None:
                        keep.append(d)
                        continue
                    if d.fn is None:
                        continue
                    if d.eng == op.eng:
                        if op.eng == "pe":
                            continue
                    d.signal = True
                    keep.append(d)
                op.deps = keep
        for e in ENGS:
            c = 0
            for op in self.q[e]:
                if op.signal and op.dma is None:
                    c += 1
                op.cnt = c
        for e in ENGS:
            waited = {}
            for op in self.q[e]:
                need = {}
                for d in op.deps:
                    if d.dma is not None:
                        lst = self.dma_gidx[d.dma]
                        n = bisect.bisect_left(lst, op.gidx)
                        key = ("dma", d.dma)
                        val = 16 * n
                    else:
                        key = ("eng", d.eng)
                        val = d.cnt
                    if val > need.get(key, 0):
                        need[key] = val
                op.waits = []
                for key, val in need.items():
                    if waited.get(key, 0) >= val:
                        continue
                    waited[key] = val
                    op.waits.append((key, val))

    def sem_keys(self):
        keys = [("eng", e) for e in ENGS if e != "sp"]
        keys += [("dma", k) for k in self.dma_gidx]
        return keys

    def emit(self, block, sems):
        engmap = {"pe": block.tensor, "act": block.scalar, "dve": block.vector,
                  "pool": block.gpsimd, "sp": block.sync}
        for e in ENGS:
            ops = self.q[e]
            if not ops:
                continue

            def body(eng, ops=ops, e=e):
                for op in ops:
                    for key, val in op.waits:
                        eng.wait_ge(sems[key], val)
                    if op.fn is None:
                        continue
                    ins = op.fn(eng)
                    if op.dma is not None:
                        ins.then_inc(sems[("dma", op.dma)], 16)
                    elif op.signal:
                        ins.then_inc(sems[("eng", e)], 1)

            engmap[e](body)


class Ring:
    def __init__(self, n):
        self.n = n
        self.i = -1

    def next(self):
        self.i = (self.i + 1) % self.n
        return self.i


class Piece:
    def __init__(self, pid, nkc, ncols, parts):
        self.pid = pid
        self.nkc = nkc
        self.ncols = ncols
        self.parts = parts
        self.slot = None
        self.view = None

    def key(self, part):
        return ("w", self.slot, part)

    def ap(self, part, kc, a, b):
        _, k0, pk, c0, pw = self.parts[part]
        return self.view[:, k0 + kc, c0 + a:c0 + b]


class WLoader:
    NSLOT = 4
    SLOT_ELEMS = 8 * 512

    def __init__(self, S, wsl):
        self.S = S
        self.wsl = wsl
        self.plan = []
        self.byid = {}
        self.free = list(range(self.NSLOT))
        self.nxt = 0

    def add(self, piece):
        assert piece.nkc * piece.ncols <= self.SLOT_ELEMS
        self.plan.append(piece)
        self.byid[piece.pid] = piece

    def pump(self):
        while self.free and self.nxt < len(self.plan):
            s = self.free.pop(0)
            p = self.plan[self.nxt]
            self.nxt += 1
            p.slot = s
            p.view = self.wsl[:, s, 0:p.nkc * p.ncols].rearrange("p (k c) -> p k c", c=p.ncols)
            extra = self.S.collect([("w", s, 0), ("w", s, 1), ("w", s, 2)])
            for pi, (src, k0, pk, c0, pw) in enumerate(p.parts):
                dst = p.view[:, k0:k0 + pk, c0:c0 + pw]
                self._dma(dst, src, [("w", s, pi)], "w%d" % s, extra)

    def _dma(self, dst, src, wkeys, sem, extra):
        self.S.add("pool", lambda q: q.dma_start(out=dst, in_=src), w=wkeys, dma=sem, extra=set(extra))

    def get(self, pid):
        p = self.byid[pid]
        assert p.slot is not None, ("weight piece not yet issued", pid)
        return p

    def done(self, pid):
        p = self.byid[pid]
        self.free.append(p.slot)
        self.pump()


def build_program(n_layers=DEPTH, stop_mid=False):
    nc = bass.Bass("TRN2", target_bir_lowering=False)

    def din(name, shape):
        return nc.dram_tensor(name, list(shape), F32, kind="ExternalInput").ap()

    x_d = din("x", [T, D])
    lngb_d = din("lngb", [DEPTH, 4, 128, D])
    gu_d = din("ffn_gu", [DEPTH, D, 2 * FF])
    dn_d = din("ffn_d", [DEPTH, FF, D])
    aqkv_d = din("a_qkv", [2, D, 3 * D])
    ao_d = din("a_o", [2, D, D])
    abias_d = din("a_bias", [2, 16, 128, 640])
    bin_d = din("b_in", [D, 3 * D])
    bg1_d = din("b_g1", [D, 16])
    bg2_d = din("b_g2aug", [17, 512])
    bgn_d = din("b_gn", [128, 256])
    bo_d = din("b_o", [D, D])
    cqkv_d = din("c_qkv", [D, 3 * D])
    co_d = din("c_o", [D, D])
    cbias_d = din("c_bias", [8, 128, 768])
    cgn_d = din("c_gn", [128, 128])
    ccc_d = din("c_cc", [128, 8])
    clam_d = din("c_lam", [128, 4, 64])
    cident_d = din("k_ident", [128, 128])
    cU_d = din("k_U", [128, 128])
    out_d = nc.dram_tensor("out", [T, D], F32, kind="ExternalOutput").ap()

    S = Sched()
    es = contextlib.ExitStack()

    def sb(name, shape, dt):
        return es.enter_context(nc.sbuf_tensor(name, list(shape), dt))

    h = sb("h", [128, NT, D], F32)
    hT = sb("hT", [128, KC, T], BF16)
    wsl = sb("wsl", [128, WLoader.NSLOT, WLoader.SLOT_ELEMS], BF16)
    gb = sb("gb", [128, 2, D], F32)
    SCRN = 31872
    scr = sb("scr", [128, SCRN], BF16)
    ident = sb("ident", [128, 128], BF16)
    hb = sb("hb", [128, 2, D], BF16)
    ebuf = sb("ebuf", [128, 3, 640], BF16)
    onb = sb("onb", [128, 2, 256], BF16)
    stat = sb("stat", [128, 4, 8], F32)
    bst = sb("bst", [128, 4, 2, 6], F32)
    stat2 = sb("stat2", [128, 4, 8], F32)
    cst = sb("cst", [128, 4], F32)
    ps = es.enter_context(nc.psum_tensor("ps", [128, 4096], F32))
    psb = ps.bitcast(BF16)

    W = WLoader(S, wsl)

    def mm(out, lhsT, rhs, start, stop, r, w):
        S.add("pe", lambda q: q.matmul(out, lhsT=lhsT, rhs=rhs, start=start, stop=stop), r=r, w=w)

    def tp(out, in_, idn, r, w):
        S.add("pe", lambda q: q.transpose(out=out, in_=in_, identity=idn), r=r, w=w)

    def act(out, in_, func, r, w, scale=1.0, bias=None, accum=None):
        def f(q):
            kw = {}
            if bias is not None:
                kw["bias"] = bias
            if accum is not None:
                kw["accum_out"] = accum
            return q.activation(out=out, in_=in_, func=func, scale=scale, **kw)
        S.add("act", f, r=r, w=w)

    def tt(eng, out, in0, in1, op, r, w):
        S.add(eng, lambda q: q.tensor_tensor(out=out, in0=in0, in1=in1, op=op), r=r, w=w)

    def ts(eng, out, in0, s1, op0, r, w, s2=None, op1=None):
        if s2 is None:
            S.add(eng, lambda q: q.tensor_scalar(out=out, in0=in0, scalar1=s1, scalar2=None, op0=op0), r=r, w=w)
        else:
            S.add(eng, lambda q: q.tensor_scalar(out=out, in0=in0, scalar1=s1, scalar2=s2, op0=op0, op1=op1), r=r, w=w)

    def stt(eng, out, in0, scalar, in1, op0, op1, r, w):
        S.add(eng, lambda q: q.scalar_tensor_tensor(out=out, in0=in0, scalar=scalar, in1=in1, op0=op0, op1=op1), r=r, w=w)

    def cp(eng, out, in_, r, w):
        if eng == "act":
            S.add("act", lambda q: q.copy(out=out, in_=in_), r=r, w=w)
        else:
            S.add(eng, lambda q: q.tensor_copy(out=out, in_=in_), r=r, w=w)

    def dma(eng, out, in_, r, w, sem):
        S.add(eng, lambda q: q.dma_start(out=out, in_=in_), r=r, w=w, dma=sem)

    def bank(b, a=0, n=512):
        return ps[:, b * 512 + a:b * 512 + a + n]

    def pskeys(b0, nb=1):
        return [("ps", b0 + i) for i in range(nb)]

    def wsrc(mat, r0, nk, c0, w):
        return mat[r0:r0 + nk * 128, c0:c0 + w].rearrange("(k p) c -> p k c", p=128)

    def plan_ffn(i):
        for grp in range(2):
            for sp in range(6):
                nf = 2 if sp < 5 else 1
                f0 = grp * 11 + 2 * sp
                W.add(Piece((i, "gu", grp, sp), 8, 512, [
                    (wsrc(gu_d[i], 0, 8, f0 * 128, nf * 128), 0, 8, 0, nf * 128),
                    (wsrc(gu_d[i], 0, 8, FF + f0 * 128, nf * 128), 0, 8, 256, nf * 128)]))
            for half in range(2):
                W.add(Piece((i, "dA", grp, half), 8, 512, [
                    (wsrc(dn_d[i], grp * 1408, 8, half * 512, 512), 0, 8, 0, 512)]))
            W.add(Piece((i, "dB", grp), 6, 512, [
                (wsrc(dn_d[i], grp * 1408 + 1024, 3, 0, 512), 0, 3, 0, 512),
                (wsrc(dn_d[i], grp * 1408 + 1024, 3, 512, 512), 3, 3, 0, 512)]))

    def plan_o(i, mat):
        for half in range(2):
            W.add(Piece((i, "o", half), 8, 512, [(wsrc(mat, 0, 8, half * 512, 512), 0, 8, 0, 512)]))

    for i in range(n_layers):
        kind = i % 3
        j = i // 3
        if kind == 0:
            for g in range(4):
                W.add(Piece((i, "qk", g), 8, 512, [
                    (wsrc(aqkv_d[j], 0, 8, g * 256, 256), 0, 8, 0, 256),
                    (wsrc(aqkv_d[j], 0, 8, D + g * 256, 256), 0, 8, 256, 256)]))
                W.add(Piece((i, "v", g), 8, 256, [(wsrc(aqkv_d[j], 0, 8, 2 * D + g * 256, 256), 0, 8, 0, 256)]))
            plan_o(i, ao_d[j])
        elif kind == 1:
            W.add(Piece((i, "g1"), 8, 16, [(wsrc(bg1_d, 0, 8, 0, 16), 0, 8, 0, 16)]))
            for hd in range(4):
                W.add(Piece((i, "qkv", hd), 8, 512, [
                    (wsrc(bin_d, 0, 8, hd * 128, 128), 0, 8, 0, 128),
                    (wsrc(bin_d, 0, 8, 512 + hd * 128, 128), 0, 8, 128, 128),
                    (wsrc(bin_d, 0, 8, 1024 + hd * 256, 256), 0, 8, 256, 256)]))
                W.add(Piece((i, "r", hd), 8, 256, [(wsrc(bin_d, 0, 8, 2048 + hd * 256, 256), 0, 8, 0, 256)]))
            plan_o(i, bo_d)
        else:
            for g in range(4):
                W.add(Piece((i, "qk", g), 8, 512, [
                    (wsrc(cqkv_d, 0, 8, g * 256, 256), 0, 8, 0, 256),
                    (wsrc(cqkv_d, 0, 8, D + g * 256, 256), 0, 8, 256, 256)]))
                W.add(Piece((i, "v", g), 8, 256, [(wsrc(cqkv_d, 0, 8, 2 * D + g * 256, 256), 0, 8, 0, 256)]))
            plan_o(i, co_d)
        if not (stop_mid and i == n_layers - 1):
            plan_ffn(i)

    S.add("dve", lambda q: q.memset(cst[:, 0:1], -0.5), r=[], w=["cst"])
    dma("pool", ident[:], cident_d, [], ["ident"], "cst")
    W.pump()

    for t in range(NT):
        dma("sp", h[:, t, :], x_d[t * 128:(t + 1) * 128, :], [], [("h", t)], "xin%d" % (t // 2))

    hb_ring = Ring(2)
    tb_ring = Ring(2)
    evac_flip = [0]

    def to_featmajor(src_keys, t, nch, dst_fn, dst_keys, src_ap_fn):
        tb = 6 + tb_ring.next()
        for c in range(nch):
            tp(psb[:, tb * 1024 + c * 128: tb * 1024 + (c + 1) * 128], src_ap_fn(c), ident[:],
               list(src_keys) + ["ident"], pskeys(tb))
        src = psb[:, tb * 1024: tb * 1024 + nch * 128].rearrange("p (c t) -> p c t", t=128)
        evac_flip[0] ^= 1
        cp("act" if evac_flip[0] else "dve", dst_fn(), src, pskeys(tb), dst_keys)

    def make_hT(t):
        r = hb_ring.next()
        cp("act", hb[:, r, :], h[:, t, :], [("h", t)], [("hb", r)])
        to_featmajor([("hb", r)], t, 8, lambda: hT[:, :, t * 128:(t + 1) * 128], [("hT", t)],
                     lambda c: hb[:, r, c * 128:(c + 1) * 128])

    init_hT = []

    st_ring = Ring(4)
    st2_ring = Ring(4)

    ln_pipe = []

    ln_backlog = []

    def ln_step():
        if ln_backlog:
            ln_pipe.append(ln_backlog.pop(0))
        for ent in list(ln_pipe):
            if ent[1]:
                ent[1].pop(0)()
        ln_pipe[:] = [ent for ent in ln_pipe if ent[1]]

    def ln_drain():
        while ln_pipe or ln_backlog:
            ln_step()

    def need_hT(tiles):
        tiles = set(tiles)
        while any(ent[0] in tiles for ent in ln_pipe + ln_backlog):
            ln_step()

    def ln_tick():
        if ln_pipe or ln_backlog:
            ln_step()

    def _mk_init(t):
        loc = {}

        def sa():
            r = hb_ring.next()
            loc["r"] = r
            cp("act", hb[:, r, :], h[:, t, :], [("h", t)], [("hb", r)])

        def sb():
            r = loc["r"]
            to_featmajor([("hb", r)], t, 8, lambda: hT[:, :, t * 128:(t + 1) * 128], [("hT", t)],
                         lambda c: hb[:, r, c * 128:(c + 1) * 128])
        return [t, [sa, sb]]

    for t in range(NT):
        ln_backlog.append(_mk_init(t))

    def layer_norm(t, y_ap, ykeys, alpha, final):
        hk = [("h", t)]
        hv = h[:, t, :]
        si = st_ring.next()
        mv = stat[:, si, 0:2]
        rstd = stat[:, si, 2:3]
        nmr = stat[:, si, 3:4]
        ve = stat[:, si, 4:5]

        def s1():
            S.tag = "ln.t%d.s1" % t
            stt("dve", hv, hv, float(alpha), y_ap, ALU.mult, ALU.add, hk + ykeys, hk)
            for k in range(2):
                S.add("dve", lambda q, k=k: q.bn_stats(out=bst[:, si, k, :], in_=h[:, t, k * 512:(k + 1) * 512]),
                      r=hk, w=[("bst", si, k)])
            S.add("dve", lambda q: q.bn_aggr(out=mv, in_=bst[:, si, :, :]), r=[("bst", si, 0), ("bst", si, 1)],
                  w=[("mv", si)])
            ts("dve", ve, stat[:, si, 1:2], LN_EPS, ALU.add, [("mv", si)], [("ve", si)])

        def s2():
            S.tag = "ln.t%d.s2" % t
            tt("pool", rstd, ve, cst[:, 0:1], ALU.pow, [("ve", si), "cst"], [("rstd", si)])
            stt("dve", nmr, stat[:, si, 0:1], -1.0, rstd, ALU.mult, ALU.mult, [("mv", si), ("rstd", si)], [("nmr", si)])
            act(hv, hv, AF.Identity, hk + [("rstd", si), ("nmr", si)], hk, scale=rstd, bias=nmr)

        def s3():
            S.tag = "ln.t%d.s3" % t
            tt("pool", hv, hv, gb[:, 0, :], ALU.mult, hk + [("gb", 0)], hk)

        def s4():
            S.tag = "ln.t%d.s4" % t
            tt("dve", hv, hv, gb[:, 1, :], ALU.add, hk + [("gb", 1)], hk)
            if final:
                dma("sp", out_d[t * 128:(t + 1) * 128, :], hv, hk, [("out", t)], "out")
            else:
                r = hb_ring.next()
                hbr[0] = r
                cp("act", hb[:, r, :], h[:, t, :], [("h", t)], [("hb", r)])

        def s5():
            S.tag = "ln.t%d.s5" % t
            if not final:
                r = hbr[0]
                to_featmajor([("hb", r)], t, 8, lambda: hT[:, :, t * 128:(t + 1) * 128], [("hT", t)],
                             lambda c: hb[:, r, c * 128:(c + 1) * 128])

        hbr = [0]
        ln_pipe.append([t, [s1, s2, s3, s4, s5]])
        ln_step()

    def load_gb(i, which):
        dma("sp", gb[:, 0, :], lngb_d[i, 2 * which, :, :], [], [("gb", 0)], "gb%d_%d" % (i, which))
        dma("sp", gb[:, 1, :], lngb_d[i, 2 * which + 1, :, :], [], [("gb", 1)], "gb%d_%d" % (i, which))

    yset_ring = Ring(3)

    def out_proj_ln(i, ao, final):
        pO = [W.get((i, "o", 0)), W.get((i, "o", 1))]
        for t in range(NT):
            S.tag = "L%d.oproj.t%d" % (i, t)
            ys = yset_ring.next()
            for half in range(2):
                b = 2 * ys + half
                for kc in range(KC):
                    mm(bank(b), ao[:, kc, t * 128:(t + 1) * 128], pO[half].ap(0, kc, 0, 512), kc == 0, kc == KC - 1,
                       [("ao", kc, t), pO[half].key(0)], pskeys(b))
            layer_norm(t, ps[:, ys * 1024:(ys + 1) * 1024], pskeys(2 * ys, 2), DN_ALPHA, final)
        if final:
            ln_drain()
        W.done((i, "o", 0))
        W.done((i, "o", 1))

    def ffn(i, final):
        actT = scr[:, 0:11 * T].rearrange("p (f t) -> p f t", t=T)
        sgt = scr[:, 11 * T:11 * T + 2048].bitcast(F32).rearrange("p (a d) -> p a d", d=512)
        sg_ring = Ring(2)
        nsg = [0]
        ffn_guard = {S.last("pe")}
        for grp in range(2):
            for sp in range(6):
                nf = 2 if sp < 5 else 1
                pc = W.get((i, "gu", grp, sp))
                S.tag = "L%d.ffn.gu%d.%d" % (i, grp, sp)
                for tq in range(4):
                  for fi in range(nf):
                        fc = 2 * sp + fi
                        need_hT(range(4 * tq, 4 * tq + 4))
                        ys = yset_ring.next()
                        bg, bu = 2 * ys, 2 * ys + 1
                        hk = [("hT", 4 * tq + u) for u in range(4)]
                        for kc in range(KC):
                            mm(bank(bg), pc.ap(0, kc, fi * 128, fi * 128 + 128), hT[:, kc, tq * 512:(tq + 1) * 512],
                               kc == 0, kc == KC - 1, hk + [pc.key(0)], pskeys(bg))
                        for kc in range(KC):
                            mm(bank(bu), pc.ap(1, kc, fi * 128, fi * 128 + 128), hT[:, kc, tq * 512:(tq + 1) * 512],
                               kc == 0, kc == KC - 1, hk + [pc.key(1)], pskeys(bu))
                        si = sg_ring.next()
                        if nsg[0] < 2:
                            S.guard = ffn_guard
                        nsg[0] += 1
                        act(sgt[:, si, :], bank(bg), AF.Silu, pskeys(bg), [("sgt", si)])
                        S.guard = None
                        tt("dve", actT[:, fc, tq * 512:(tq + 1) * 512], sgt[:, si, :], bank(bu), ALU.mult,
                           [("sgt", si)] + pskeys(bu), [("actT", fc, tq)])
                        ln_tick()
                W.done((i, "gu", grp, sp))
            if "ffn_a" in DBG:
                for t in range(NT):
                    dma("sp", out_d[t * 128:(t + 1) * 128, :], h[:, t, :], [("h", t)], [("out", t)], "out")
                return
            pA = [W.get((i, "dA", grp, 0)), W.get((i, "dA", grp, 1))]
            pB = W.get((i, "dB", grp))
            if grp == 1 and "ffn_e" not in DBG:
                load_gb(i, 1)
            for t in range(NT):
                S.tag = "L%d.ffn.dn%d.t%d" % (i, grp, t)
                ys = yset_ring.next()
                for half in range(2):
                    b = 2 * ys + half
                    for kc in range(11):
                        if kc < 8:
                            rhs = pA[half].ap(0, kc, 0, 512)
                            wk = pA[half].key(0)
                        else:
                            rhs = pB.ap(half, kc - 8, 0, 512)
                            wk = pB.key(half)
                        mm(bank(b), actT[:, kc, t * 128:(t + 1) * 128], rhs, kc == 0, kc == 10,
                           [("actT", kc, t // 4), wk], pskeys(b))
                y_ap = ps[:, ys * 1024:(ys + 1) * 1024]
                if grp == 0 or "ffn_c" in DBG:
                    hk = [("h", t)]
                    stt("dve", h[:, t, :], h[:, t, :], float(DN_ALPHA if grp == 0 else 1.0), y_ap, ALU.mult, ALU.add,
                        hk + pskeys(2 * ys, 2), hk)
                else:
                    layer_norm(t, y_ap, pskeys(2 * ys, 2), 1.0, final and "ffn_d" not in DBG)
            if final or grp == 0:
                ln_drain()
            if "ffn_d" in DBG and grp == 1:
                for t in range(NT):
                    dma("sp", out_d[t * 128:(t + 1) * 128, :], h[:, t, :], [("h", t)], [("out", t)], "out")
            if "ffn_b" in DBG or ("ffn_c" in DBG and grp == 1):
                for t in range(NT):
                    dma("sp", out_d[t * 128:(t + 1) * 128, :], h[:, t, :], [("h", t)], [("out", t)], "out")
                return
            W.done((i, "dA", grp, 0))
            W.done((i, "dA", grp, 1))
            W.done((i, "dB", grp))

    pj_ring = Ring(4)

    def proj_fm(dst_fn, dkey_fn, pc, part, c0, scale):
        for tq in range(4):
            need_hT(range(4 * tq, 4 * tq + 4))
            b = pj_ring.next()
            hk = [("hT", 4 * tq + u) for u in range(4)]
            for kc in range(KC):
                mm(bank(b), pc.ap(part, kc, c0, c0 + 128), hT[:, kc, tq * 512:(tq + 1) * 512], kc == 0, kc == KC - 1,
                   hk + [pc.key(part)], pskeys(b))
            ln_tick()
            if scale is None:
                cp("dve", dst_fn(tq), bank(b), pskeys(b), [dkey_fn(tq)])
            else:
                S.add("act", lambda q, o=dst_fn(tq), b=b: q.mul(out=o, in_=bank(b), mul=float(scale)),
                      r=pskeys(b), w=[dkey_fn(tq)])

    def proj_tm(t, pc, part, c0, ncols):
        need_hT([t])
        b = pj_ring.next()
        for kc in range(KC):
            mm(bank(b, 0, ncols), hT[:, kc, t * 128:(t + 1) * 128], pc.ap(part, kc, c0, c0 + ncols), kc == 0,
               kc == KC - 1, [("hT", t), pc.key(part)], pskeys(b))
        return b

    class Pipe:
        def __init__(self):
            self.fl = []

        def push(self, stages):
            self.fl.append(list(stages))
            self.step()

        def step(self):
            for st_ in list(self.fl):
                if st_:
                    st_.pop(0)()
            self.fl = [s_ for s_ in self.fl if s_]

        def drain(self):
            while self.fl:
                self.step()

    bstage = hb[:].rearrange("p a d -> p (a d)").bitcast(F32)

    def chunk_attention(i, j):
        qT = scr[:, 0:4096].rearrange("p (c t) -> p c t", t=T)
        kT = scr[:, 4096:8192].rearrange("p (c t) -> p c t", t=T)
        v = scr[:, 8192:8192 + 4160].rearrange("p (t h d) -> p t h d", h=4, d=65)
        ao = scr[:, 12352:12352 + 16384].rearrange("p (c t) -> p c t", t=T)
        expB = scr[:, 28736:28736 + 2560].rearrange("p (h q) -> p h q", q=640)
        gb_loaded = [False]
        mix_guard = {S.last("pe")} if S.last("pe") is not None else None
        v_init = [False]
        e_ring = Ring(3)
        s_ring = Ring(2)
        on_ring = Ring(2)
        for g in range(4):
            S.tag = "L%d.g%d.proj" % (i, g)
            pqk = W.get((i, "qk", g))
            pv = W.get((i, "v", g))

            def build_expb(hh):
                dma("pool", bstage[:, 0:640], abias_d[j, 4 * g + hh], [], ["bstage", ("hb", 0), ("hb", 1)], "bias")
                act(expB[:, hh, :], bstage[:, 0:640], AF.Exp, ["bstage", ("hb", 0), ("hb", 1)], [("expB", hh)])

            for c in range(2):
                proj_fm(lambda tq, c=c: qT[:, c, tq * 512:(tq + 1) * 512], lambda tq, c=c: ("qT", c, tq), pqk, 0, c * 128, None)
                ln_drain()
                if not gb_loaded[0]:
                    gb_loaded[0] = True
                    load_gb(i, 0)
                build_expb(2 * c)
                build_expb(2 * c + 1)
                proj_fm(lambda tq, c=c: kT[:, c, tq * 512:(tq + 1) * 512], lambda tq, c=c: ("kT", c, tq), pqk, 1, c * 128, None)
            if not v_init[0]:
                v_init[0] = True
                S.guard = mix_guard
                S.add("dve", lambda q: q.memset(scr[:, 8192:8192 + 4160], 1.0), r=[], w=[("v", t) for t in range(NT)])
                S.guard = None
            for t in range(NT):
                b = proj_tm(t, pv, 0, 0, 256)
                cp("act" if t % 2 else "dve", v[:, t, :, 0:64], bank(b, 0, 256).rearrange("p (h d) -> p h d", d=64),
                   pskeys(b), [("v", t)])
            W.done((i, "qk", g))
            W.done((i, "v", g))

            pipe = Pipe()
            for qb in range(NT):
                for hh in range(4):
                    def mk(qb=qb, hh=hh):
                        c, po = hh // 2, (hh % 2) * 64
                        nkb = min(qb, 4) + 1
                        jlo = 5 - nkb
                        loc = {}
                        ob = 4 + (qb % 2)
                        tagb = "L%d.g%d.q%d.h%d." % (i, g, qb, hh)

                        def sA():
                            S.tag = tagb + "A"
                            ss = s_ring.next()
                            loc["ss"] = ss
                            base = ss * 1024
                            for jj in range(jlo, 5):
                                kb = qb - 4 + jj
                                mm(ps[:, base + jj * 128:base + (jj + 1) * 128], kT[po:po + 64, c, kb * 128:(kb + 1) * 128],
                                   qT[po:po + 64, c, qb * 128:(qb + 1) * 128], True, True,
                                   [("kT", c, kb // 4), ("qT", c, qb // 4)], pskeys(2 * ss + (1 if jj == 4 else 0)))

                        def sB():
                            S.tag = tagb + "B"
                            ss = loc["ss"]
                            base = ss * 1024
                            e = e_ring.next()
                            loc["e"] = e
                            act(ebuf[:, e, jlo * 128:640], ps[:, base + jlo * 128:base + 640], AF.Exp, pskeys(2 * ss, 2), [("E", e)],
                                scale=0.125)

                        def sB2():
                            S.tag = tagb + "B2"
                            e = loc["e"]
                            tt("dve", ebuf[:, e, jlo * 128:640], ebuf[:, e, jlo * 128:640], expB[:, hh, jlo * 128:640], ALU.mult,
                               [("E", e), ("expB", hh)], [("E", e)])

                        def sC():
                            S.tag = tagb + "C"
                            e = loc["e"]
                            for jj in range(jlo, 5):
                                kb = qb - 4 + jj
                                mm(bank(ob, hh * 65, 65), ebuf[:, e, jj * 128:(jj + 1) * 128], v[:, kb, hh, :], jj == jlo, jj == 4,
                                   [("E", e), ("v", kb)], pskeys(ob))

                        def sD1():
                            S.tag = tagb + "D1"
                            o3 = bank(ob, 0, 260).rearrange("p (h d) -> p h d", d=65)
                            si = st2_ring.next()
                            rec = stat2[:, si, 0:4]
                            S.add("dve", lambda q: q.reciprocal(out=rec, in_=o3[:, :, 64]), r=pskeys(ob), w=[("rec", si)])
                            r = on_ring.next()
                            loc["r"] = r
                            tt("dve", onb[:, r, :].rearrange("p (h d) -> p h d", d=64), o3[:, :, 0:64],
                               rec.unsqueeze(2).to_broadcast([128, 4, 64]), ALU.mult, pskeys(ob) + [("rec", si)], [("on", r)])

                        def sD2():
                            S.tag = tagb + "D2"
                            r = loc["r"]
                            tb = 6 + tb_ring.next()
                            loc["tb"] = tb
                            for cc in range(2):
                                tp(psb[:, tb * 1024 + cc * 128: tb * 1024 + (cc + 1) * 128], onb[:, r, cc * 128:(cc + 1) * 128],
                                   ident[:], [("on", r), "ident"], pskeys(tb))

                        def sD3():
                            S.tag = tagb + "D3"
                            tb = loc["tb"]
                            srcp = psb[:, tb * 1024: tb * 1024 + 256].rearrange("p (c t) -> p c t", t=128)
                            cp("dve", ao[:, 2 * g:2 * g + 2, qb * 128:(qb + 1) * 128], srcp, pskeys(tb),
                               [("ao", 2 * g, qb), ("ao", 2 * g + 1, qb)])

                        stages = [sA, sB, sB2, sC]
                        if hh == 3:
                            stages += [sD1, sD2, sD3]
                        return stages
                    pipe.push(mk())
            pipe.drain()
        return ao

    def diff_attention(i, j):
        lam_init = 0.8 - 0.6 * math.exp(-0.3 * i)
        qT = scr[:, 0:4096].rearrange("p (c t) -> p c t", t=T)
        kT = scr[:, 4096:8192].rearrange("p (c t) -> p c t", t=T)
        v = scr[:, 8192:8192 + 4128].rearrange("p (t h d) -> p t h d", h=2, d=129)
        ao = scr[:, 12352:12352 + 16384].rearrange("p (c t) -> p c t", t=T)
        expBp = scr[:, 28736:28736 + 512].rearrange("p (h q) -> p h q", q=256)
        f32z = scr[:, 30272:30272 + 1600].bitcast(F32)
        gC = f32z[:, 0:128]
        lamv = f32z[:, 128:384].rearrange("p (a d) -> p a d", d=64)
        t1 = f32z[:, 384:512]
        dtm3 = scr[:, 29248:29248 + 768].bitcast(F32).rearrange("p (a d) -> p a d", d=128)
        dt_ring = Ring(3)
        ltmp = f32z[:, 640:768]
        sc = f32z[:, 768:784]
        cc = f32z[:, 784:792]
        ncc = f32z[:, 792:800]
        gb_loaded = [False]
        dma("sp", gC, cgn_d, [], ["gC"], "cst2_%d" % i)
        dma("sp", lamv, clam_d, [], ["lamv"], "cst2_%d" % i)
        dma("sp", cc, ccc_d, [], ["cc"], "cst2_%d" % i)
        ts("dve", ncc, cc, -1.0, ALU.mult, ["cc"], ["ncc"])
        mix_guard = {S.last("pe")} if S.last("pe") is not None else None
        v_init = [False]
        for m in range(2):
            tt("dve", ltmp[:, m * 64:(m + 1) * 64], lamv[:, 2 * m, :], lamv[:, 2 * m + 1, :], ALU.mult, ["lamv"], [("ltmp", m)])
            S.add("dve", lambda q, m=m: q.reduce_sum(out=sc[:, m:m + 1], in_=ltmp[:, m * 64:(m + 1) * 64],
                                                     axis=mybir.AxisListType.X), r=[("ltmp", m)], w=[("sc", m)])
            act(sc[:, 2 + m:3 + m], sc[:, m:m + 1], AF.Exp, [("sc", m)], [("sc", 2 + m)])
        tt("dve", sc[:, 4:5], sc[:, 3:4], sc[:, 2:3], ALU.subtract, [("sc", 2), ("sc", 3)], [("sc", 4)])
        ts("dve", sc[:, 5:6], sc[:, 4:5], -float(lam_init), ALU.add, [("sc", 4)], ["nlam"])
        nlam = sc[:, 5:6]
        e_ring = Ring(3)
        s_ring = Ring(4)
        on_ring = Ring(2)
        ccn = float(1.0 - lam_init) ** 2
        for g in range(4):
            S.tag = "L%d.g%d.proj" % (i, g)
            pqk = W.get((i, "qk", g))
            pv = W.get((i, "v", g))

            def build_expb(hl):
                hd = 2 * g + hl
                dma("pool", bstage[:, 0:256], cbias_d[hd][:, 0:256], [], ["bstage", ("hb", 0), ("hb", 1)], "bias")
                act(expBp[:, hl, :], bstage[:, 0:256], AF.Exp, ["bstage", ("hb", 0), ("hb", 1), "ncc"], [("expB", hl)],
                    bias=ncc[:, hd:hd + 1])

            for c in range(2):
                proj_fm(lambda tq, c=c: qT[:, c, tq * 512:(tq + 1) * 512], lambda tq, c=c: ("qT", c, tq), pqk, 0, c * 128, 0.125)
                ln_drain()
                if not gb_loaded[0]:
                    gb_loaded[0] = True
                    load_gb(i, 0)
                build_expb(c)
                proj_fm(lambda tq, c=c: kT[:, c, tq * 512:(tq + 1) * 512], lambda tq, c=c: ("kT", c, tq), pqk, 1, c * 128, None)
            if not v_init[0]:
                v_init[0] = True
                S.guard = mix_guard
                S.add("dve", lambda q: q.memset(scr[:, 8192:8192 + 4128], 1.0), r=[], w=[("v", t) for t in range(NT)])
                S.guard = None
            for t in range(NT):
                b = proj_tm(t, pv, 0, 0, 256)
                cp("act" if t % 2 else "dve", v[:, t, :, 0:128], bank(b, 0, 256).rearrange("p (h d) -> p h d", d=128),
                   pskeys(b), [("v", t)])
            W.done((i, "qk", g))
            W.done((i, "v", g))

            pipe = Pipe()
            shared = {}
            for qb in range(NT):
                for hl in range(2):
                    for m in range(2):
                        nk4 = qb // 4 + 1
                        for k4 in range(nk4):
                            def mk(qb=qb, hl=hl, m=m, k4=k4, last=(k4 == nk4 - 1)):
                                hd = 2 * g + hl
                                po = m * 64
                                nb = min(4, qb + 1 - 4 * k4)
                                ob = 4 + hl
                                loc = {}
                                tagb = "L%d.g%d.q%d.h%d.m%d.k%d." % (i, g, qb, hl, m, k4)

                                def sA():
                                    S.tag = tagb + "A"
                                    sbk = s_ring.next()
                                    loc["s"] = sbk
                                    for jj in range(nb):
                                        kb = qb - (4 * k4 + jj)
                                        mm(bank(sbk, jj * 128, 128), kT[po:po + 64, hl, kb * 128:(kb + 1) * 128],
                                           qT[po:po + 64, hl, qb * 128:(qb + 1) * 128], True, True,
                                           [("kT", hl, kb // 4), ("qT", hl, qb // 4)], pskeys(sbk))
                                    if "warm" in DBG:
                                        mm(bank(7, 0, 512), ident[:], hT[:, 0, 0:512], True, True, ["ident"], pskeys(7))

                                def sB():
                                    S.tag = tagb + "B"
                                    sbk = loc["s"]
                                    e = e_ring.next()
                                    loc["e"] = e
                                    act(ebuf[:, e, 0:nb * 128], bank(sbk, 0, nb * 128), AF.Exp, pskeys(sbk) + ["cc"], [("E", e)],
                                        bias=cc[:, hd:hd + 1])

                                def sB2():
                                    S.tag = tagb + "B2"
                                    e = loc["e"]
                                    if k4 == 0:
                                        nmul = min(nb, 2)
                                        tt("dve", ebuf[:, e, 0:nmul * 128], ebuf[:, e, 0:nmul * 128], expBp[:, hl, 0:nmul * 128],
                                           ALU.mult, [("E", e), ("expB", hl)], [("E", e)])

                                def sC():
                                    S.tag = tagb + "C"
                                    e = loc["e"]
                                    for jj in range(nb):
                                        kb = qb - (4 * k4 + jj)
                                        mm(bank(ob, m * 129, 129), ebuf[:, e, jj * 128:(jj + 1) * 128], v[:, kb, hl, :],
                                           k4 == 0 and jj == 0, last and jj == nb - 1, [("E", e), ("v", kb)], pskeys(ob))

                                o3 = bank(ob, 0, 258).rearrange("p (h d) -> p h d", d=129)

                                def sD1a():
                                    S.tag = tagb + "D1a"
                                    si = st2_ring.next()
                                    loc["si"] = si
                                    di = dt_ring.next()
                                    loc["di"] = di
                                    dtm = dtm3[:, di, :]
                                    rec = stat2[:, si, 0:2]
                                    S.add("dve", lambda q: q.reciprocal(out=rec, in_=o3[:, :, 128]), r=pskeys(ob), w=[("rec", si)])
                                    tt("dve", stat2[:, si, 2:3], stat2[:, si, 1:2], nlam, ALU.mult, [("rec", si), "nlam"], [("rec2", si)])
                                    ts("dve", t1, o3[:, 0, 0:128], stat2[:, si, 0:1], ALU.mult, pskeys(ob) + [("rec", si)], ["t1"])
                                    stt("dve", dtm, o3[:, 1, 0:128], stat2[:, si, 2:3], t1, ALU.mult, ALU.add,
                                        pskeys(ob) + [("rec2", si), "t1"], [("dtm", di)])

                                def sD1b():
                                    S.tag = tagb + "D1b"
                                    si = loc["si"]
                                    di = loc["di"]
                                    act(ltmp, dtm3[:, di, :], AF.Square, [("dtm", di)], ["sqj", ("ss", si)], accum=stat2[:, si, 3:4])

                                def sD1c():
                                    S.tag = tagb + "D1c"
                                    si = loc["si"]
                                    ts("dve", stat2[:, si, 4:5], stat2[:, si, 3:4], 1.0 / (128.0 * ccn), ALU.mult, [("ss", si)],
                                       [("ms", si)], s2=RMS_EPS / ccn, op1=ALU.add)
                                    tt("pool", stat2[:, si, 5:6], stat2[:, si, 4:5], cst[:, 0:1], ALU.pow, [("ms", si), "cst"],
                                       [("rs", si)])

                                def sD1d():
                                    S.tag = tagb + "D1d"
                                    si = loc["si"]
                                    if hl == 0:
                                        shared[qb] = on_ring.next()
                                    r = shared[qb]
                                    di = loc["di"]
                                    stt("dve", onb[:, r, hl * 128:(hl + 1) * 128], dtm3[:, di, :], stat2[:, si, 5:6], gC, ALU.mult, ALU.mult,
                                        [("dtm", di), ("rs", si), "gC"], [("on", r, hl)])

                                def sD2():
                                    S.tag = tagb + "D2"
                                    r = shared[qb]
                                    tb = 6 if "warm" in DBG else 6 + tb_ring.next()
                                    loc["tb"] = tb
                                    for cc_ in range(2):
                                        tp(psb[:, tb * 1024 + cc_ * 128: tb * 1024 + (cc_ + 1) * 128],
                                           onb[:, r, cc_ * 128:(cc_ + 1) * 128], ident[:],
                                           [("on", r, 0), ("on", r, 1), "ident"], pskeys(tb))

                                def sD3():
                                    S.tag = tagb + "D3"
                                    tb = loc["tb"]
                                    srcp = psb[:, tb * 1024: tb * 1024 + 256].rearrange("p (c t) -> p c t", t=128)
                                    cp("act", ao[:, 2 * g:2 * g + 2, qb * 128:(qb + 1) * 128], srcp, pskeys(tb),
                                       [("ao", 2 * g, qb), ("ao", 2 * g + 1, qb)])

                                stages = [sA, sB, sB2, sC]
                                if last and m == 1:
                                    stages += [sD1a, sD1b, sD1c, sD1d]
                                    if hl == 1:
                                        stages += [sD2, sD3]
                                return stages
                            pipe.push(mk())
            pipe.drain()
        return ao

    def gla(i):
        qT2 = scr[:, 0:4096].rearrange("p (a t) -> p a t", t=T)
        g1T = scr[:, 4096:6144]
        w2 = scr[:, 6144:6656]
        fz = scr[:, 6656:6656 + 5456].bitcast(F32)
        gB = fz[:, 0:256]
        U = fz[:, 256:384]
        ones2 = fz[:, 384:386]

        def hbuf(k, off, n):
            base = 392 + k * 1168 + off
            return fz[:, base:base + n]
        bz = scr[:, 28736:28736 + 2048]
        ao = scr[:, 12352:12352 + 16384].rearrange("p (c t) -> p c t", t=T)
        S.guard = {S.last("pe")} if S.last("pe") is not None else None
        dma("sp", gB, bgn_d, [], ["gB"], "cst2_%d" % i)
        dma("sp", U, cU_d, [], ["U"], "cst2_%d" % i)
        dma("pool", w2[0:17, :], bg2_d, [], ["w2"], "bias")
        S.add("dve", lambda q: q.memset(ones2, 0.0), r=[], w=["ones2"])
        S.add("dve", lambda q: q.memset(ones2[0:64, 0:1], 1.0), r=["ones2"], w=["ones2"])
        S.add("dve", lambda q: q.memset(ones2[64:128, 1:2], 1.0), r=["ones2"], w=["ones2"])
        S.add("dve", lambda q: q.memset(g1T[0:32, :], 1.0), r=[], w=["g1T"])
        S.add("dve", lambda q: q.memset(bz[:, 0:512], 0.0), r=[], w=[("kdec", 0), ("kdec", 1)])
        S.guard = None
        pg1 = W.get((i, "g1"))
        for tq in range(4):
            need_hT(range(4 * tq, 4 * tq + 4))
            b = pj_ring.next()
            hk = [("hT", 4 * tq + u) for u in range(4)]
            for kc in range(KC):
                mm(ps[0:16, b * 512:(b + 1) * 512], pg1.ap(0, kc, 0, 16), hT[:, kc, tq * 512:(tq + 1) * 512],
                   kc == 0, kc == KC - 1, hk + [pg1.key(0)], pskeys(b))
            cp("dve", g1T[0:16, tq * 512:(tq + 1) * 512], ps[0:16, b * 512:(b + 1) * 512], pskeys(b) + ["g1T"], ["g1T"])
        W.done((i, "g1"))
        ln_drain()
        load_gb(i, 0)

        def head_gen(hd, k):
            P1, P2, P3 = 3 * k, 3 * k + 1, 3 * k + 2
            spb = hbuf(k, 0, 128)
            wdec = hbuf(k, 128, 128)
            decay = hbuf(k, 256, 8)
            state = hbuf(k, 264, 256)
            er = hbuf(k, 520, 256)
            onf = hbuf(k, 776, 256)
            e1 = hbuf(k, 1032, 128)
            ssq = hbuf(k, 1160, 8)
            kdec2 = bz[:, k * 256:(k + 1) * 256].rearrange("p (a d) -> p a d", d=128)
            vbf = bz[:, 512 + k * 256:512 + (k + 1) * 256]
            stbf = bz[:, 1024 + k * 512:1024 + (k + 1) * 512].rearrange("p (c d) -> p c d", d=256)
            K_ = lambda name: (name, k)
            pA = W.get((i, "qkv", hd))
            pR = W.get((i, "r", hd))
            proj_fm(lambda tq: qT2[:, k, tq * 512:(tq + 1) * 512], lambda tq: ("qT", k, tq), pA, 0, 0, 128.0 ** -0.5)
            S.add("dve", lambda q: q.memset(state, 0.0), r=[], w=[K_("state")])
            yield
            for t in range(NT):
                S.tag = "L%d.gla.h%d.t%d" % (i, hd, t)
                tok = slice(t * 128, (t + 1) * 128)
                need_hT([t])
                for kc in range(KC):
                    mm(bank(P1, 0, 128), hT[:, kc, tok], pA.ap(1, kc, 0, 128), kc == 0, kc == KC - 1,
                       [("hT", t), pA.key(1)], pskeys(P1))
                mm(bank(P1, 128, 128), g1T[0:17, tok], w2[0:17, hd * 128:(hd + 1) * 128], True, True, ["g1T", "w2"], pskeys(P1))
                for kc in range(KC):
                    mm(bank(P2, 0, 256), hT[:, kc, tok], pA.ap(2, kc, 0, 256), kc == 0, kc == KC - 1,
                       [("hT", t), pA.key(2)], pskeys(P2))
                for kc in range(KC):
                    mm(bank(P3, 0, 256), hT[:, kc, tok], pR.ap(0, kc, 0, 256), kc == 0, kc == KC - 1,
                       [("hT", t), pR.key(0)], pskeys(P3))
                yield
                act(e1, bank(P1, 128, 128), AF.Exp, pskeys(P1), [K_("e1")], scale=-1.0)
                cp("act", vbf, bank(P2, 0, 256), pskeys(P2), [K_("vbf")])
                act(er, bank(P3, 0, 256), AF.Exp, pskeys(P3), [K_("er")], scale=-1.0)
                yield
                act(spb, e1, AF.Ln, [K_("e1")], [K_("spb")], bias=1.0)
                act(er, er, AF.Ln, [K_("er")], [K_("er")], bias=1.0)
                yield
                mm(bank(P1, 256, 128), U, spb, True, True, ["U", K_("spb")], pskeys(P1))
                mm(bank(P1, 384, 2), spb, ones2, True, True, ["ones2", K_("spb")], pskeys(P1))
                act(er, er, AF.Exp, [K_("er")], [K_("er")], scale=-1.0)
                yield
                act(wdec, bank(P1, 256, 128), AF.Exp, pskeys(P1), [K_("wdec")], scale=-1.0 / 16.0)
                act(decay[:, 0:2], bank(P1, 384, 2), AF.Exp, pskeys(P1), [K_("decay")], scale=-1.0 / 16.0)
                yield
                for c in range(2):
                    lo, hi = c * 64, (c + 1) * 64
                    tt("dve", kdec2[lo:hi, c, :], ps[lo:hi, P1 * 512:P1 * 512 + 128], wdec[lo:hi, :], ALU.mult,
                       pskeys(P1) + [K_("wdec")], [("kdec", k)])
                yield
                mm(bank(P1, 0, 256), kdec2[:, 0, :], vbf, True, True, [("kdec", k), K_("vbf")], pskeys(P1))
                mm(bank(P2, 256, 256), kdec2[:, 1, :], vbf, True, True, [("kdec", k), K_("vbf")], pskeys(P2))
                yield
                kvs = [bank(P1, 0, 256), bank(P2, 256, 256)]
                kvk = [pskeys(P1), pskeys(P2)]
                for c in range(2):
                    stt("dve", state, state, decay[:, c:c + 1], kvs[c], ALU.mult, ALU.add,
                        [K_("state"), K_("decay")] + kvk[c], [K_("state")])
                    cp("dve", stbf[:, c, :], state, [K_("state")], [("stbf", k, c)])
                yield
                mm(bank(P1, 0, 256), qT2[:, k, tok], stbf[:, 0, :], True, True, [("qT", k, t // 4), ("stbf", k, 0)], pskeys(P1))
                mm(bank(P2, 0, 256), qT2[:, k, tok], stbf[:, 1, :], True, True, [("qT", k, t // 4), ("stbf", k, 1)], pskeys(P2))
                yield
                obk = [P1, P2]
                for c in range(2):
                    lo, hi = c * 64, (c + 1) * 64
                    act(onf[lo:hi, :], ps[lo:hi, obk[c] * 512:obk[c] * 512 + 256], AF.Square, pskeys(obk[c]),
                        [("onf", k, c), ("ssq", k, c)], accum=ssq[lo:hi, 0:1])
                yield
                ts("dve", ssq[:, 1:2], ssq[:, 0:1], 1.0 / 256.0, ALU.mult, [("ssq", k, 0), ("ssq", k, 1)], [K_("rs1")],
                   s2=RMS_EPS, op1=ALU.add)
                tt("pool", ssq[:, 2:3], ssq[:, 1:2], cst[:, 0:1], ALU.pow, [K_("rs1"), "cst"], [K_("rs2")])
                yield
                for c in range(2):
                    lo, hi = c * 64, (c + 1) * 64
                    stt("dve", onf[lo:hi, :], ps[lo:hi, obk[c] * 512:obk[c] * 512 + 256], ssq[lo:hi, 2:3],
                        gB[lo:hi, :], ALU.mult, ALU.mult, pskeys(obk[c]) + [K_("rs2"), "gB", ("onf", k, c)], [("onf", k, c)])
                tt("dve", er, er, bank(P3, 0, 256), ALU.mult, [K_("er")] + pskeys(P3), [K_("er")])
                tt("dve", onb[:, k, :], onf, er, ALU.mult, [("onf", k, 0), ("onf", k, 1), K_("er")], [("on", k)])
                yield
                for cc_ in range(2):
                    tp(psb[:, P3 * 1024 + 512 + cc_ * 128: P3 * 1024 + 512 + (cc_ + 1) * 128], onb[:, k, cc_ * 128:(cc_ + 1) * 128],
                       ident[:], [("on", k), "ident"], pskeys(P3))
                yield
                srcp = psb[:, P3 * 1024 + 512: P3 * 1024 + 768].rearrange("p (c t) -> p c t", t=128)
                cp("act", ao[:, 2 * hd:2 * hd + 2, t * 128:(t + 1) * 128], srcp, pskeys(P3),
                   [("ao", 2 * hd, t), ("ao", 2 * hd + 1, t)])
                yield
            W.done((i, "qkv", hd))
            W.done((i, "r", hd))

        for pair in ((0, 1), (2, 3)):
            gens = [head_gen(hd, k) for k, hd in enumerate(pair)]
            alive = list(gens)
            for gen in gens:
                next(gen)
            for _ in range(7):
                next(gens[0])
            while alive:
                for gi, gen in enumerate(list(alive)):
                    try:
                        next(gen)
                    except StopIteration:
                        alive.remove(gen)
        return ao

    for i in range(n_layers):
        kind = i % 3
        j = i // 3
        last_layer = (i == n_layers - 1)
        if kind == 0:
            ao = chunk_attention(i, j)
        elif kind == 1:
            ao = gla(i)
        else:
            ao = diff_attention(i, j)
        if ao is None:
            for t in range(NT):
                dma("sp", out_d[t * 128:(t + 1) * 128, :], h[:, t, :], [("h", t)], [("out", t)], "out")
            break
        out_proj_ln(i, ao, final=(last_layer and stop_mid))
        if not (last_layer and stop_mid):
            ffn(i, final=last_layer)

    ln_drain()
    S.fence("sp", [("out", t) for t in range(NT)])
    S.resolve()
    sems = {k: es.enter_context(nc.semaphore("s_%s_%s" % k)) for k in S.sem_keys()}
    with nc.Block() as block:
        S.emit(block, sems)
    es.close()
    return nc


def _t5_bucket_np(rel):
    nb = 16
    max_exact = 8
    ret = (rel > 0).astype(np.int32) * nb
    n = np.abs(rel)
    is_small = n < max_exact
    n_f = np.maximum(n, 1).astype(np.float32)
    large = max_exact + (np.log(n_f / np.float32(max_exact)) / np.float32(math.log(128 / max_exact))
                         * np.float32(nb - max_exact)).astype(np.int32)
    large = np.minimum(large, nb - 1)
    return ret + np.where(is_small, n, large)


def _prep_shared(inp):
    f = np.float32
    k_in = np.arange(128)[:, None]
    q_in = np.arange(128)[None, :]
    a_rel = np.asarray(inp["a_rel_bias"], f)
    a_bias = np.empty((2, 16, 128, 640), f)
    for jj in range(5):
        rel = q_in - k_in + (4 - jj) * 128
        idx = np.clip(rel, -128, 128) + 128
        vals = a_rel[:, idx, :]
        vals = np.transpose(vals, (0, 3, 1, 2))
        if jj == 4:
            mask = (k_in >= 64) & (q_in < 64)
        elif jj == 0:
            mask = (k_in < 64) & (q_in >= 64)
        else:
            mask = np.zeros((128, 128), bool)
        a_bias[:, :, :, jj * 128:(jj + 1) * 128] = np.where(mask[None, None], f(NEG), vals)
    t5 = np.asarray(inp["t5_table"], f)
    c_bias = np.empty((8, 128, 768), f)
    for bi, jd in enumerate([0, 1, 2, 2, 2, 2]):
        rel = (k_in - q_in - 128 * jd).astype(np.int32)
        vals = np.transpose(t5[_t5_bucket_np(rel)], (2, 0, 1))
        if jd == 0:
            vals = np.where(((k_in >= 64) & (q_in < 64))[None], f(NEG), vals)
        c_bias[:, :, bi * 128:(bi + 1) * 128] = vals
    assert (_t5_bucket_np(np.arange(-2048, -128)) == 15).all()
    lngb = np.stack([inp["ln1_g"], inp["ln1_b"], inp["ln2_g"], inp["ln2_b"]], axis=1).astype(f)
    lngb = np.ascontiguousarray(np.broadcast_to(lngb[:, :, None, :], (DEPTH, 4, 128, D)))
    lam = np.stack([inp["c_lam_q1"][0], inp["c_lam_k1"][0], inp["c_lam_q2"][0], inp["c_lam_k2"][0]], 0).astype(f)
    s = np.arange(128)
    U = ((s[:, None] // 64 == s[None, :] // 64) & (s[:, None] > s[None, :])).astype(f)
    return {
        "lngb": lngb,
        "ffn_gu": np.ascontiguousarray(inp["ffn_w_gate_up"], f),
        "ffn_d": np.ascontiguousarray(inp["ffn_w_down"], f),
        "a_qkv": np.ascontiguousarray(inp["a_w_qkv"], f),
        "a_o": np.ascontiguousarray(inp["a_w_o"], f),
        "a_bias": a_bias,
        "b_in": np.ascontiguousarray(inp["b_w_in"][0], f),
        "b_g1": np.ascontiguousarray(inp["b_w_g1"][0], f),
        "b_g2aug": np.ascontiguousarray(np.concatenate([inp["b_w_g2"][0], inp["b_b_g"][0][None, :]], 0), f),
        "b_gn": np.ascontiguousarray(np.broadcast_to(np.asarray(inp["b_g_norm"][0], f)[None, :], (128, 256))),
        "b_o": np.ascontiguousarray(inp["b_w_o"][0], f),
        "c_qkv": np.ascontiguousarray(inp["c_w_qkv"][0], f),
        "c_o": np.ascontiguousarray(inp["c_w_o"][0], f),
        "c_bias": c_bias,
        "c_cc": np.ascontiguousarray(np.broadcast_to(t5[15, :][None, :], (128, 8))),
        "c_gn": np.ascontiguousarray(np.broadcast_to(np.asarray(inp["c_g_norm"][0], f)[None, :], (128, 128))),
        "c_lam": np.ascontiguousarray(np.broadcast_to(lam[None], (128, 4, 64))),
        "k_ident": np.eye(128, dtype=f),
        "k_U": U,
    }


_NC_CACHE = {}


def kernel(_n_layers=DEPTH, _stop_mid=False, _cores=8, **inputs):
    inp = {k: np.asarray(v) for k, v in inputs.items()}
    shared = _prep_shared(inp)
    key = (_n_layers, _stop_mid)
    if key not in _NC_CACHE:
        _NC_CACHE[key] = build_program(_n_layers, _stop_mid)
    nc = _NC_CACHE[key]
    x = np.asarray(inp["x"], np.float32)
    in_maps = []
    for b in range(_cores):
        m = dict(shared)
        m["x"] = np.ascontiguousarray(x[b])
        in_maps.append(m)
    res = run_bass_kernel_spmd(nc, in_maps, core_ids=list(range(_cores)))
    return np.stack([res.results[b]["out"] for b in range(_cores)], axis=0)
```

```python
import math
import bisect
import contextlib
import numpy as np
import concourse.bass as bass
import concourse.mybir as mybir
from concourse.bass_utils import run_bass_kernel_spmd

F32 = mybir.dt.float32
BF16 = mybir.dt.bfloat16
AF = mybir.ActivationFunctionType
ALU = mybir.AluOpType

D = 1024
T = 2048
NT = 16
KC = 8
DEPTH = 4
FF = 2816
DN_ALPHA = (2.0 * DEPTH) ** 0.25
LN_EPS = 1e-5
RMS_EPS = 1e-6
NEG = -30000.0
DBG = set()

ENGS = ("pe", "act", "dve", "pool", "sp")


class Op:
    __slots__ = ("eng", "fn", "deps", "idx", "gidx", "signal", "cnt", "dma", "waits", "tag")


class Sched:
    def __init__(self):
        self.q = {e: [] for e in ENGS}
        self.lastw = {}
        self.readers = {}
        self.g = 0
        self.dma_gidx = {}
        self.tag = ""
        self.guard = None

    def last(self, eng):
        for op in reversed(self.q[eng]):
            if op.fn is not None and op.dma is None:
                return op
        return None

    def collect(self, keys):
        deps = set()
        for k in keys:
            d = self.lastw.get(k)
            if d is not None:
                deps.add(d)
            rd = self.readers.get(k)
            if rd:
                for x in rd.values():
                    if isinstance(x, list):
                        deps.update(x)
                    else:
                        deps.add(x)
        return deps

    def add(self, eng, fn, r=(), w=(), dma=None, extra=None):
        op = Op()
        op.eng = eng
        op.fn = fn
        op.idx = len(self.q[eng])
        op.gidx = self.g
        self.g += 1
        op.dma = dma
        op.signal = False
        op.cnt = 0
        op.tag = self.tag
        deps = set()
        for k in r:
            d = self.lastw.get(k)
            if d is not None:
                deps.add(d)
        for k in w:
            d = self.lastw.get(k)
            if d is not None:
                deps.add(d)
            rd = self.readers.get(k)
            if rd:
                for x in rd.values():
                    if isinstance(x, list):
                        deps.update(x)
                    else:
                        deps.add(x)
        if extra:
            deps |= extra
        if self.guard:
            deps |= self.guard
        op.deps = deps
        for k in w:
            self.lastw[k] = op
            self.readers[k] = {}
        for k in r:
            rd = self.readers.setdefault(k, {})
            if dma is not None:
                rd.setdefault("dma", []).append(op)
            else:
                rd[eng] = op
        self.q[eng].append(op)
        if dma is not None:
            self.dma_gidx.setdefault(dma, []).append(op.gidx)
        return op

    def fence(self, eng, keys):
        return self.add(eng, None, r=keys)

    def barrier(self, engs=("pe", "act", "dve")):
        lasts = []
        for e in engs:
            for op in reversed(self.q[e]):
                if op.fn is not None and op.dma is None:
                    lasts.append(op)
                    break
        for e in engs:
            op = self.add(e, None)
            op.deps = set(x for x in lasts if x.eng != e)

    def resolve(self):
        for e in ENGS:
            for op in self.q[e]:
                keep = []
                for d in op.deps:
                    if d.dma is not None:
                        keep.append(d)
                        continue
                    if d.fn is None:
                        continue
                    if d.eng == op.eng:
                        if op.eng == "pe":
                            continue
                    d.signal = True
                    keep.append(d)
                op.deps = keep
        for e in ENGS:
            c = 0
            for op in self.q[e]:
                if op.signal and op.dma is None:
                    c += 1
                op.cnt = c
        for e in ENGS:
            waited = {}
            for op in self.q[e]:
                need = {}
                for d in op.deps:
                    if d.dma is not None:
                        lst = self.dma_gidx[d.dma]
                        n = bisect.bisect_left(lst, op.gidx)
                        key = ("dma", d.dma)
                        val = 16 * n
                    else:
                        key = ("eng", d.eng)
                        val = d.cnt
                    if val > need.get(key, 0):
                        need[key] = val
                op.waits = []
                for key, val in need.items():
                    if waited.get(key, 0) >= val:
                        continue
                    waited[key] = val
                    op.waits.append((key, val))

    def sem_keys(self):
        keys = [("eng", e) for e in ENGS if e != "sp"]
        keys += [("dma", k) for k in self.dma_gidx]
        return keys

    def emit(self, block, sems):
        engmap = {"pe": block.tensor, "act": block.scalar, "dve": block.vector,
                  "pool": block.gpsimd, "sp": block.sync}
        for e in ENGS:
            ops = self.q[e]
            if not ops:
                continue

            def body(eng, ops=ops, e=e):
                for op in ops:
                    for key, val in op.waits:
                        eng.wait_ge(sems[key], val)
                    if op.fn is None:
                        continue
                    ins = op.fn(eng)
                    if op.dma is not None:
                        ins.then_inc(sems[("dma", op.dma)], 16)
                    elif op.signal:
                        ins.then_inc(sems[("eng", e)], 1)

            engmap[e](body)


class Ring:
    def __init__(self, n):
        self.n = n
        self.i = -1

    def next(self):
        self.i = (self.i + 1) % self.n
        return self.i


class Piece:
    def __init__(self, pid, nkc, ncols, parts):
        self.pid = pid
        self.nkc = nkc
        self.ncols = ncols
        self.parts = parts
        self.slot = None
        self.view = None

    def key(self, part):
        return ("w", self.slot, part)

    def ap(self, part, kc, a, b):
        _, k0, pk, c0, pw = self.parts[part]
        return self.view[:, k0 + kc, c0 + a:c0 + b]


class WLoader:
    NSLOT = 4
    SLOT_ELEMS = 8 * 512

    def __init__(self, S, wsl):
        self.S = S
        self.wsl = wsl
        self.plan = []
        self.byid = {}
        self.free = list(range(self.NSLOT))
        self.nxt = 0

    def add(self, piece):
        assert piece.nkc * piece.ncols <= self.SLOT_ELEMS
        self.plan.append(piece)
        self.byid[piece.pid] = piece

    def pump(self):
        while self.free and self.nxt < len(self.plan):
            s = self.free.pop(0)
            p = self.plan[self.nxt]
            self.nxt += 1
            p.slot = s
            p.view = self.wsl[:, s, 0:p.nkc * p.ncols].rearrange("p (k c) -> p k c", c=p.ncols)
            extra = self.S.collect([("w", s, 0), ("w", s, 1), ("w", s, 2)])
            for pi, (src, k0, pk, c0, pw) in enumerate(p.parts):
                dst = p.view[:, k0:k0 + pk, c0:c0 + pw]
                self._dma(dst, src, [("w", s, pi)], "w%d" % s, extra)

    def _dma(self, dst, src, wkeys, sem, extra):
        self.S.add("pool", lambda q: q.dma_start(out=dst, in_=src), w=wkeys, dma=sem, extra=set(extra))

    def get(self, pid):
        p = self.byid[pid]
        assert p.slot is not None, ("weight piece not yet issued", pid)
        return p

    def done(self, pid):
        p = self.byid[pid]
        self.free.append(p.slot)
        self.pump()


def build_program(n_layers=DEPTH, stop_mid=False):
    nc = bass.Bass("TRN2", target_bir_lowering=False)

    def din(name, shape):
        return nc.dram_tensor(name, list(shape), F32, kind="ExternalInput").ap()

    x_d = din("x", [T, D])
    lngb_d = din("lngb", [DEPTH, 4, 128, D])
    gu_d = din("ffn_gu", [DEPTH, D, 2 * FF])
    dn_d = din("ffn_d", [DEPTH, FF, D])
    aqkv_d = din("a_qkv", [2, D, 3 * D])
    ao_d = din("a_o", [2, D, D])
    abias_d = din("a_bias", [2, 16, 128, 640])
    bin_d = din("b_in", [D, 3 * D])
    bg1_d = din("b_g1", [D, 16])
    bg2_d = din("b_g2aug", [17, 512])
    bgn_d = din("b_gn", [128, 256])
    bo_d = din("b_o", [D, D])
    cqkv_d = din("c_qkv", [D, 3 * D])
    co_d = din("c_o", [D, D])
    cbias_d = din("c_bias", [8, 128, 768])
    cgn_d = din("c_gn", [128, 128])
    ccc_d = din("c_cc", [128, 8])
    clam_d = din("c_lam", [128, 4, 64])
    cident_d = din("k_ident", [128, 128])
    cU_d = din("k_U", [128, 128])
    out_d = nc.dram_tensor("out", [T, D], F32, kind="ExternalOutput").ap()

    S = Sched()
    es = contextlib.ExitStack()

    def sb(name, shape, dt):
        return es.enter_context(nc.sbuf_tensor(name, list(shape), dt))

    h = sb("h", [128, NT, D], F32)
    hT = sb("hT", [128, KC, T], BF16)
    wsl = sb("wsl", [128, WLoader.NSLOT, WLoader.SLOT_ELEMS], BF16)
    gb = sb("gb", [128, 2, D], F32)
    SCRN = 31872
    scr = sb("scr", [128, SCRN], BF16)
    ident = sb("ident", [128, 128], BF16)
    hb = sb("hb", [128, 2, D], BF16)
    ebuf = sb("ebuf", [128, 3, 640], BF16)
    onb = sb("onb", [128, 2, 256], BF16)
    stat = sb("stat", [128, 4, 8], F32)
    bst = sb("bst", [128, 4, 2, 6], F32)
    stat2 = sb("stat2", [128, 4, 8], F32)
    cst = sb("cst", [128, 4], F32)
    ps = es.enter_context(nc.psum_tensor("ps", [128, 4096], F32))
    psb = ps.bitcast(BF16)

    W = WLoader(S, wsl)

    def mm(out, lhsT, rhs, start, stop, r, w):
        S.add("pe", lambda q: q.matmul(out, lhsT=lhsT, rhs=rhs, start=start, stop=stop), r=r, w=w)

    def tp(out, in_, idn, r, w):
        S.add("pe", lambda q: q.transpose(out=out, in_=in_, identity=idn), r=r, w=w)

    def act(out, in_, func, r, w, scale=1.0, bias=None, accum=None):
        def f(q):
            kw = {}
            if bias is not None:
                kw["bias"] = bias
            if accum is not None:
                kw["accum_out"] = accum
            return q.activation(out=out, in_=in_, func=func, scale=scale, **kw)
        S.add("act", f, r=r, w=w)

    def tt(eng, out, in0, in1, op, r, w):
        S.add(eng, lambda q: q.tensor_tensor(out=out, in0=in0, in1=in1, op=op), r=r, w=w)

    def ts(eng, out, in0, s1, op0, r, w, s2=None, op1=None):
        if s2 is None:
            S.add(eng, lambda q: q.tensor_scalar(out=out, in0=in0, scalar1=s1, scalar2=None, op0=op0), r=r, w=w)
        else:
            S.add(eng, lambda q: q.tensor_scalar(out=out, in0=in0, scalar1=s1, scalar2=s2, op0=op0, op1=op1), r=r, w=w)

    def stt(eng, out, in0, scalar, in1, op0, op1, r, w):
        S.add(eng, lambda q: q.scalar_tensor_tensor(out=out, in0=in0, scalar=scalar, in1=in1, op0=op0, op1=op1), r=r, w=w)

    def cp(eng, out, in_, r, w):
        if eng == "act":
            S.add("act", lambda q: q.copy(out=out, in_=in_), r=r, w=w)
        else:
            S.add(eng, lambda q: q.tensor_copy(out=out, in_=in_), r=r, w=w)

    def dma(eng, out, in_, r, w, sem):
        S.add(eng, lambda q: q.dma_start(out=out, in_=in_), r=r, w=w, dma=sem)

    def bank(b, a=0, n=512):
        return ps[:, b * 512 + a:b * 512 + a + n]

    def pskeys(b0, nb=1):
        return [("ps", b0 + i) for i in range(nb)]

    def wsrc(mat, r0, nk, c0, w):
        return mat[r0:r0 + nk * 128, c0:c0 + w].rearrange("(k p) c -> p k c", p=128)

    def plan_ffn(i):
        for grp in range(2):
            for sp in range(6):
                nf = 2 if sp < 5 else 1
                f0 = grp * 11 + 2 * sp
                W.add(Piece((i, "gu", grp, sp), 8, 512, [
                    (wsrc(gu_d[i], 0, 8, f0 * 128, nf * 128), 0, 8, 0, nf * 128),
                    (wsrc(gu_d[i], 0, 8, FF + f0 * 128, nf * 128), 0, 8, 256, nf * 128)]))
            for half in range(2):
                W.add(Piece((i, "dA", grp, half), 8, 512, [
                    (wsrc(dn_d[i], grp * 1408, 8, half * 512, 512), 0, 8, 0, 512)]))
            W.add(Piece((i, "dB", grp), 6, 512, [
                (wsrc(dn_d[i], grp * 1408 + 1024, 3, 0, 512), 0, 3, 0, 512),
                (wsrc(dn_d[i], grp * 1408 + 1024, 3, 512, 512), 3, 3, 0, 512)]))

    def plan_o(i, mat):
        for half in range(2):
            W.add(Piece((i, "o", half), 8, 512, [(wsrc(mat, 0, 8, half * 512, 512), 0, 8, 0, 512)]))

    for i in range(n_layers):
        kind = i % 3
        j = i // 3
        if kind == 0:
            for g in range(4):
                W.add(Piece((i, "qk", g), 8, 512, [
                    (wsrc(aqkv_d[j], 0, 8, g * 256, 256), 0, 8, 0, 256),
                    (wsrc(aqkv_d[j], 0, 8, D + g * 256, 256), 0, 8, 256, 256)]))
                W.add(Piece((i, "v", g), 8, 256, [(wsrc(aqkv_d[j], 0, 8, 2 * D + g * 256, 256), 0, 8, 0, 256)]))
            plan_o(i, ao_d[j])
        elif kind == 1:
            W.add(Piece((i, "g1"), 8, 16, [(wsrc(bg1_d, 0, 8, 0, 16), 0, 8, 0, 16)]))
            for hd in range(4):
                W.add(Piece((i, "qkv", hd), 8, 512, [
                    (wsrc(bin_d, 0, 8, hd * 128, 128), 0, 8, 0, 128),
                    (wsrc(bin_d, 0, 8, 512 + hd * 128, 128), 0, 8, 128, 128),
                    (wsrc(bin_d, 0, 8, 1024 + hd * 256, 256), 0, 8, 256, 256)]))
                W.add(Piece((i, "r", hd), 8, 256, [(wsrc(bin_d, 0, 8, 2048 + hd * 256, 256), 0, 8, 0, 256)]))
            plan_o(i, bo_d)
        else:
            for g in range(4):
                W.add(Piece((i, "qk", g), 8, 512, [
                    (wsrc(cqkv_d, 0, 8, g * 256, 256), 0, 8, 0, 256),
                    (wsrc(cqkv_d, 0, 8, D + g * 256, 256), 0, 8, 256, 256)]))
                W.add(Piece((i, "v", g), 8, 256, [(wsrc(cqkv_d, 0, 8, 2 * D + g * 256, 256), 0, 8, 0, 256)]))
            plan_o(i, co_d)
        if not (stop_mid and i == n_layers - 1):
            plan_ffn(i)

    S.add("dve", lambda q: q.memset(cst[:, 0:1], -0.5), r=[], w=["cst"])
    dma("pool", ident[:], cident_d, [], ["ident"], "cst")
    W.pump()

    for t in range(NT):
        dma("sp", h[:, t, :], x_d[t * 128:(t + 1) * 128, :], [], [("h", t)], "xin%d" % (t // 2))

    hb_ring = Ring(2)
    tb_ring = Ring(2)
    evac_flip = [0]

    def to_featmajor(src_keys, t, nch, dst_fn, dst_keys, src_ap_fn):
        tb = 6 + tb_ring.next()
        for c in range(nch):
            tp(psb[:, tb * 1024 + c * 128: tb * 1024 + (c + 1) * 128], src_ap_fn(c), ident[:],
               list(src_keys) + ["ident"], pskeys(tb))
        src = psb[:, tb * 1024: tb * 1024 + nch * 128].rearrange("p (c t) -> p c t", t=128)
        evac_flip[0] ^= 1
        cp("act" if evac_flip[0] else "dve", dst_fn(), src, pskeys(tb), dst_keys)

    def make_hT(t):
        r = hb_ring.next()
        cp("act", hb[:, r, :], h[:, t, :], [("h", t)], [("hb", r)])
        to_featmajor([("hb", r)], t, 8, lambda: hT[:, :, t * 128:(t + 1) * 128], [("hT", t)],
                     lambda c: hb[:, r, c * 128:(c + 1) * 128])

    init_hT = []

    st_ring = Ring(4)
    st2_ring = Ring(4)

    ln_pipe = []

    ln_backlog = []

    def ln_step():
        if ln_backlog:
            ln_pipe.append(ln_backlog.pop(0))
        for ent in list(ln_pipe):
            if ent[1]:
                ent[1].pop(0)()
        ln_pipe[:] = [ent for ent in ln_pipe if ent[1]]

    def ln_drain():
        while ln_pipe or ln_backlog:
            ln_step()

    def need_hT(tiles):
        tiles = set(tiles)
        while any(ent[0] in tiles for ent in ln_pipe + ln_backlog):
            ln_step()

    def ln_tick():
        if ln_pipe or ln_backlog:
            ln_step()

    def _mk_init(t):
        loc = {}

        def sa():
            r = hb_ring.next()
            loc["r"] = r
            cp("act", hb[:, r, :], h[:, t, :], [("h", t)], [("hb", r)])

        def sb():
            r = loc["r"]
            to_featmajor([("hb", r)], t, 8, lambda: hT[:, :, t * 128:(t + 1) * 128], [("hT", t)],
                         lambda c: hb[:, r, c * 128:(c + 1) * 128])
        return [t, [sa, sb]]

    for t in range(NT):
        make_hT(t)

    def layer_norm(t, y_ap, ykeys, alpha, final):
        hk = [("h", t)]
        hv = h[:, t, :]
        si = st_ring.next()
        mv = stat[:, si, 0:2]
        rstd = stat[:, si, 2:3]
        nmr = stat[:, si, 3:4]
        ve = stat[:, si, 4:5]

        def s1():
            S.tag = "ln.t%d.s1" % t
            stt("dve", hv, hv, float(alpha), y_ap, ALU.mult, ALU.add, hk + ykeys, hk)
            for k in range(2):
                S.add("dve", lambda q, k=k: q.bn_stats(out=bst[:, si, k, :], in_=h[:, t, k * 512:(k + 1) * 512]),
                      r=hk, w=[("bst", si, k)])
            S.add("dve", lambda q: q.bn_aggr(out=mv, in_=bst[:, si, :, :]), r=[("bst", si, 0), ("bst", si, 1)],
                  w=[("mv", si)])
            ts("dve", ve, stat[:, si, 1:2], LN_EPS, ALU.add, [("mv", si)], [("ve", si)])

        def s2():
            S.tag = "ln.t%d.s2" % t
            tt("pool", rstd, ve, cst[:, 0:1], ALU.pow, [("ve", si), "cst"], [("rstd", si)])
            stt("dve", nmr, stat[:, si, 0:1], -1.0, rstd, ALU.mult, ALU.mult, [("mv", si), ("rstd", si)], [("nmr", si)])
            act(hv, hv, AF.Identity, hk + [("rstd", si), ("nmr", si)], hk, scale=rstd, bias=nmr)

        def s3():
            S.tag = "ln.t%d.s3" % t
            tt("pool", hv, hv, gb[:, 0, :], ALU.mult, hk + [("gb", 0)], hk)

        def s4():
            S.tag = "ln.t%d.s4" % t
            tt("dve", hv, hv, gb[:, 1, :], ALU.add, hk + [("gb", 1)], hk)
            if final:
                dma("sp", out_d[t * 128:(t + 1) * 128, :], hv, hk, [("out", t)], "out")
            else:
                r = hb_ring.next()
                hbr[0] = r
                cp("act", hb[:, r, :], h[:, t, :], [("h", t)], [("hb", r)])

        def s5():
            S.tag = "ln.t%d.s5" % t
            if not final:
                r = hbr[0]
                to_featmajor([("hb", r)], t, 8, lambda: hT[:, :, t * 128:(t + 1) * 128], [("hT", t)],
                             lambda c: hb[:, r, c * 128:(c + 1) * 128])

        hbr = [0]
        ln_pipe.append([t, [s1, s2, s3, s4, s5]])
        ln_step()

    def load_gb(i, which):
        dma("sp", gb[:, 0, :], lngb_d[i, 2 * which, :, :], [], [("gb", 0)], "gb%d_%d" % (i, which))
        dma("sp", gb[:, 1, :], lngb_d[i, 2 * which + 1, :, :], [], [("gb", 1)], "gb%d_%d" % (i, which))

    yset_ring = Ring(3)

    def out_proj_ln(i, ao, final):
        pO = [W.get((i, "o", 0)), W.get((i, "o", 1))]
        for t in range(NT):
            S.tag = "L%d.oproj.t%d" % (i, t)
            ys = yset_ring.next()
            for half in range(2):
                b = 2 * ys + half
                for kc in range(KC):
                    mm(bank(b), ao[:, kc, t * 128:(t + 1) * 128], pO[half].ap(0, kc, 0, 512), kc == 0, kc == KC - 1,
                       [("ao", kc, t), pO[half].key(0)], pskeys(b))
            layer_norm(t, ps[:, ys * 1024:(ys + 1) * 1024], pskeys(2 * ys, 2), DN_ALPHA, final)
        if final:
            ln_drain()
        W.done((i, "o", 0))
        W.done((i, "o", 1))

    def ffn(i, final):
        actT = scr[:, 0:11 * T].rearrange("p (f t) -> p f t", t=T)
        sgt = scr[:, 11 * T:11 * T + 2048].bitcast(F32).rearrange("p (a d) -> p a d", d=512)
        sg_ring = Ring(2)
        nsg = [0]
        ffn_guard = {S.last("pe")}
        for grp in range(2):
            for sp in range(6):
                nf = 2 if sp < 5 else 1
                pc = W.get((i, "gu", grp, sp))
                S.tag = "L%d.ffn.gu%d.%d" % (i, grp, sp)
                for tq in range(4):
                  for fi in range(nf):
                        fc = 2 * sp + fi
                        need_hT(range(4 * tq, 4 * tq + 4))
                        ys = yset_ring.next()
                        bg, bu = 2 * ys, 2 * ys + 1
                        hk = [("hT", 4 * tq + u) for u in range(4)]
                        for kc in range(KC):
                            mm(bank(bg), pc.ap(0, kc, fi * 128, fi * 128 + 128), hT[:, kc, tq * 512:(tq + 1) * 512],
                               kc == 0, kc == KC - 1, hk + [pc.key(0)], pskeys(bg))
                        for kc in range(KC):
                            mm(bank(bu), pc.ap(1, kc, fi * 128, fi * 128 + 128), hT[:, kc, tq * 512:(tq + 1) * 512],
                               kc == 0, kc == KC - 1, hk + [pc.key(1)], pskeys(bu))
                        si = sg_ring.next()
                        if nsg[0] < 2:
                            S.guard = ffn_guard
                        nsg[0] += 1
                        act(sgt[:, si, :], bank(bg), AF.Silu, pskeys(bg), [("sgt", si)])
                        S.guard = None
                        tt("dve", actT[:, fc, tq * 512:(tq + 1) * 512], sgt[:, si, :], bank(bu), ALU.mult,
                           [("sgt", si)] + pskeys(bu), [("actT", fc, tq)])
                        ln_tick()
                W.done((i, "gu", grp, sp))
            if "ffn_a" in DBG:
                for t in range(NT):
                    dma("sp", out_d[t * 128:(t + 1) * 128, :], h[:, t, :], [("h", t)], [("out", t)], "out")
                return
            pA = [W.get((i, "dA", grp, 0)), W.get((i, "dA", grp, 1))]
            pB = W.get((i, "dB", grp))
            if grp == 1 and "ffn_e" not in DBG:
                load_gb(i, 1)
            for t in range(NT):
                S.tag = "L%d.ffn.dn%d.t%d" % (i, grp, t)
                ys = yset_ring.next()
                for half in range(2):
                    b = 2 * ys + half
                    for kc in range(11):
                        if kc < 8:
                            rhs = pA[half].ap(0, kc, 0, 512)
                            wk = pA[half].key(0)
                        else:
                            rhs = pB.ap(half, kc - 8, 0, 512)
                            wk = pB.key(half)
                        mm(bank(b), actT[:, kc, t * 128:(t + 1) * 128], rhs, kc == 0, kc == 10,
                           [("actT", kc, t // 4), wk], pskeys(b))
                y_ap = ps[:, ys * 1024:(ys + 1) * 1024]
                if grp == 0 or "ffn_c" in DBG:
                    hk = [("h", t)]
                    stt("dve", h[:, t, :], h[:, t, :], float(DN_ALPHA if grp == 0 else 1.0), y_ap, ALU.mult, ALU.add,
                        hk + pskeys(2 * ys, 2), hk)
                else:
                    layer_norm(t, y_ap, pskeys(2 * ys, 2), 1.0, final and "ffn_d" not in DBG)
            if final or grp == 0:
                ln_drain()
            if "ffn_d" in DBG and grp == 1:
                for t in range(NT):
                    dma("sp", out_d[t * 128:(t + 1) * 128, :], h[:, t, :], [("h", t)], [("out", t)], "out")
            if "ffn_b" in DBG or ("ffn_c" in DBG and grp == 1):
                for t in range(NT):
                    dma("sp", out_d[t * 128:(t + 1) * 128, :], h[:, t, :], [("h", t)], [("out", t)], "out")
                return
            W.done((i, "dA", grp, 0))
            W.done((i, "dA", grp, 1))
            W.done((i, "dB", grp))

    pj_ring = Ring(4)

    def proj_fm(dst_fn, dkey_fn, pc, part, c0, scale):
        for tq in range(4):
            need_hT(range(4 * tq, 4 * tq + 4))
            b = pj_ring.next()
            hk = [("hT", 4 * tq + u) for u in range(4)]
            for kc in range(KC):
                mm(bank(b), pc.ap(part, kc, c0, c0 + 128), hT[:, kc, tq * 512:(tq + 1) * 512], kc == 0, kc == KC - 1,
                   hk + [pc.key(part)], pskeys(b))
            ln_tick()
            if scale is None:
                cp("dve", dst_fn(tq), bank(b), pskeys(b), [dkey_fn(tq)])
            else:
                S.add("act", lambda q, o=dst_fn(tq), b=b: q.mul(out=o, in_=bank(b), mul=float(scale)),
                      r=pskeys(b), w=[dkey_fn(tq)])

    def proj_tm(t, pc, part, c0, ncols):
        need_hT([t])
        b = pj_ring.next()
        for kc in range(KC):
            mm(bank(b, 0, ncols), hT[:, kc, t * 128:(t + 1) * 128], pc.ap(part, kc, c0, c0 + ncols), kc == 0,
               kc == KC - 1, [("hT", t), pc.key(part)], pskeys(b))
        return b

    class Pipe:
        def __init__(self):
            self.fl = []

        def push(self, stages):
            self.fl.append(list(stages))
            self.step()

        def step(self):
            for st_ in list(self.fl):
                if st_:
                    st_.pop(0)()
            self.fl = [s_ for s_ in self.fl if s_]

        def drain(self):
            while self.fl:
                self.step()

    bstage = hb[:].rearrange("p a d -> p (a d)").bitcast(F32)

    def chunk_attention(i, j):
        qT = scr[:, 0:4096].rearrange("p (c t) -> p c t", t=T)
        kT = scr[:, 4096:8192].rearrange("p (c t) -> p c t", t=T)
        v = scr[:, 8192:8192 + 4160].rearrange("p (t h d) -> p t h d", h=4, d=65)
        ao = scr[:, 12352:12352 + 16384].rearrange("p (c t) -> p c t", t=T)
        expB = scr[:, 28736:28736 + 2560].rearrange("p (h q) -> p h q", q=640)
        gb_loaded = [False]
        mix_guard = {S.last("pe")} if S.last("pe") is not None else None
        v_init = [False]
        e_ring = Ring(3)
        s_ring = Ring(2)
        on_ring = Ring(2)
        for g in range(4):
            S.tag = "L%d.g%d.proj" % (i, g)
            pqk = W.get((i, "qk", g))
            pv = W.get((i, "v", g))

            def build_expb(hh):
                dma("pool", bstage[:, 0:640], abias_d[j, 4 * g + hh], [], ["bstage", ("hb", 0), ("hb", 1)], "bias")
                act(expB[:, hh, :], bstage[:, 0:640], AF.Exp, ["bstage", ("hb", 0), ("hb", 1)], [("expB", hh)])

            for c in range(2):
                proj_fm(lambda tq, c=c: qT[:, c, tq * 512:(tq + 1) * 512], lambda tq, c=c: ("qT", c, tq), pqk, 0, c * 128, None)
                ln_drain()
                if not gb_loaded[0]:
                    gb_loaded[0] = True
                    load_gb(i, 0)
                build_expb(2 * c)
                build_expb(2 * c + 1)
                proj_fm(lambda tq, c=c: kT[:, c, tq * 512:(tq + 1) * 512], lambda tq, c=c: ("kT", c, tq), pqk, 1, c * 128, None)
            if not v_init[0]:
                v_init[0] = True
                S.guard = mix_guard
                S.add("dve", lambda q: q.memset(scr[:, 8192:8192 + 4160], 1.0), r=[], w=[("v", t) for t in range(NT)])
                S.guard = None
            for t in range(NT):
                b = proj_tm(t, pv, 0, 0, 256)
                cp("act" if t % 2 else "dve", v[:, t, :, 0:64], bank(b, 0, 256).rearrange("p (h d) -> p h d", d=64),
                   pskeys(b), [("v", t)])
            W.done((i, "qk", g))
            W.done((i, "v", g))

            pipe = Pipe()
            for qb in range(NT):
                for hh in range(4):
                    def mk(qb=qb, hh=hh):
                        c, po = hh // 2, (hh % 2) * 64
                        nkb = min(qb, 4) + 1
                        jlo = 5 - nkb
                        loc = {}
                        ob = 4 + (qb % 2)
                        tagb = "L%d.g%d.q%d.h%d." % (i, g, qb, hh)

                        def sA():
                            S.tag = tagb + "A"
                            ss = s_ring.next()
                            loc["ss"] = ss
                            base = ss * 1024
                            for jj in range(jlo, 5):
                                kb = qb - 4 + jj
                                mm(ps[:, base + jj * 128:base + (jj + 1) * 128], kT[po:po + 64, c, kb * 128:(kb + 1) * 128],
                                   qT[po:po + 64, c, qb * 128:(qb + 1) * 128], True, True,
                                   [("kT", c, kb // 4), ("qT", c, qb // 4)], pskeys(2 * ss + (1 if jj == 4 else 0)))

                        def sB():
                            S.tag = tagb + "B"
                            ss = loc["ss"]
                            base = ss * 1024
                            e = e_ring.next()
                            loc["e"] = e
                            act(ebuf[:, e, jlo * 128:640], ps[:, base + jlo * 128:base + 640], AF.Exp, pskeys(2 * ss, 2), [("E", e)],
                                scale=0.125)

                        def sB2():
                            S.tag = tagb + "B2"
                            e = loc["e"]
                            tt("dve", ebuf[:, e, jlo * 128:640], ebuf[:, e, jlo * 128:640], expB[:, hh, jlo * 128:640], ALU.mult,
                               [("E", e), ("expB", hh)], [("E", e)])

                        def sC():
                            S.tag = tagb + "C"
                            e = loc["e"]
                            for jj in range(jlo, 5):
                                kb = qb - 4 + jj
                                mm(bank(ob, hh * 65, 65), ebuf[:, e, jj * 128:(jj + 1) * 128], v[:, kb, hh, :], jj == jlo, jj == 4,
                                   [("E", e), ("v", kb)], pskeys(ob))

                        def sD1():
                            S.tag = tagb + "D1"
                            o3 = bank(ob, 0, 260).rearrange("p (h d) -> p h d", d=65)
                            si = st2_ring.next()
                            rec = stat2[:, si, 0:4]
                            S.add("dve", lambda q: q.reciprocal(out=rec, in_=o3[:, :, 64]), r=pskeys(ob), w=[("rec", si)])
                            r = on_ring.next()
                            loc["r"] = r
                            tt("dve", onb[:, r, :].rearrange("p (h d) -> p h d", d=64), o3[:, :, 0:64],
                               rec.unsqueeze(2).to_broadcast([128, 4, 64]), ALU.mult, pskeys(ob) + [("rec", si)], [("on", r)])

                        def sD2():
                            S.tag = tagb + "D2"
                            r = loc["r"]
                            tb = 6 + tb_ring.next()
                            loc["tb"] = tb
                            for cc in range(2):
                                tp(psb[:, tb * 1024 + cc * 128: tb * 1024 + (cc + 1) * 128], onb[:, r, cc * 128:(cc + 1) * 128],
                                   ident[:], [("on", r), "ident"], pskeys(tb))

                        def sD3():
                            S.tag = tagb + "D3"
                            tb = loc["tb"]
                            srcp = psb[:, tb * 1024: tb * 1024 + 256].rearrange("p (c t) -> p c t", t=128)
                            cp("dve", ao[:, 2 * g:2 * g + 2, qb * 128:(qb + 1) * 128], srcp, pskeys(tb),
                               [("ao", 2 * g, qb), ("ao", 2 * g + 1, qb)])

                        stages = [sA, sB, sB2, sC]
                        if hh == 3:
                            stages += [sD1, sD2, sD3]
                        return stages
                    pipe.push(mk())
            pipe.drain()
        return ao

    def diff_attention(i, j):
        lam_init = 0.8 - 0.6 * math.exp(-0.3 * i)
        qT = scr[:, 0:4096].rearrange("p (c t) -> p c t", t=T)
        kT = scr[:, 4096:8192].rearrange("p (c t) -> p c t", t=T)
        v = scr[:, 8192:8192 + 4128].rearrange("p (t h d) -> p t h d", h=2, d=129)
        ao = scr[:, 12352:12352 + 16384].rearrange("p (c t) -> p c t", t=T)
        expBp = scr[:, 28736:28736 + 512].rearrange("p (h q) -> p h q", q=256)
        f32z = scr[:, 30272:30272 + 1600].bitcast(F32)
        gC = f32z[:, 0:128]
        lamv = f32z[:, 128:384].rearrange("p (a d) -> p a d", d=64)
        t1 = f32z[:, 384:512]
        dtm3 = scr[:, 29248:29248 + 768].bitcast(F32).rearrange("p (a d) -> p a d", d=128)
        dt_ring = Ring(3)
        ltmp = f32z[:, 640:768]
        sc = f32z[:, 768:784]
        cc = f32z[:, 784:792]
        ncc = f32z[:, 792:800]
        gb_loaded = [False]
        dma("sp", gC, cgn_d, [], ["gC"], "cst2_%d" % i)
        dma("sp", lamv, clam_d, [], ["lamv"], "cst2_%d" % i)
        dma("sp", cc, ccc_d, [], ["cc"], "cst2_%d" % i)
        ts("dve", ncc, cc, -1.0, ALU.mult, ["cc"], ["ncc"])
        mix_guard = {S.last("pe")} if S.last("pe") is not None else None
        v_init = [False]
        for m in range(2):
            tt("dve", ltmp[:, m * 64:(m + 1) * 64], lamv[:, 2 * m, :], lamv[:, 2 * m + 1, :], ALU.mult, ["lamv"], [("ltmp", m)])
            S.add("dve", lambda q, m=m: q.reduce_sum(out=sc[:, m:m + 1], in_=ltmp[:, m * 64:(m + 1) * 64],
                                                     axis=mybir.AxisListType.X), r=[("ltmp", m)], w=[("sc", m)])
            act(sc[:, 2 + m:3 + m], sc[:, m:m + 1], AF.Exp, [("sc", m)], [("sc", 2 + m)])
        tt("dve", sc[:, 4:5], sc[:, 3:4], sc[:, 2:3], ALU.subtract, [("sc", 2), ("sc", 3)], [("sc", 4)])
        ts("dve", sc[:, 5:6], sc[:, 4:5], -float(lam_init), ALU.add, [("sc", 4)], ["nlam"])
        nlam = sc[:, 5:6]
        e_ring = Ring(3)
        s_ring = Ring(4)
        on_ring = Ring(2)
        ccn = float(1.0 - lam_init) ** 2
        for g in range(4):
            S.tag = "L%d.g%d.proj" % (i, g)
            pqk = W.get((i, "qk", g))
            pv = W.get((i, "v", g))

            def build_expb(hl):
                hd = 2 * g + hl
                dma("pool", bstage[:, 0:256], cbias_d[hd][:, 0:256], [], ["bstage", ("hb", 0), ("hb", 1)], "bias")
                act(expBp[:, hl, :], bstage[:, 0:256], AF.Exp, ["bstage", ("hb", 0), ("hb", 1), "ncc"], [("expB", hl)],
                    bias=ncc[:, hd:hd + 1])

            for c in range(2):
                proj_fm(lambda tq, c=c: qT[:, c, tq * 512:(tq + 1) * 512], lambda tq, c=c: ("qT", c, tq), pqk, 0, c * 128, 0.125)
                ln_drain()
                if not gb_loaded[0]:
                    gb_loaded[0] = True
                    load_gb(i, 0)
                build_expb(c)
                proj_fm(lambda tq, c=c: kT[:, c, tq * 512:(tq + 1) * 512], lambda tq, c=c: ("kT", c, tq), pqk, 1, c * 128, None)
            if not v_init[0]:
                v_init[0] = True
                S.guard = mix_guard
                S.add("dve", lambda q: q.memset(scr[:, 8192:8192 + 4128], 1.0), r=[], w=[("v", t) for t in range(NT)])
                S.guard = None
            for t in range(NT):
                b = proj_tm(t, pv, 0, 0, 256)
                cp("act" if t % 2 else "dve", v[:, t, :, 0:128], bank(b, 0, 256).rearrange("p (h d) -> p h d", d=128),
                   pskeys(b), [("v", t)])
            W.done((i, "qk", g))
            W.done((i, "v", g))

            pipe = Pipe()
            shared = {}
            for qb in range(NT):
                for hl in range(2):
                    for m in range(2):
                        nk4 = qb // 4 + 1
                        for k4 in range(nk4):
                            def mk(qb=qb, hl=hl, m=m, k4=k4, last=(k4 == nk4 - 1)):
                                hd = 2 * g + hl
                                po = m * 64
                                nb = min(4, qb + 1 - 4 * k4)
                                ob = 4 + hl
                                loc = {}
                                tagb = "L%d.g%d.q%d.h%d.m%d.k%d." % (i, g, qb, hl, m, k4)

                                def sA():
                                    S.tag = tagb + "A"
                                    sbk = s_ring.next()
                                    loc["s"] = sbk
                                    for jj in range(nb):
                                        kb = qb - (4 * k4 + jj)
                                        mm(bank(sbk, jj * 128, 128), kT[po:po + 64, hl, kb * 128:(kb + 1) * 128],
                                           qT[po:po + 64, hl, qb * 128:(qb + 1) * 128], True, True,
                                           [("kT", hl, kb // 4), ("qT", hl, qb // 4)], pskeys(sbk))
                                    if "warm" in DBG:
                                        mm(bank(7, 0, 512), ident[:], hT[:, 0, 0:512], True, True, ["ident"], pskeys(7))

                                def sB():
                                    S.tag = tagb + "B"
                                    sbk = loc["s"]
                                    e = e_ring.next()
                                    loc["e"] = e
                                    act(ebuf[:, e, 0:nb * 128], bank(sbk, 0, nb * 128), AF.Exp, pskeys(sbk) + ["cc"], [("E", e)],
                                        bias=cc[:, hd:hd + 1])

                                def sB2():
                                    S.tag = tagb + "B2"
                                    e = loc["e"]
                                    if k4 == 0:
                                        nmul = min(nb, 2)
                                        tt("dve", ebuf[:, e, 0:nmul * 128], ebuf[:, e, 0:nmul * 128], expBp[:, hl, 0:nmul * 128],
                                           ALU.mult, [("E", e), ("expB", hl)], [("E", e)])

                                def sC():
                                    S.tag = tagb + "C"
                                    e = loc["e"]
                                    for jj in range(nb):
                                        kb = qb - (4 * k4 + jj)
                                        mm(bank(ob, m * 129, 129), ebuf[:, e, jj * 128:(jj + 1) * 128], v[:, kb, hl, :],
                                           k4 == 0 and jj == 0, last and jj == nb - 1, [("E", e), ("v", kb)], pskeys(ob))

                                o3 = bank(ob, 0, 258).rearrange("p (h d) -> p h d", d=129)

                                def sD1a():
                                    S.tag = tagb + "D1a"
                                    si = st2_ring.next()
                                    loc["si"] = si
                                    di = dt_ring.next()
                                    loc["di"] = di
                                    dtm = dtm3[:, di, :]
                                    rec = stat2[:, si, 0:2]
                                    S.add("dve", lambda q: q.reciprocal(out=rec, in_=o3[:, :, 128]), r=pskeys(ob), w=[("rec", si)])
                                    tt("dve", stat2[:, si, 2:3], stat2[:, si, 1:2], nlam, ALU.mult, [("rec", si), "nlam"], [("rec2", si)])
                                    ts("dve", t1, o3[:, 0, 0:128], stat2[:, si, 0:1], ALU.mult, pskeys(ob) + [("rec", si)], ["t1"])
                                    stt("dve", dtm, o3[:, 1, 0:128], stat2[:, si, 2:3], t1, ALU.mult, ALU.add,
                                        pskeys(ob) + [("rec2", si), "t1"], [("dtm", di)])

                                def sD1b():
                                    S.tag = tagb + "D1b"
                                    si = loc["si"]
                                    di = loc["di"]
                                    act(ltmp, dtm3[:, di, :], AF.Square, [("dtm", di)], ["sqj", ("ss", si)], accum=stat2[:, si, 3:4])

                                def sD1c():
                                    S.tag = tagb + "D1c"
                                    si = loc["si"]
                                    ts("dve", stat2[:, si, 4:5], stat2[:, si, 3:4], 1.0 / (128.0 * ccn), ALU.mult, [("ss", si)],
                                       [("ms", si)], s2=RMS_EPS / ccn, op1=ALU.add)
                                    tt("pool", stat2[:, si, 5:6], stat2[:, si, 4:5], cst[:, 0:1], ALU.pow, [("ms", si), "cst"],
                                       [("rs", si)])

                                def sD1d():
                                    S.tag = tagb + "D1d"
                                    si = loc["si"]
                                    if hl == 0:
                                        shared[qb] = on_ring.next()
                                    r = shared[qb]
                                    di = loc["di"]
                                    stt("dve", onb[:, r, hl * 128:(hl + 1) * 128], dtm3[:, di, :], stat2[:, si, 5:6], gC, ALU.mult, ALU.mult,
                                        [("dtm", di), ("rs", si), "gC"], [("on", r, hl)])

                                def sD2():
                                    S.tag = tagb + "D2"
                                    r = shared[qb]
                                    tb = 6 if "warm" in DBG else 6 + tb_ring.next()
                                    loc["tb"] = tb
                                    for cc_ in range(2):
                                        tp(psb[:, tb * 1024 + cc_ * 128: tb * 1024 + (cc_ + 1) * 128],
                                           onb[:, r, cc_ * 128:(cc_ + 1) * 128], ident[:],
                                           [("on", r, 0), ("on", r, 1), "ident"], pskeys(tb))

                                def sD3():
                                    S.tag = tagb + "D3"
                                    tb = loc["tb"]
                                    srcp = psb[:, tb * 1024: tb * 1024 + 256].rearrange("p (c t) -> p c t", t=128)
                                    cp("act", ao[:, 2 * g:2 * g + 2, qb * 128:(qb + 1) * 128], srcp, pskeys(tb),
                                       [("ao", 2 * g, qb), ("ao", 2 * g + 1, qb)])

                                stages = [sA, sB, sB2, sC]
                                if last and m == 1:
                                    stages += [sD1a, sD1b, sD1c, sD1d]
                                    if hl == 1:
                                        stages += [sD2, sD3]
                                return stages
                            pipe.push(mk())
            pipe.drain()
        return ao

    def gla(i):
        qT2 = scr[:, 0:4096].rearrange("p (a t) -> p a t", t=T)
        g1T = scr[:, 4096:6144]
        w2 = scr[:, 6144:6656]
        fz = scr[:, 6656:6656 + 5456].bitcast(F32)
        gB = fz[:, 0:256]
        U = fz[:, 256:384]
        ones2 = fz[:, 384:386]

        def hbuf(k, off, n):
            base = 392 + k * 1168 + off
            return fz[:, base:base + n]
        bz = scr[:, 28736:28736 + 2048]
        ao = scr[:, 12352:12352 + 16384].rearrange("p (c t) -> p c t", t=T)
        S.guard = {S.last("pe")} if S.last("pe") is not None else None
        dma("sp", gB, bgn_d, [], ["gB"], "cst2_%d" % i)
        dma("sp", U, cU_d, [], ["U"], "cst2_%d" % i)
        dma("pool", w2[0:17, :], bg2_d, [], ["w2"], "bias")
        S.add("dve", lambda q: q.memset(ones2, 0.0), r=[], w=["ones2"])
        S.add("dve", lambda q: q.memset(ones2[0:64, 0:1], 1.0), r=["ones2"], w=["ones2"])
        S.add("dve", lambda q: q.memset(ones2[64:128, 1:2], 1.0), r=["ones2"], w=["ones2"])
        S.add("dve", lambda q: q.memset(g1T[0:32, :], 1.0), r=[], w=["g1T"])
        S.add("dve", lambda q: q.memset(bz[:, 0:512], 0.0), r=[], w=[("kdec", 0), ("kdec", 1)])
        S.guard = None
        pg1 = W.get((i, "g1"))
        for tq in range(4):
            need_hT(range(4 * tq, 4 * tq + 4))
            b = pj_ring.next()
            hk = [("hT", 4 * tq + u) for u in range(4)]
            for kc in range(KC):
                mm(ps[0:16, b * 512:(b + 1) * 512], pg1.ap(0, kc, 0, 16), hT[:, kc, tq * 512:(tq + 1) * 512],
                   kc == 0, kc == KC - 1, hk + [pg1.key(0)], pskeys(b))
            cp("dve", g1T[0:16, tq * 512:(tq + 1) * 512], ps[0:16, b * 512:(b + 1) * 512], pskeys(b) + ["g1T"], ["g1T"])
        W.done((i, "g1"))
        ln_drain()
        load_gb(i, 0)

        def head_gen(hd, k):
            P1, P2, P3 = 3 * k, 3 * k + 1, 3 * k + 2
            spb = hbuf(k, 0, 128)
            wdec = hbuf(k, 128, 128)
            decay = hbuf(k, 256, 8)
            state = hbuf(k, 264, 256)
            er = hbuf(k, 520, 256)
            onf = hbuf(k, 776, 256)
            e1 = hbuf(k, 1032, 128)
            ssq = hbuf(k, 1160, 8)
            kdec2 = bz[:, k * 256:(k + 1) * 256].rearrange("p (a d) -> p a d", d=128)
            vbf = bz[:, 512 + k * 256:512 + (k + 1) * 256]
            stbf = bz[:, 1024 + k * 512:1024 + (k + 1) * 512].rearrange("p (c d) -> p c d", d=256)
            K_ = lambda name: (name, k)
            pA = W.get((i, "qkv", hd))
            pR = W.get((i, "r", hd))
            proj_fm(lambda tq: qT2[:, k, tq * 512:(tq + 1) * 512], lambda tq: ("qT", k, tq), pA, 0, 0, 128.0 ** -0.5)
            S.add("dve", lambda q: q.memset(state, 0.0), r=[], w=[K_("state")])
            yield
            for t in range(NT):
                S.tag = "L%d.gla.h%d.t%d" % (i, hd, t)
                tok = slice(t * 128, (t + 1) * 128)
                need_hT([t])
                for kc in range(KC):
                    mm(bank(P1, 0, 128), hT[:, kc, tok], pA.ap(1, kc, 0, 128), kc == 0, kc == KC - 1,
                       [("hT", t), pA.key(1)], pskeys(P1))
                mm(bank(P1, 128, 128), g1T[0:17, tok], w2[0:17, hd * 128:(hd + 1) * 128], True, True, ["g1T", "w2"], pskeys(P1))
                for kc in range(KC):
                    mm(bank(P2, 0, 256), hT[:, kc, tok], pA.ap(2, kc, 0, 256), kc == 0, kc == KC - 1,
                       [("hT", t), pA.key(2)], pskeys(P2))
                for kc in range(KC):
                    mm(bank(P3, 0, 256), hT[:, kc, tok], pR.ap(0, kc, 0, 256), kc == 0, kc == KC - 1,
                       [("hT", t), pR.key(0)], pskeys(P3))
                yield
                act(e1, bank(P1, 128, 128), AF.Exp, pskeys(P1), [K_("e1")], scale=-1.0)
                cp("act", vbf, bank(P2, 0, 256), pskeys(P2), [K_("vbf")])
                act(er, bank(P3, 0, 256), AF.Exp, pskeys(P3), [K_("er")], scale=-1.0)
                yield
                act(spb, e1, AF.Ln, [K_("e1")], [K_("spb")], bias=1.0)
                act(er, er, AF.Ln, [K_("er")], [K_("er")], bias=1.0)
                yield
                mm(bank(P1, 256, 128), U, spb, True, True, ["U", K_("spb")], pskeys(P1))
                mm(bank(P1, 384, 2), spb, ones2, True, True, ["ones2", K_("spb")], pskeys(P1))
                act(er, er, AF.Exp, [K_("er")], [K_("er")], scale=-1.0)
                yield
                act(wdec, bank(P1, 256, 128), AF.Exp, pskeys(P1), [K_("wdec")], scale=-1.0 / 16.0)
                act(decay[:, 0:2], bank(P1, 384, 2), AF.Exp, pskeys(P1), [K_("decay")], scale=-1.0 / 16.0)
                yield
                for c in range(2):
                    lo, hi = c * 64, (c + 1) * 64
                    tt("dve", kdec2[lo:hi, c, :], ps[lo:hi, P1 * 512:P1 * 512 + 128], wdec[lo:hi, :], ALU.mult,
                       pskeys(P1) + [K_("wdec")], [("kdec", k)])
                yield
                mm(bank(P1, 0, 256), kdec2[:, 0, :], vbf, True, True, [("kdec", k), K_("vbf")], pskeys(P1))
                mm(bank(P2, 256, 256), kdec2[:, 1, :], vbf, True, True, [("kdec", k), K_("vbf")], pskeys(P2))
                yield
                kvs = [bank(P1, 0, 256), bank(P2, 256, 256)]
                kvk = [pskeys(P1), pskeys(P2)]
                for c in range(2):
                    stt("dve", state, state, decay[:, c:c + 1], kvs[c], ALU.mult, ALU.add,
                        [K_("state"), K_("decay")] + kvk[c], [K_("state")])
                    cp("dve", stbf[:, c, :], state, [K_("state")], [("stbf", k, c)])
                yield
                mm(bank(P1, 0, 256), qT2[:, k, tok], stbf[:, 0, :], True, True, [("qT", k, t // 4), ("stbf", k, 0)], pskeys(P1))
                mm(bank(P2, 0, 256), qT2[:, k, tok], stbf[:, 1, :], True, True, [("qT", k, t // 4), ("stbf", k, 1)], pskeys(P2))
                yield
                obk = [P1, P2]
                for c in range(2):
                    lo, hi = c * 64, (c + 1) * 64
                    act(onf[lo:hi, :], ps[lo:hi, obk[c] * 512:obk[c] * 512 + 256], AF.Square, pskeys(obk[c]),
                        [("onf", k, c), ("ssq", k, c)], accum=ssq[lo:hi, 0:1])
                yield
                ts("dve", ssq[:, 1:2], ssq[:, 0:1], 1.0 / 256.0, ALU.mult, [("ssq", k, 0), ("ssq", k, 1)], [K_("rs1")],
                   s2=RMS_EPS, op1=ALU.add)
                tt("pool", ssq[:, 2:3], ssq[:, 1:2], cst[:, 0:1], ALU.pow, [K_("rs1"), "cst"], [K_("rs2")])
                yield
                for c in range(2):
                    lo, hi = c * 64, (c + 1) * 64
                    stt("dve", onf[lo:hi, :], ps[lo:hi, obk[c] * 512:obk[c] * 512 + 256], ssq[lo:hi, 2:3],
                        gB[lo:hi, :], ALU.mult, ALU.mult, pskeys(obk[c]) + [K_("rs2"), "gB", ("onf", k, c)], [("onf", k, c)])
                tt("dve", er, er, bank(P3, 0, 256), ALU.mult, [K_("er")] + pskeys(P3), [K_("er")])
                tt("dve", onb[:, k, :], onf, er, ALU.mult, [("onf", k, 0), ("onf", k, 1), K_("er")], [("on", k)])
                yield
                for cc_ in range(2):
                    tp(psb[:, P3 * 1024 + 512 + cc_ * 128: P3 * 1024 + 512 + (cc_ + 1) * 128], onb[:, k, cc_ * 128:(cc_ + 1) * 128],
                       ident[:], [("on", k), "ident"], pskeys(P3))
                yield
                srcp = psb[:, P3 * 1024 + 512: P3 * 1024 + 768].rearrange("p (c t) -> p c t", t=128)
                cp("act", ao[:, 2 * hd:2 * hd + 2, t * 128:(t + 1) * 128], srcp, pskeys(P3),
                   [("ao", 2 * hd, t), ("ao", 2 * hd + 1, t)])
                yield
            W.done((i, "qkv", hd))
            W.done((i, "r", hd))

        for pair in ((0, 1), (2, 3)):
            gens = [head_gen(hd, k) for k, hd in enumerate(pair)]
            alive = list(gens)
            for gen in gens:
                next(gen)
            for _ in range(7):
                next(gens[0])
            while alive:
                for gen in reversed(list(alive)):
                    try:
                        next(gen)
                    except StopIteration:
                        alive.remove(gen)
        return ao

    for i in range(n_layers):
        kind = i % 3
        j = i // 3
        last_layer = (i == n_layers - 1)
        if kind == 0:
            ao = chunk_attention(i, j)
        elif kind == 1:
            ao = gla(i)
        else:
            ao = diff_attention(i, j)
        if ao is None:
            for t in range(NT):
                dma("sp", out_d[t * 128:(t + 1) * 128, :], h[:, t, :], [("h", t)], [("out", t)], "out")
            break
        out_proj_ln(i, ao, final=(last_layer and stop_mid))
        if not (last_layer and stop_mid):
            ffn(i, final=last_layer)

    ln_drain()
    S.fence("sp", [("out", t) for t in range(NT)])
    S.resolve()
    sems = {k: es.enter_context(nc.semaphore("s_%s_%s" % k)) for k in S.sem_keys()}
    with nc.Block() as block:
        S.emit(block, sems)
    es.close()
    return nc


def _t5_bucket_np(rel):
    nb = 16
    max_exact = 8
    ret = (rel > 0).astype(np.int32) * nb
    n = np.abs(rel)
    is_small = n < max_exact
    n_f = np.maximum(n, 1).astype(np.float32)
    large = max_exact + (np.log(n_f / np.float32(max_exact)) / np.float32(math.log(128 / max_exact))
                         * np.float32(nb - max_exact)).astype(np.int32)
    large = np.minimum(large, nb - 1)
    return ret + np.where(is_small, n, large)


def _prep_shared(inp):
    f = np.float32
    k_in = np.arange(128)[:, None]
    q_in = np.arange(128)[None, :]
    a_rel = np.asarray(inp["a_rel_bias"], f)
    a_bias = np.empty((2, 16, 128, 640), f)
    for jj in range(5):
        rel = q_in - k_in + (4 - jj) * 128
        idx = np.clip(rel, -128, 128) + 128
        vals = a_rel[:, idx, :]
        vals = np.transpose(vals, (0, 3, 1, 2))
        if jj == 4:
            mask = (k_in >= 64) & (q_in < 64)
        elif jj == 0:
            mask = (k_in < 64) & (q_in >= 64)
        else:
            mask = np.zeros((128, 128), bool)
        a_bias[:, :, :, jj * 128:(jj + 1) * 128] = np.where(mask[None, None], f(NEG), vals)
    t5 = np.asarray(inp["t5_table"], f)
    c_bias = np.empty((8, 128, 768), f)
    for bi, jd in enumerate([0, 1, 2, 2, 2, 2]):
        rel = (k_in - q_in - 128 * jd).astype(np.int32)
        vals = np.transpose(t5[_t5_bucket_np(rel)], (2, 0, 1))
        if jd == 0:
            vals = np.where(((k_in >= 64) & (q_in < 64))[None], f(NEG), vals)
        c_bias[:, :, bi * 128:(bi + 1) * 128] = vals
    assert (_t5_bucket_np(np.arange(-2048, -128)) == 15).all()
    lngb = np.stack([inp["ln1_g"], inp["ln1_b"], inp["ln2_g"], inp["ln2_b"]], axis=1).astype(f)
    lngb = np.ascontiguousarray(np.broadcast_to(lngb[:, :, None, :], (DEPTH, 4, 128, D)))
    lam = np.stack([inp["c_lam_q1"][0], inp["c_lam_k1"][0], inp["c_lam_q2"][0], inp["c_lam_k2"][0]], 0).astype(f)
    s = np.arange(128)
    U = ((s[:, None] // 64 == s[None, :] // 64) & (s[:, None] > s[None, :])).astype(f)
    return {
        "lngb": lngb,
        "ffn_gu": np.ascontiguousarray(inp["ffn_w_gate_up"], f),
        "ffn_d": np.ascontiguousarray(inp["ffn_w_down"], f),
        "a_qkv": np.ascontiguousarray(inp["a_w_qkv"], f),
        "a_o": np.ascontiguousarray(inp["a_w_o"], f),
        "a_bias": a_bias,
        "b_in": np.ascontiguousarray(inp["b_w_in"][0], f),
        "b_g1": np.ascontiguousarray(inp["b_w_g1"][0], f),
        "b_g2aug": np.ascontiguousarray(np.concatenate([inp["b_w_g2"][0], inp["b_b_g"][0][None, :]], 0), f),
        "b_gn": np.ascontiguousarray(np.broadcast_to(np.asarray(inp["b_g_norm"][0], f)[None, :], (128, 256))),
        "b_o": np.ascontiguousarray(inp["b_w_o"][0], f),
        "c_qkv": np.ascontiguousarray(inp["c_w_qkv"][0], f),
        "c_o": np.ascontiguousarray(inp["c_w_o"][0], f),
        "c_bias": c_bias,
        "c_cc": np.ascontiguousarray(np.broadcast_to(t5[15, :][None, :], (128, 8))),
        "c_gn": np.ascontiguousarray(np.broadcast_to(np.asarray(inp["c_g_norm"][0], f)[None, :], (128, 128))),
        "c_lam": np.ascontiguousarray(np.broadcast_to(lam[None], (128, 4, 64))),
        "k_ident": np.eye(128, dtype=f),
        "k_U": U,
    }


_NC_CACHE = {}


def kernel(_n_layers=DEPTH, _stop_mid=False, _cores=8, **inputs):
    inp = {k: np.asarray(v) for k, v in inputs.items()}
    shared = _prep_shared(inp)
    key = (_n_layers, _stop_mid)
    if key not in _NC_CACHE:
        _NC_CACHE[key] = build_program(_n_layers, _stop_mid)
    nc = _NC_CACHE[key]
    x = np.asarray(inp["x"], np.float32)
    in_maps = []
    for b in range(_cores):
        m = dict(shared)
        m["x"] = np.ascontiguousarray(x[b])
        in_maps.append(m)
    res = run_bass_kernel_spmd(nc, in_maps, core_ids=list(range(_cores)))
    return np.stack([res.results[b]["out"] for b in range(_cores)], axis=0)
```

```python
import math
import bisect
import contextlib
import numpy as np
import concourse.bass as bass
import concourse.mybir as mybir
from concourse.bass_utils import run_bass_kernel_spmd

F32 = mybir.dt.float32
BF16 = mybir.dt.bfloat16
AF = mybir.ActivationFunctionType
ALU = mybir.AluOpType

D = 1024
T = 2048
NT = 16
KC = 8
DEPTH = 4
FF = 2816
DN_ALPHA = (2.0 * DEPTH) ** 0.25
LN_EPS = 1e-5
RMS_EPS = 1e-6
NEG = -30000.0
DBG = set()

ENGS = ("pe", "act", "dve", "pool", "sp")


class Op:
    __slots__ = ("eng", "fn", "deps", "idx", "gidx", "signal", "cnt", "dma", "waits", "tag")


class Sched:
    def __init__(self):
        self.q = {e: [] for e in ENGS}
        self.lastw = {}
        self.readers = {}
        self.g = 0
        self.dma_gidx = {}
        self.tag = ""
        self.guard = None

    def last(self, eng):
        for op in reversed(self.q[eng]):
            if op.fn is not None and op.dma is None:
                return op
        return None

    def collect(self, keys):
        deps = set()
        for k in keys:
            d = self.lastw.get(k)
            if d is not None:
                deps.add(d)
            rd = self.readers.get(k)
            if rd:
                for x in rd.values():
                    if isinstance(x, list):
                        deps.update(x)
                    else:
                        deps.add(x)
        return deps

    def add(self, eng, fn, r=(), w=(), dma=None, extra=None):
        op = Op()
        op.eng = eng
        op.fn = fn
        op.idx = len(self.q[eng])
        op.gidx = self.g
        self.g += 1
        op.dma = dma
        op.signal = False
        op.cnt = 0
        op.tag = self.tag
        deps = set()
        for k in r:
            d = self.lastw.get(k)
            if d is not None:
                deps.add(d)
        for k in w:
            d = self.lastw.get(k)
            if d is not None:
                deps.add(d)
            rd = self.readers.get(k)
            if rd:
                for x in rd.values():
                    if isinstance(x, list):
                        deps.update(x)
                    else:
                        deps.add(x)
        if extra:
            deps |= extra
        if self.guard:
            deps |= self.guard
        op.deps = deps
        for k in w:
            self.lastw[k] = op
            self.readers[k] = {}
        for k in r:
            rd = self.readers.setdefault(k, {})
            if dma is not None:
                rd.setdefault("dma", []).append(op)
            else:
                rd[eng] = op
        self.q[eng].append(op)
        if dma is not None:
            self.dma_gidx.setdefault(dma, []).append(op.gidx)
        return op

    def fence(self, eng, keys):
        return self.add(eng, None, r=keys)

    def barrier(self, engs=("pe", "act", "dve")):
        lasts = []
        for e in engs:
            for op in reversed(self.q[e]):
                if op.fn is not None and op.dma is None:
                    lasts.append(op)
                    break
        for e in engs:
            op = self.add(e, None)
            op.deps = set(x for x in lasts if x.eng != e)

    def resolve(self):
        for e in ENGS:
            for op in self.q[e]:
                keep = []
                for d in op.deps:
                    if d.dma is not None:
                        keep.append(d)
                        continue
                    if d.fn is None:
                        continue
                    if d.eng == op.eng:
                        if op.eng == "pe":
                            continue
                    d.signal = True
                    keep.append(d)
                op.deps = keep
        for e in ENGS:
            c = 0
            for op in self.q[e]:
                if op.signal and op.dma is None:
                    c += 1
                op.cnt = c
        for e in ENGS:
            waited = {}
            for op in self.q[e]:
                need = {}
                for d in op.deps:
                    if d.dma is not None:
                        lst = self.dma_gidx[d.dma]
                        n = bisect.bisect_left(lst, op.gidx)
                        key = ("dma", d.dma)
                        val = 16 * n
                    else:
                        key = ("eng", d.eng)
                        val = d.cnt
                    if val > need.get(key, 0):
                        need[key] = val
                op.waits = []
                for key, val in need.items():
                    if waited.get(key, 0) >= val:
                        continue
                    waited[key] = val
                    op.waits.append((key, val))

    def sem_keys(self):
        keys = [("eng", e) for e in ENGS if e != "sp"]
        keys += [("dma", k) for k in self.dma_gidx]
        return keys

    def emit(self, block, sems):
        engmap = {"pe": block.tensor, "act": block.scalar, "dve": block.vector,
                  "pool": block.gpsimd, "sp": block.sync}
        for e in ENGS:
            ops = self.q[e]
            if not ops:
                continue

            def body(eng, ops=ops, e=e):
                for op in ops:
                    for key, val in op.waits:
                        eng.wait_ge(sems[key], val)
                    if op.fn is None:
                        continue
                    ins = op.fn(eng)
                    if op.dma is not None:
                        ins.then_inc(sems[("dma", op.dma)], 16)
                    elif op.signal:
                        ins.then_inc(sems[("eng", e)], 1)

            engmap[e](body)


class Ring:
    def __init__(self, n):
        self.n = n
        self.i = -1

    def next(self):
        self.i = (self.i + 1) % self.n
        return self.i


class Piece:
    def __init__(self, pid, nkc, ncols, parts):
        self.pid = pid
        self.nkc = nkc
        self.ncols = ncols
        self.parts = parts
        self.slot = None
        self.view = None

    def key(self, part):
        return ("w", self.slot, part)

    def ap(self, part, kc, a, b):
        _, k0, pk, c0, pw = self.parts[part]
        return self.view[:, k0 + kc, c0 + a:c0 + b]


class WLoader:
    NSLOT = 4
    SLOT_ELEMS = 8 * 512

    def __init__(self, S, wsl):
        self.S = S
        self.wsl = wsl
        self.plan = []
        self.byid = {}
        self.free = list(range(self.NSLOT))
        self.nxt = 0

    def add(self, piece):
        assert piece.nkc * piece.ncols <= self.SLOT_ELEMS
        self.plan.append(piece)
        self.byid[piece.pid] = piece

    def pump(self):
        while self.free and self.nxt < len(self.plan):
            s = self.free.pop(0)
            p = self.plan[self.nxt]
            self.nxt += 1
            p.slot = s
            p.view = self.wsl[:, s, 0:p.nkc * p.ncols].rearrange("p (k c) -> p k c", c=p.ncols)
            extra = self.S.collect([("w", s, 0), ("w", s, 1), ("w", s, 2)])
            for pi, (src, k0, pk, c0, pw) in enumerate(p.parts):
                dst = p.view[:, k0:k0 + pk, c0:c0 + pw]
                self._dma(dst, src, [("w", s, pi)], "w%d" % s, extra)

    def _dma(self, dst, src, wkeys, sem, extra):
        self.S.add("pool", lambda q: q.dma_start(out=dst, in_=src), w=wkeys, dma=sem, extra=set(extra))

    def get(self, pid):
        p = self.byid[pid]
        assert p.slot is not None, ("weight piece not yet issued", pid)
        return p

    def done(self, pid):
        p = self.byid[pid]
        self.free.append(p.slot)
        self.pump()


def build_program(n_layers=DEPTH, stop_mid=False):
    nc = bass.Bass("TRN2", target_bir_lowering=False)

    def din(name, shape):
        return nc.dram_tensor(name, list(shape), F32, kind="ExternalInput").ap()

    x_d = din("x", [T, D])
    lngb_d = din("lngb", [DEPTH, 4, 128, D])
    gu_d = din("ffn_gu", [DEPTH, D, 2 * FF])
    dn_d = din("ffn_d", [DEPTH, FF, D])
    aqkv_d = din("a_qkv", [2, D, 3 * D])
    ao_d = din("a_o", [2, D, D])
    abias_d = din("a_bias", [2, 16, 128, 640])
    bin_d = din("b_in", [D, 3 * D])
    bg1_d = din("b_g1", [D, 16])
    bg2_d = din("b_g2aug", [17, 512])
    bgn_d = din("b_gn", [128, 256])
    bo_d = din("b_o", [D, D])
    cqkv_d = din("c_qkv", [D, 3 * D])
    co_d = din("c_o", [D, D])
    cbias_d = din("c_bias", [8, 128, 768])
    cgn_d = din("c_gn", [128, 128])
    ccc_d = din("c_cc", [128, 8])
    clam_d = din("c_lam", [128, 4, 64])
    cident_d = din("k_ident", [128, 128])
    cU_d = din("k_U", [128, 128])
    out_d = nc.dram_tensor("out", [T, D], F32, kind="ExternalOutput").ap()

    S = Sched()
    es = contextlib.ExitStack()

    def sb(name, shape, dt):
        return es.enter_context(nc.sbuf_tensor(name, list(shape), dt))

    h = sb("h", [128, NT, D], F32)
    hT = sb("hT", [128, KC, T], BF16)
    wsl = sb("wsl", [128, WLoader.NSLOT, WLoader.SLOT_ELEMS], BF16)
    gb = sb("gb", [128, 2, D], F32)
    SCRN = 31872
    scr = sb("scr", [128, SCRN], BF16)
    ident = sb("ident", [128, 128], BF16)
    hb = sb("hb", [128, 2, D], BF16)
    ebuf = sb("ebuf", [128, 3, 640], BF16)
    onb = sb("onb", [128, 2, 256], BF16)
    stat = sb("stat", [128, 4, 8], F32)
    bst = sb("bst", [128, 4, 2, 6], F32)
    stat2 = sb("stat2", [128, 4, 8], F32)
    cst = sb("cst", [128, 4], F32)
    ps = es.enter_context(nc.psum_tensor("ps", [128, 4096], F32))
    psb = ps.bitcast(BF16)

    W = WLoader(S, wsl)

    def mm(out, lhsT, rhs, start, stop, r, w):
        S.add("pe", lambda q: q.matmul(out, lhsT=lhsT, rhs=rhs, start=start, stop=stop), r=r, w=w)

    def tp(out, in_, idn, r, w):
        S.add("pe", lambda q: q.transpose(out=out, in_=in_, identity=idn), r=r, w=w)

    def act(out, in_, func, r, w, scale=1.0, bias=None, accum=None):
        def f(q):
            kw = {}
            if bias is not None:
                kw["bias"] = bias
            if accum is not None:
                kw["accum_out"] = accum
            return q.activation(out=out, in_=in_, func=func, scale=scale, **kw)
        S.add("act", f, r=r, w=w)

    def tt(eng, out, in0, in1, op, r, w):
        S.add(eng, lambda q: q.tensor_tensor(out=out, in0=in0, in1=in1, op=op), r=r, w=w)

    def ts(eng, out, in0, s1, op0, r, w, s2=None, op1=None):
        if s2 is None:
            S.add(eng, lambda q: q.tensor_scalar(out=out, in0=in0, scalar1=s1, scalar2=None, op0=op0), r=r, w=w)
        else:
            S.add(eng, lambda q: q.tensor_scalar(out=out, in0=in0, scalar1=s1, scalar2=s2, op0=op0, op1=op1), r=r, w=w)

    def stt(eng, out, in0, scalar, in1, op0, op1, r, w):
        S.add(eng, lambda q: q.scalar_tensor_tensor(out=out, in0=in0, scalar=scalar, in1=in1, op0=op0, op1=op1), r=r, w=w)

    def cp(eng, out, in_, r, w):
        if eng == "act":
            S.add("act", lambda q: q.copy(out=out, in_=in_), r=r, w=w)
        else:
            S.add(eng, lambda q: q.tensor_copy(out=out, in_=in_), r=r, w=w)

    def dma(eng, out, in_, r, w, sem):
        S.add(eng, lambda q: q.dma_start(out=out, in_=in_), r=r, w=w, dma=sem)

    def bank(b, a=0, n=512):
        return ps[:, b * 512 + a:b * 512 + a + n]

    def pskeys(b0, nb=1):
        return [("ps", b0 + i) for i in range(nb)]

    def wsrc(mat, r0, nk, c0, w):
        return mat[r0:r0 + nk * 128, c0:c0 + w].rearrange("(k p) c -> p k c", p=128)

    def plan_ffn(i):
        for grp in range(2):
            for sp in range(6):
                nf = 2 if sp < 5 else 1
                f0 = grp * 11 + 2 * sp
                W.add(Piece((i, "gu", grp, sp), 8, 512, [
                    (wsrc(gu_d[i], 0, 8, f0 * 128, nf * 128), 0, 8, 0, nf * 128),
                    (wsrc(gu_d[i], 0, 8, FF + f0 * 128, nf * 128), 0, 8, 256, nf * 128)]))
            for half in range(2):
                W.add(Piece((i, "dA", grp, half), 8, 512, [
                    (wsrc(dn_d[i], grp * 1408, 8, half * 512, 512), 0, 8, 0, 512)]))
            W.add(Piece((i, "dB", grp), 6, 512, [
                (wsrc(dn_d[i], grp * 1408 + 1024, 3, 0, 512), 0, 3, 0, 512),
                (wsrc(dn_d[i], grp * 1408 + 1024, 3, 512, 512), 3, 3, 0, 512)]))

    def plan_o(i, mat):
        for half in range(2):
            W.add(Piece((i, "o", half), 8, 512, [(wsrc(mat, 0, 8, half * 512, 512), 0, 8, 0, 512)]))

    for i in range(n_layers):
        kind = i % 3
        j = i // 3
        if kind == 0:
            for g in range(4):
                W.add(Piece((i, "qk", g), 8, 512, [
                    (wsrc(aqkv_d[j], 0, 8, g * 256, 256), 0, 8, 0, 256),
                    (wsrc(aqkv_d[j], 0, 8, D + g * 256, 256), 0, 8, 256, 256)]))
                W.add(Piece((i, "v", g), 8, 256, [(wsrc(aqkv_d[j], 0, 8, 2 * D + g * 256, 256), 0, 8, 0, 256)]))
            plan_o(i, ao_d[j])
        elif kind == 1:
            W.add(Piece((i, "g1"), 8, 16, [(wsrc(bg1_d, 0, 8, 0, 16), 0, 8, 0, 16)]))
            for hd in range(4):
                W.add(Piece((i, "qkv", hd), 8, 512, [
                    (wsrc(bin_d, 0, 8, hd * 128, 128), 0, 8, 0, 128),
                    (wsrc(bin_d, 0, 8, 512 + hd * 128, 128), 0, 8, 128, 128),
                    (wsrc(bin_d, 0, 8, 1024 + hd * 256, 256), 0, 8, 256, 256)]))
                W.add(Piece((i, "r", hd), 8, 256, [(wsrc(bin_d, 0, 8, 2048 + hd * 256, 256), 0, 8, 0, 256)]))
            plan_o(i, bo_d)
        else:
            for g in range(4):
                W.add(Piece((i, "qk", g), 8, 512, [
                    (wsrc(cqkv_d, 0, 8, g * 256, 256), 0, 8, 0, 256),
                    (wsrc(cqkv_d, 0, 8, D + g * 256, 256), 0, 8, 256, 256)]))
                W.add(Piece((i, "v", g), 8, 256, [(wsrc(cqkv_d, 0, 8, 2 * D + g * 256, 256), 0, 8, 0, 256)]))
            plan_o(i, co_d)
        if not (stop_mid and i == n_layers - 1):
            plan_ffn(i)

    S.add("dve", lambda q: q.memset(cst[:, 0:1], -0.5), r=[], w=["cst"])
    dma("pool", ident[:], cident_d, [], ["ident"], "cst")
    W.pump()

    for t in range(NT):
        dma("sp", h[:, t, :], x_d[t * 128:(t + 1) * 128, :], [], [("h", t)], "xin%d" % (t // 2))

    hb_ring = Ring(2)
    tb_ring = Ring(2)
    evac_flip = [0]

    def to_featmajor(src_keys, t, nch, dst_fn, dst_keys, src_ap_fn, evac_eng=None):
        tb = 6 + tb_ring.next()
        for c in range(nch):
            tp(psb[:, tb * 1024 + c * 128: tb * 1024 + (c + 1) * 128], src_ap_fn(c), ident[:],
               list(src_keys) + ["ident"], pskeys(tb))
        src = psb[:, tb * 1024: tb * 1024 + nch * 128].rearrange("p (c t) -> p c t", t=128)
        evac_flip[0] ^= 1
        if evac_eng is None:
            evac_eng = "act" if evac_flip[0] else "dve"
        cp(evac_eng, dst_fn(), src, pskeys(tb), dst_keys)

    def make_hT(t):
        r = hb_ring.next()
        cp("act", hb[:, r, :], h[:, t, :], [("h", t)], [("hb", r)])
        to_featmajor([("hb", r)], t, 8, lambda: hT[:, :, t * 128:(t + 1) * 128], [("hT", t)],
                     lambda c: hb[:, r, c * 128:(c + 1) * 128])

    init_hT = []

    st_ring = Ring(4)
    st2_ring = Ring(4)

    ln_pipe = []

    ln_backlog = []

    def ln_step():
        if ln_backlog:
            ln_pipe.append(ln_backlog.pop(0))
        for ent in list(ln_pipe):
            if ent[1]:
                ent[1].pop(0)()
        ln_pipe[:] = [ent for ent in ln_pipe if ent[1]]

    def ln_drain():
        while ln_pipe or ln_backlog:
            ln_step()

    def need_hT(tiles):
        tiles = set(tiles)
        while any(ent[0] in tiles for ent in ln_pipe + ln_backlog):
            ln_step()

    def ln_tick():
        if ln_pipe or ln_backlog:
            ln_step()

    def _mk_init(t):
        loc = {}

        def sa():
            r = hb_ring.next()
            loc["r"] = r
            cp("act", hb[:, r, :], h[:, t, :], [("h", t)], [("hb", r)])

        def sb():
            r = loc["r"]
            to_featmajor([("hb", r)], t, 8, lambda: hT[:, :, t * 128:(t + 1) * 128], [("hT", t)],
                         lambda c: hb[:, r, c * 128:(c + 1) * 128])
        return [t, [sa, sb]]

    pend = None
    for t in range(NT + 1):
        cur = None
        if t < NT:
            r = hb_ring.next()
            eng = "act" if t % 2 == 0 else "dve"
            cp(eng, hb[:, r, :], h[:, t, :], [("h", t)], [("hb", r)])
            cur = (t, r, "dve" if eng == "act" else "act")
        if pend is not None:
            pt, pr, pe_ = pend
            to_featmajor([("hb", pr)], pt, 8, lambda pt=pt: hT[:, :, pt * 128:(pt + 1) * 128], [("hT", pt)],
                         lambda c, pr=pr: hb[:, pr, c * 128:(c + 1) * 128], evac_eng=pe_)
        pend = cur

    def layer_norm(t, y_ap, ykeys, alpha, final):
        hk = [("h", t)]
        hv = h[:, t, :]
        si = st_ring.next()
        mv = stat[:, si, 0:2]
        rstd = stat[:, si, 2:3]
        nmr = stat[:, si, 3:4]
        ve = stat[:, si, 4:5]

        def s1():
            S.tag = "ln.t%d.s1" % t
            stt("dve", hv, hv, float(alpha), y_ap, ALU.mult, ALU.add, hk + ykeys, hk)
            for k in range(2):
                S.add("dve", lambda q, k=k: q.bn_stats(out=bst[:, si, k, :], in_=h[:, t, k * 512:(k + 1) * 512]),
                      r=hk, w=[("bst", si, k)])
            S.add("dve", lambda q: q.bn_aggr(out=mv, in_=bst[:, si, :, :]), r=[("bst", si, 0), ("bst", si, 1)],
                  w=[("mv", si)])
            ts("dve", ve, stat[:, si, 1:2], LN_EPS, ALU.add, [("mv", si)], [("ve", si)])

        def s2():
            S.tag = "ln.t%d.s2" % t
            tt("pool", rstd, ve, cst[:, 0:1], ALU.pow, [("ve", si), "cst"], [("rstd", si)])
            stt("dve", nmr, stat[:, si, 0:1], -1.0, rstd, ALU.mult, ALU.mult, [("mv", si), ("rstd", si)], [("nmr", si)])
            act(hv, hv, AF.Identity, hk + [("rstd", si), ("nmr", si)], hk, scale=rstd, bias=nmr)

        def s3():
            S.tag = "ln.t%d.s3" % t
            tt("pool", hv, hv, gb[:, 0, :], ALU.mult, hk + [("gb", 0)], hk)

        def s4():
            S.tag = "ln.t%d.s4" % t
            tt("dve", hv, hv, gb[:, 1, :], ALU.add, hk + [("gb", 1)], hk)
            if final:
                dma("sp", out_d[t * 128:(t + 1) * 128, :], hv, hk, [("out", t)], "out")
            else:
                r = hb_ring.next()
                hbr[0] = r
                cp("act", hb[:, r, :], h[:, t, :], [("h", t)], [("hb", r)])

        def s5():
            S.tag = "ln.t%d.s5" % t
            if not final:
                r = hbr[0]
                to_featmajor([("hb", r)], t, 8, lambda: hT[:, :, t * 128:(t + 1) * 128], [("hT", t)],
                             lambda c: hb[:, r, c * 128:(c + 1) * 128])

        hbr = [0]
        ln_pipe.append([t, [s1, s2, s3, s4, s5]])
        ln_step()

    def load_gb(i, which):
        dma("sp", gb[:, 0, :], lngb_d[i, 2 * which, :, :], [], [("gb", 0)], "gb%d_%d" % (i, which))
        dma("sp", gb[:, 1, :], lngb_d[i, 2 * which + 1, :, :], [], [("gb", 1)], "gb%d_%d" % (i, which))

    yset_ring = Ring(3)

    def out_proj_ln(i, ao, final):
        pO = [W.get((i, "o", 0)), W.get((i, "o", 1))]
        for t in range(NT):
            S.tag = "L%d.oproj.t%d" % (i, t)
            ys = yset_ring.next()
            for half in range(2):
                b = 2 * ys + half
                for kc in range(KC):
                    mm(bank(b), ao[:, kc, t * 128:(t + 1) * 128], pO[half].ap(0, kc, 0, 512), kc == 0, kc == KC - 1,
                       [("ao", kc, t), pO[half].key(0)], pskeys(b))
            layer_norm(t, ps[:, ys * 1024:(ys + 1) * 1024], pskeys(2 * ys, 2), DN_ALPHA, final)
        if final:
            ln_drain()
        W.done((i, "o", 0))
        W.done((i, "o", 1))

    def ffn(i, final):
        actT = scr[:, 0:11 * T].rearrange("p (f t) -> p f t", t=T)
        sgt = scr[:, 11 * T:11 * T + 2048].bitcast(F32).rearrange("p (a d) -> p a d", d=512)
        sg_ring = Ring(2)
        nsg = [0]
        ffn_guard = {S.last("pe")}
        for grp in range(2):
            for sp in range(6):
                nf = 2 if sp < 5 else 1
                pc = W.get((i, "gu", grp, sp))
                S.tag = "L%d.ffn.gu%d.%d" % (i, grp, sp)
                for tq in range(4):
                  for fi in range(nf):
                        fc = 2 * sp + fi
                        need_hT(range(4 * tq, 4 * tq + 4))
                        ys = yset_ring.next()
                        bg, bu = 2 * ys, 2 * ys + 1
                        hk = [("hT", 4 * tq + u) for u in range(4)]
                        for kc in range(KC):
                            mm(bank(bg), pc.ap(0, kc, fi * 128, fi * 128 + 128), hT[:, kc, tq * 512:(tq + 1) * 512],
                               kc == 0, kc == KC - 1, hk + [pc.key(0)], pskeys(bg))
                        for kc in range(KC):
                            mm(bank(bu), pc.ap(1, kc, fi * 128, fi * 128 + 128), hT[:, kc, tq * 512:(tq + 1) * 512],
                               kc == 0, kc == KC - 1, hk + [pc.key(1)], pskeys(bu))
                        si = sg_ring.next()
                        if nsg[0] < 2:
                            S.guard = ffn_guard
                        nsg[0] += 1
                        act(sgt[:, si, :], bank(bg), AF.Silu, pskeys(bg), [("sgt", si)])
                        S.guard = None
                        tt("dve", actT[:, fc, tq * 512:(tq + 1) * 512], sgt[:, si, :], bank(bu), ALU.mult,
                           [("sgt", si)] + pskeys(bu), [("actT", fc, tq)])
                        ln_tick()
                W.done((i, "gu", grp, sp))
            if "ffn_a" in DBG:
                for t in range(NT):
                    dma("sp", out_d[t * 128:(t + 1) * 128, :], h[:, t, :], [("h", t)], [("out", t)], "out")
                return
            pA = [W.get((i, "dA", grp, 0)), W.get((i, "dA", grp, 1))]
            pB = W.get((i, "dB", grp))
            if grp == 1 and "ffn_e" not in DBG:
                load_gb(i, 1)
            for t in range(NT):
                S.tag = "L%d.ffn.dn%d.t%d" % (i, grp, t)
                ys = yset_ring.next()
                for half in range(2):
                    b = 2 * ys + half
                    for kc in range(11):
                        if kc < 8:
                            rhs = pA[half].ap(0, kc, 0, 512)
                            wk = pA[half].key(0)
                        else:
                            rhs = pB.ap(half, kc - 8, 0, 512)
                            wk = pB.key(half)
                        mm(bank(b), actT[:, kc, t * 128:(t + 1) * 128], rhs, kc == 0, kc == 10,
                           [("actT", kc, t // 4), wk], pskeys(b))
                y_ap = ps[:, ys * 1024:(ys + 1) * 1024]
                if grp == 0 or "ffn_c" in DBG:
                    hk = [("h", t)]
                    stt("dve", h[:, t, :], h[:, t, :], float(DN_ALPHA if grp == 0 else 1.0), y_ap, ALU.mult, ALU.add,
                        hk + pskeys(2 * ys, 2), hk)
                else:
                    layer_norm(t, y_ap, pskeys(2 * ys, 2), 1.0, final and "ffn_d" not in DBG)
            if final or grp == 0:
                ln_drain()
            if "ffn_d" in DBG and grp == 1:
                for t in range(NT):
                    dma("sp", out_d[t * 128:(t + 1) * 128, :], h[:, t, :], [("h", t)], [("out", t)], "out")
            if "ffn_b" in DBG or ("ffn_c" in DBG and grp == 1):
                for t in range(NT):
                    dma("sp", out_d[t * 128:(t + 1) * 128, :], h[:, t, :], [("h", t)], [("out", t)], "out")
                return
            W.done((i, "dA", grp, 0))
            W.done((i, "dA", grp, 1))
            W.done((i, "dB", grp))

    pj_ring = Ring(4)

    def proj_fm(dst_fn, dkey_fn, pc, part, c0, scale):
        for tq in range(4):
            need_hT(range(4 * tq, 4 * tq + 4))
            b = pj_ring.next()
            hk = [("hT", 4 * tq + u) for u in range(4)]
            for kc in range(KC):
                mm(bank(b), pc.ap(part, kc, c0, c0 + 128), hT[:, kc, tq * 512:(tq + 1) * 512], kc == 0, kc == KC - 1,
                   hk + [pc.key(part)], pskeys(b))
            ln_tick()
            if scale is None:
                cp("dve", dst_fn(tq), bank(b), pskeys(b), [dkey_fn(tq)])
            else:
                S.add("act", lambda q, o=dst_fn(tq), b=b: q.mul(out=o, in_=bank(b), mul=float(scale)),
                      r=pskeys(b), w=[dkey_fn(tq)])

    def proj_tm(t, pc, part, c0, ncols):
        need_hT([t])
        b = pj_ring.next()
        for kc in range(KC):
            mm(bank(b, 0, ncols), hT[:, kc, t * 128:(t + 1) * 128], pc.ap(part, kc, c0, c0 + ncols), kc == 0,
               kc == KC - 1, [("hT", t), pc.key(part)], pskeys(b))
        return b

    class Pipe:
        def __init__(self):
            self.fl = []

        def push(self, stages):
            self.fl.append(list(stages))
            self.step()

        def step(self):
            for st_ in list(self.fl):
                if st_:
                    st_.pop(0)()
            self.fl = [s_ for s_ in self.fl if s_]

        def drain(self):
            while self.fl:
                self.step()

    bstage = hb[:].rearrange("p a d -> p (a d)").bitcast(F32)

    def chunk_attention(i, j):
        qT = scr[:, 0:4096].rearrange("p (c t) -> p c t", t=T)
        kT = scr[:, 4096:8192].rearrange("p (c t) -> p c t", t=T)
        v = scr[:, 8192:8192 + 4160].rearrange("p (t h d) -> p t h d", h=4, d=65)
        ao = scr[:, 12352:12352 + 16384].rearrange("p (c t) -> p c t", t=T)
        expB = scr[:, 28736:28736 + 2560].rearrange("p (h q) -> p h q", q=640)
        gb_loaded = [False]
        mix_guard = {S.last("pe")} if S.last("pe") is not None else None
        v_init = [False]
        e_ring = Ring(3)
        s_ring = Ring(2)
        on_ring = Ring(2)
        for g in range(4):
            S.tag = "L%d.g%d.proj" % (i, g)
            pqk = W.get((i, "qk", g))
            pv = W.get((i, "v", g))

            def build_expb(hh):
                dma("pool", bstage[:, 0:640], abias_d[j, 4 * g + hh], [], ["bstage", ("hb", 0), ("hb", 1)], "bias")
                act(expB[:, hh, :], bstage[:, 0:640], AF.Exp, ["bstage", ("hb", 0), ("hb", 1)], [("expB", hh)])

            for c in range(2):
                proj_fm(lambda tq, c=c: qT[:, c, tq * 512:(tq + 1) * 512], lambda tq, c=c: ("qT", c, tq), pqk, 0, c * 128, None)
                ln_drain()
                if not gb_loaded[0]:
                    gb_loaded[0] = True
                    load_gb(i, 0)
                build_expb(2 * c)
                build_expb(2 * c + 1)
                proj_fm(lambda tq, c=c: kT[:, c, tq * 512:(tq + 1) * 512], lambda tq, c=c: ("kT", c, tq), pqk, 1, c * 128, None)
            if not v_init[0]:
                v_init[0] = True
                S.guard = mix_guard
                S.add("dve", lambda q: q.memset(scr[:, 8192:8192 + 4160], 1.0), r=[], w=[("v", t) for t in range(NT)])
                S.guard = None
            for t in range(NT):
                b = proj_tm(t, pv, 0, 0, 256)
                cp("act" if t % 2 else "dve", v[:, t, :, 0:64], bank(b, 0, 256).rearrange("p (h d) -> p h d", d=64),
                   pskeys(b), [("v", t)])
            W.done((i, "qk", g))
            W.done((i, "v", g))

            pipe = Pipe()
            for qb in range(NT):
                for hh in range(4):
                    def mk(qb=qb, hh=hh):
                        c, po = hh // 2, (hh % 2) * 64
                        nkb = min(qb, 4) + 1
                        jlo = 5 - nkb
                        loc = {}
                        ob = 4 + (qb % 2)
                        tagb = "L%d.g%d.q%d.h%d." % (i, g, qb, hh)

                        def sA():
                            S.tag = tagb + "A"
                            ss = s_ring.next()
                            loc["ss"] = ss
                            base = ss * 1024
                            for jj in range(jlo, 5):
                                kb = qb - 4 + jj
                                mm(ps[:, base + jj * 128:base + (jj + 1) * 128], kT[po:po + 64, c, kb * 128:(kb + 1) * 128],
                                   qT[po:po + 64, c, qb * 128:(qb + 1) * 128], True, True,
                                   [("kT", c, kb // 4), ("qT", c, qb // 4)], pskeys(2 * ss + (1 if jj == 4 else 0)))

                        def sB():
                            S.tag = tagb + "B"
                            ss = loc["ss"]
                            base = ss * 1024
                            e = e_ring.next()
                            loc["e"] = e
                            act(ebuf[:, e, jlo * 128:640], ps[:, base + jlo * 128:base + 640], AF.Exp, pskeys(2 * ss, 2), [("E", e)],
                                scale=0.125)

                        def sB2():
                            S.tag = tagb + "B2"
                            e = loc["e"]
                            tt("dve", ebuf[:, e, jlo * 128:640], ebuf[:, e, jlo * 128:640], expB[:, hh, jlo * 128:640], ALU.mult,
                               [("E", e), ("expB", hh)], [("E", e)])

                        def sC():
                            S.tag = tagb + "C"
                            e = loc["e"]
                            for jj in range(jlo, 5):
                                kb = qb - 4 + jj
                                mm(bank(ob, hh * 65, 65), ebuf[:, e, jj * 128:(jj + 1) * 128], v[:, kb, hh, :], jj == jlo, jj == 4,
                                   [("E", e), ("v", kb)], pskeys(ob))

                        def sD1():
                            S.tag = tagb + "D1"
                            o3 = bank(ob, 0, 260).rearrange("p (h d) -> p h d", d=65)
                            si = st2_ring.next()
                            rec = stat2[:, si, 0:4]
                            S.add("dve", lambda q: q.reciprocal(out=rec, in_=o3[:, :, 64]), r=pskeys(ob), w=[("rec", si)])
                            r = on_ring.next()
                            loc["r"] = r
                            tt("dve", onb[:, r, :].rearrange("p (h d) -> p h d", d=64), o3[:, :, 0:64],
                               rec.unsqueeze(2).to_broadcast([128, 4, 64]), ALU.mult, pskeys(ob) + [("rec", si)], [("on", r)])

                        def sD2():
                            S.tag = tagb + "D2"
                            r = loc["r"]
                            tb = 6 + tb_ring.next()
                            loc["tb"] = tb
                            for cc in range(2):
                                tp(psb[:, tb * 1024 + cc * 128: tb * 1024 + (cc + 1) * 128], onb[:, r, cc * 128:(cc + 1) * 128],
                                   ident[:], [("on", r), "ident"], pskeys(tb))

                        def sD3():
                            S.tag = tagb + "D3"
                            tb = loc["tb"]
                            srcp = psb[:, tb * 1024: tb * 1024 + 256].rearrange("p (c t) -> p c t", t=128)
                            cp("dve", ao[:, 2 * g:2 * g + 2, qb * 128:(qb + 1) * 128], srcp, pskeys(tb),
                               [("ao", 2 * g, qb), ("ao", 2 * g + 1, qb)])

                        stages = [sA, sB, sB2, sC]
                        if hh == 3:
                            stages += [sD1, sD2, sD3]
                        return stages
                    pipe.push(mk())
            pipe.drain()
        return ao

    def diff_attention(i, j):
        lam_init = 0.8 - 0.6 * math.exp(-0.3 * i)
        qT = scr[:, 0:4096].rearrange("p (c t) -> p c t", t=T)
        kT = scr[:, 4096:8192].rearrange("p (c t) -> p c t", t=T)
        v = scr[:, 8192:8192 + 4128].rearrange("p (t h d) -> p t h d", h=2, d=129)
        ao = scr[:, 12352:12352 + 16384].rearrange("p (c t) -> p c t", t=T)
        expBp = scr[:, 28736:28736 + 512].rearrange("p (h q) -> p h q", q=256)
        f32z = scr[:, 30272:30272 + 1600].bitcast(F32)
        gC = f32z[:, 0:128]
        lamv = f32z[:, 128:384].rearrange("p (a d) -> p a d", d=64)
        t1 = f32z[:, 384:512]
        dtm3 = scr[:, 29248:29248 + 768].bitcast(F32).rearrange("p (a d) -> p a d", d=128)
        dt_ring = Ring(3)
        ltmp = f32z[:, 640:768]
        sc = f32z[:, 768:784]
        cc = f32z[:, 784:792]
        ncc = f32z[:, 792:800]
        gb_loaded = [False]
        dma("sp", gC, cgn_d, [], ["gC"], "cst2_%d" % i)
        dma("sp", lamv, clam_d, [], ["lamv"], "cst2_%d" % i)
        dma("sp", cc, ccc_d, [], ["cc"], "cst2_%d" % i)
        ts("dve", ncc, cc, -1.0, ALU.mult, ["cc"], ["ncc"])
        mix_guard = {S.last("pe")} if S.last("pe") is not None else None
        v_init = [False]
        for m in range(2):
            tt("dve", ltmp[:, m * 64:(m + 1) * 64], lamv[:, 2 * m, :], lamv[:, 2 * m + 1, :], ALU.mult, ["lamv"], [("ltmp", m)])
            S.add("dve", lambda q, m=m: q.reduce_sum(out=sc[:, m:m + 1], in_=ltmp[:, m * 64:(m + 1) * 64],
                                                     axis=mybir.AxisListType.X), r=[("ltmp", m)], w=[("sc", m)])
            act(sc[:, 2 + m:3 + m], sc[:, m:m + 1], AF.Exp, [("sc", m)], [("sc", 2 + m)])
        tt("dve", sc[:, 4:5], sc[:, 3:4], sc[:, 2:3], ALU.subtract, [("sc", 2), ("sc", 3)], [("sc", 4)])
        ts("dve", sc[:, 5:6], sc[:, 4:5], -float(lam_init), ALU.add, [("sc", 4)], ["nlam"])
        nlam = sc[:, 5:6]
        e_ring = Ring(3)
        s_ring = Ring(4)
        on_ring = Ring(2)
        ccn = float(1.0 - lam_init) ** 2
        for g in range(4):
            S.tag = "L%d.g%d.proj" % (i, g)
            pqk = W.get((i, "qk", g))
            pv = W.get((i, "v", g))

            def build_expb(hl):
                hd = 2 * g + hl
                dma("pool", bstage[:, 0:256], cbias_d[hd][:, 0:256], [], ["bstage", ("hb", 0), ("hb", 1)], "bias")
                act(expBp[:, hl, :], bstage[:, 0:256], AF.Exp, ["bstage", ("hb", 0), ("hb", 1), "ncc"], [("expB", hl)],
                    bias=ncc[:, hd:hd + 1])

            for c in range(2):
                proj_fm(lambda tq, c=c: qT[:, c, tq * 512:(tq + 1) * 512], lambda tq, c=c: ("qT", c, tq), pqk, 0, c * 128, 0.125)
                ln_drain()
                if not gb_loaded[0]:
                    gb_loaded[0] = True
                    load_gb(i, 0)
                build_expb(c)
                proj_fm(lambda tq, c=c: kT[:, c, tq * 512:(tq + 1) * 512], lambda tq, c=c: ("kT", c, tq), pqk, 1, c * 128, None)
            if not v_init[0]:
                v_init[0] = True
                S.guard = mix_guard
                S.add("dve", lambda q: q.memset(scr[:, 8192:8192 + 4128], 1.0), r=[], w=[("v", t) for t in range(NT)])
                S.guard = None
            for t in range(NT):
                b = proj_tm(t, pv, 0, 0, 256)
                cp("act" if t % 2 else "dve", v[:, t, :, 0:128], bank(b, 0, 256).rearrange("p (h d) -> p h d", d=128),
                   pskeys(b), [("v", t)])
            W.done((i, "qk", g))
            W.done((i, "v", g))

            pipe = Pipe()
            shared = {}
            for qb in range(NT):
                for hl in range(2):
                    for m in range(2):
                        nk4 = qb // 4 + 1
                        for k4 in range(nk4):
                            def mk(qb=qb, hl=hl, m=m, k4=k4, last=(k4 == nk4 - 1)):
                                hd = 2 * g + hl
                                po = m * 64
                                nb = min(4, qb + 1 - 4 * k4)
                                ob = 4 + hl
                                loc = {}
                                tagb = "L%d.g%d.q%d.h%d.m%d.k%d." % (i, g, qb, hl, m, k4)

                                def sA():
                                    S.tag = tagb + "A"
                                    sbk = s_ring.next()
                                    loc["s"] = sbk
                                    for jj in range(nb):
                                        kb = qb - (4 * k4 + jj)
                                        mm(bank(sbk, jj * 128, 128), kT[po:po + 64, hl, kb * 128:(kb + 1) * 128],
                                           qT[po:po + 64, hl, qb * 128:(qb + 1) * 128], True, True,
                                           [("kT", hl, kb // 4), ("qT", hl, qb // 4)], pskeys(sbk))
                                    if "warm" in DBG:
                                        mm(bank(7, 0, 512), ident[:], hT[:, 0, 0:512], True, True, ["ident"], pskeys(7))

                                def sB():
                                    S.tag = tagb + "B"
                                    sbk = loc["s"]
                                    e = e_ring.next()
                                    loc["e"] = e
                                    act(ebuf[:, e, 0:nb * 128], bank(sbk, 0, nb * 128), AF.Exp, pskeys(sbk) + ["cc"], [("E", e)],
                                        bias=cc[:, hd:hd + 1])

                                def sB2():
                                    S.tag = tagb + "B2"
                                    e = loc["e"]
                                    if k4 == 0:
                                        nmul = min(nb, 2)
                                        tt("dve", ebuf[:, e, 0:nmul * 128], ebuf[:, e, 0:nmul * 128], expBp[:, hl, 0:nmul * 128],
                                           ALU.mult, [("E", e), ("expB", hl)], [("E", e)])

                                def sC():
                                    S.tag = tagb + "C"
                                    e = loc["e"]
                                    for jj in range(nb):
                                        kb = qb - (4 * k4 + jj)
                                        mm(bank(ob, m * 129, 129), ebuf[:, e, jj * 128:(jj + 1) * 128], v[:, kb, hl, :],
                                           k4 == 0 and jj == 0, last and jj == nb - 1, [("E", e), ("v", kb)], pskeys(ob))

                                o3 = bank(ob, 0, 258).rearrange("p (h d) -> p h d", d=129)

                                def sD1a():
                                    S.tag = tagb + "D1a"
                                    si = st2_ring.next()
                                    loc["si"] = si
                                    di = dt_ring.next()
                                    loc["di"] = di
                                    dtm = dtm3[:, di, :]
                                    rec = stat2[:, si, 0:2]
                                    S.add("dve", lambda q: q.reciprocal(out=rec, in_=o3[:, :, 128]), r=pskeys(ob), w=[("rec", si)])
                                    tt("dve", stat2[:, si, 2:3], stat2[:, si, 1:2], nlam, ALU.mult, [("rec", si), "nlam"], [("rec2", si)])
                                    ts("dve", t1, o3[:, 0, 0:128], stat2[:, si, 0:1], ALU.mult, pskeys(ob) + [("rec", si)], ["t1"])
                                    stt("dve", dtm, o3[:, 1, 0:128], stat2[:, si, 2:3], t1, ALU.mult, ALU.add,
                                        pskeys(ob) + [("rec2", si), "t1"], [("dtm", di)])

                                def sD1b():
                                    S.tag = tagb + "D1b"
                                    si = loc["si"]
                                    di = loc["di"]
                                    act(ltmp, dtm3[:, di, :], AF.Square, [("dtm", di)], ["sqj", ("ss", si)], accum=stat2[:, si, 3:4])

                                def sD1c():
                                    S.tag = tagb + "D1c"
                                    si = loc["si"]
                                    ts("dve", stat2[:, si, 4:5], stat2[:, si, 3:4], 1.0 / (128.0 * ccn), ALU.mult, [("ss", si)],
                                       [("ms", si)], s2=RMS_EPS / ccn, op1=ALU.add)
                                    tt("pool", stat2[:, si, 5:6], stat2[:, si, 4:5], cst[:, 0:1], ALU.pow, [("ms", si), "cst"],
                                       [("rs", si)])

                                def sD1d():
                                    S.tag = tagb + "D1d"
                                    si = loc["si"]
                                    if hl == 0:
                                        shared[qb] = on_ring.next()
                                    r = shared[qb]
                                    di = loc["di"]
                                    stt("dve", onb[:, r, hl * 128:(hl + 1) * 128], dtm3[:, di, :], stat2[:, si, 5:6], gC, ALU.mult, ALU.mult,
                                        [("dtm", di), ("rs", si), "gC"], [("on", r, hl)])

                                def sD2():
                                    S.tag = tagb + "D2"
                                    r = shared[qb]
                                    tb = 6 if "warm" in DBG else 6 + tb_ring.next()
                                    loc["tb"] = tb
                                    for cc_ in range(2):
                                        tp(psb[:, tb * 1024 + cc_ * 128: tb * 1024 + (cc_ + 1) * 128],
                                           onb[:, r, cc_ * 128:(cc_ + 1) * 128], ident[:],
                                           [("on", r, 0), ("on", r, 1), "ident"], pskeys(tb))

                                def sD3():
                                    S.tag = tagb + "D3"
                                    tb = loc["tb"]
                                    srcp = psb[:, tb * 1024: tb * 1024 + 256].rearrange("p (c t) -> p c t", t=128)
                                    cp("act", ao[:, 2 * g:2 * g + 2, qb * 128:(qb + 1) * 128], srcp, pskeys(tb),
                                       [("ao", 2 * g, qb), ("ao", 2 * g + 1, qb)])

                                stages = [sA, sB, sB2, sC]
                                if last and m == 1:
                                    stages += [sD1a, sD1b, sD1c, sD1d]
                                    if hl == 1:
                                        stages += [sD2, sD3]
                                return stages
                            pipe.push(mk())
            pipe.drain()
        return ao

    def gla(i):
        qT2 = scr[:, 0:4096].rearrange("p (a t) -> p a t", t=T)
        g1T = scr[:, 4096:6144]
        w2 = scr[:, 6144:6656]
        fz = scr[:, 6656:6656 + 5456].bitcast(F32)
        gB = fz[:, 0:256]
        U = fz[:, 256:384]
        ones2 = fz[:, 384:386]

        def hbuf(k, off, n):
            base = 392 + k * 1168 + off
            return fz[:, base:base + n]
        bz = scr[:, 28736:28736 + 2048]
        ao = scr[:, 12352:12352 + 16384].rearrange("p (c t) -> p c t", t=T)
        S.guard = {S.last("pe")} if S.last("pe") is not None else None
        dma("sp", gB, bgn_d, [], ["gB"], "cst2_%d" % i)
        dma("sp", U, cU_d, [], ["U"], "cst2_%d" % i)
        dma("pool", w2[0:17, :], bg2_d, [], ["w2"], "bias")
        S.add("dve", lambda q: q.memset(ones2, 0.0), r=[], w=["ones2"])
        S.add("dve", lambda q: q.memset(ones2[0:64, 0:1], 1.0), r=["ones2"], w=["ones2"])
        S.add("dve", lambda q: q.memset(ones2[64:128, 1:2], 1.0), r=["ones2"], w=["ones2"])
        S.add("dve", lambda q: q.memset(g1T[0:32, :], 1.0), r=[], w=["g1T"])
        S.add("dve", lambda q: q.memset(bz[:, 0:512], 0.0), r=[], w=[("kdec", 0), ("kdec", 1)])
        S.guard = None
        pg1 = W.get((i, "g1"))
        for tq in range(4):
            need_hT(range(4 * tq, 4 * tq + 4))
            b = pj_ring.next()
            hk = [("hT", 4 * tq + u) for u in range(4)]
            for kc in range(KC):
                mm(ps[0:16, b * 512:(b + 1) * 512], pg1.ap(0, kc, 0, 16), hT[:, kc, tq * 512:(tq + 1) * 512],
                   kc == 0, kc == KC - 1, hk + [pg1.key(0)], pskeys(b))
            cp("dve", g1T[0:16, tq * 512:(tq + 1) * 512], ps[0:16, b * 512:(b + 1) * 512], pskeys(b) + ["g1T"], ["g1T"])
        W.done((i, "g1"))
        ln_drain()
        load_gb(i, 0)

        def head_gen(hd, k):
            P1, P2, P3 = 3 * k, 3 * k + 1, 3 * k + 2
            spb = hbuf(k, 0, 128)
            wdec = hbuf(k, 128, 128)
            decay = hbuf(k, 256, 8)
            state = hbuf(k, 264, 256)
            er = hbuf(k, 520, 256)
            onf = hbuf(k, 776, 256)
            e1 = hbuf(k, 1032, 128)
            ssq = hbuf(k, 1160, 8)
            kdec2 = bz[:, k * 256:(k + 1) * 256].rearrange("p (a d) -> p a d", d=128)
            vbf = bz[:, 512 + k * 256:512 + (k + 1) * 256]
            stbf = bz[:, 1024 + k * 512:1024 + (k + 1) * 512].rearrange("p (c d) -> p c d", d=256)
            K_ = lambda name: (name, k)
            pA = W.get((i, "qkv", hd))
            pR = W.get((i, "r", hd))
            proj_fm(lambda tq: qT2[:, k, tq * 512:(tq + 1) * 512], lambda tq: ("qT", k, tq), pA, 0, 0, 128.0 ** -0.5)
            S.add("dve", lambda q: q.memset(state, 0.0), r=[], w=[K_("state")])
            yield
            for t in range(NT):
                S.tag = "L%d.gla.h%d.t%d" % (i, hd, t)
                tok = slice(t * 128, (t + 1) * 128)
                need_hT([t])
                for kc in range(KC):
                    mm(bank(P1, 0, 128), hT[:, kc, tok], pA.ap(1, kc, 0, 128), kc == 0, kc == KC - 1,
                       [("hT", t), pA.key(1)], pskeys(P1))
                mm(bank(P1, 128, 128), g1T[0:17, tok], w2[0:17, hd * 128:(hd + 1) * 128], True, True, ["g1T", "w2"], pskeys(P1))
                for kc in range(KC):
                    mm(bank(P2, 0, 256), hT[:, kc, tok], pA.ap(2, kc, 0, 256), kc == 0, kc == KC - 1,
                       [("hT", t), pA.key(2)], pskeys(P2))
                for kc in range(KC):
                    mm(bank(P3, 0, 256), hT[:, kc, tok], pR.ap(0, kc, 0, 256), kc == 0, kc == KC - 1,
                       [("hT", t), pR.key(0)], pskeys(P3))
                yield
                act(e1, bank(P1, 128, 128), AF.Exp, pskeys(P1), [K_("e1")], scale=-1.0)
                cp("act", vbf, bank(P2, 0, 256), pskeys(P2), [K_("vbf")])
                act(er, bank(P3, 0, 256), AF.Exp, pskeys(P3), [K_("er")], scale=-1.0)
                yield
                act(spb, e1, AF.Ln, [K_("e1")], [K_("spb")], bias=1.0)
                act(er, er, AF.Ln, [K_("er")], [K_("er")], bias=1.0)
                yield
                mm(bank(P1, 256, 128), U, spb, True, True, ["U", K_("spb")], pskeys(P1))
                mm(bank(P1, 384, 2), spb, ones2, True, True, ["ones2", K_("spb")], pskeys(P1))
                act(er, er, AF.Exp, [K_("er")], [K_("er")], scale=-1.0)
                yield
                act(wdec, bank(P1, 256, 128), AF.Exp, pskeys(P1), [K_("wdec")], scale=-1.0 / 16.0)
                act(decay[:, 0:2], bank(P1, 384, 2), AF.Exp, pskeys(P1), [K_("decay")], scale=-1.0 / 16.0)
                yield
                for c in range(2):
                    lo, hi = c * 64, (c + 1) * 64
                    tt("dve", kdec2[lo:hi, c, :], ps[lo:hi, P1 * 512:P1 * 512 + 128], wdec[lo:hi, :], ALU.mult,
                       pskeys(P1) + [K_("wdec")], [("kdec", k)])
                yield
                mm(bank(P1, 0, 256), kdec2[:, 0, :], vbf, True, True, [("kdec", k), K_("vbf")], pskeys(P1))
                mm(bank(P2, 256, 256), kdec2[:, 1, :], vbf, True, True, [("kdec", k), K_("vbf")], pskeys(P2))
                yield
                kvs = [bank(P1, 0, 256), bank(P2, 256, 256)]
                kvk = [pskeys(P1), pskeys(P2)]
                for c in range(2):
                    stt("dve", state, state, decay[:, c:c + 1], kvs[c], ALU.mult, ALU.add,
                        [K_("state"), K_("decay")] + kvk[c], [K_("state")])
                    cp("dve", stbf[:, c, :], state, [K_("state")], [("stbf", k, c)])
                yield
                mm(bank(P1, 0, 256), qT2[:, k, tok], stbf[:, 0, :], True, True, [("qT", k, t // 4), ("stbf", k, 0)], pskeys(P1))
                mm(bank(P2, 0, 256), qT2[:, k, tok], stbf[:, 1, :], True, True, [("qT", k, t // 4), ("stbf", k, 1)], pskeys(P2))
                yield
                obk = [P1, P2]
                for c in range(2):
                    lo, hi = c * 64, (c + 1) * 64
                    act(onf[lo:hi, :], ps[lo:hi, obk[c] * 512:obk[c] * 512 + 256], AF.Square, pskeys(obk[c]),
                        [("onf", k, c), ("ssq", k, c)], accum=ssq[lo:hi, 0:1])
                yield
                ts("dve", ssq[:, 1:2], ssq[:, 0:1], 1.0 / 256.0, ALU.mult, [("ssq", k, 0), ("ssq", k, 1)], [K_("rs1")],
                   s2=RMS_EPS, op1=ALU.add)
                tt("pool", ssq[:, 2:3], ssq[:, 1:2], cst[:, 0:1], ALU.pow, [K_("rs1"), "cst"], [K_("rs2")])
                yield
                for c in range(2):
                    lo, hi = c * 64, (c + 1) * 64
                    stt("dve", onf[lo:hi, :], ps[lo:hi, obk[c] * 512:obk[c] * 512 + 256], ssq[lo:hi, 2:3],
                        gB[lo:hi, :], ALU.mult, ALU.mult, pskeys(obk[c]) + [K_("rs2"), "gB", ("onf", k, c)], [("onf", k, c)])
                tt("dve", er, er, bank(P3, 0, 256), ALU.mult, [K_("er")] + pskeys(P3), [K_("er")])
                tt("dve", onb[:, k, :], onf, er, ALU.mult, [("onf", k, 0), ("onf", k, 1), K_("er")], [("on", k)])
                yield
                for cc_ in range(2):
                    tp(psb[:, P3 * 1024 + 512 + cc_ * 128: P3 * 1024 + 512 + (cc_ + 1) * 128], onb[:, k, cc_ * 128:(cc_ + 1) * 128],
                       ident[:], [("on", k), "ident"], pskeys(P3))
                yield
                srcp = psb[:, P3 * 1024 + 512: P3 * 1024 + 768].rearrange("p (c t) -> p c t", t=128)
                cp("act", ao[:, 2 * hd:2 * hd + 2, t * 128:(t + 1) * 128], srcp, pskeys(P3),
                   [("ao", 2 * hd, t), ("ao", 2 * hd + 1, t)])
                yield
            W.done((i, "qkv", hd))
            W.done((i, "r", hd))

        for pair in ((0, 1), (2, 3)):
            gens = [head_gen(hd, k) for k, hd in enumerate(pair)]
            alive = list(gens)
            for gen in gens:
                next(gen)
            for _ in range(7):
                next(gens[0])
            while alive:
                for gi, gen in enumerate(list(alive)):
                    try:
                        next(gen)
                    except StopIteration:
                        alive.remove(gen)
        return ao

    for i in range(n_layers):
        kind = i % 3
        j = i // 3
        last_layer = (i == n_layers - 1)
        if kind == 0:
            ao = chunk_attention(i, j)
        elif kind == 1:
            ao = gla(i)
        else:
            ao = diff_attention(i, j)
        if ao is None:
            for t in range(NT):
                dma("sp", out_d[t * 128:(t + 1) * 128, :], h[:, t, :], [("h", t)], [("out", t)], "out")
            break
        out_proj_ln(i, ao, final=(last_layer and stop_mid))
        if not (last_layer and stop_mid):
            ffn(i, final=last_layer)

    ln_drain()
    S.fence("sp", [("out", t) for t in range(NT)])
    S.resolve()
    sems = {k: es.enter_context(nc.semaphore("s_%s_%s" % k)) for k in S.sem_keys()}
    with nc.Block() as block:
        S.emit(block, sems)
    es.close()
    return nc


def _t5_bucket_np(rel):
    nb = 16
    max_exact = 8
    ret = (rel > 0).astype(np.int32) * nb
    n = np.abs(rel)
    is_small = n < max_exact
    n_f = np.maximum(n, 1).astype(np.float32)
    large = max_exact + (np.log(n_f / np.float32(max_exact)) / np.float32(math.log(128 / max_exact))
                         * np.float32(nb - max_exact)).astype(np.int32)
    large = np.minimum(large, nb - 1)
    return ret + np.where(is_small, n, large)


def _prep_shared(inp):
    f = np.float32
    k_in = np.arange(128)[:, None]
    q_in = np.arange(128)[None, :]
    a_rel = np.asarray(inp["a_rel_bias"], f)
    a_bias = np.empty((2, 16, 128, 640), f)
    for jj in range(5):
        rel = q_in - k_in + (4 - jj) * 128
        idx = np.clip(rel, -128, 128) + 128
        vals = a_rel[:, idx, :]
        vals = np.transpose(vals, (0, 3, 1, 2))
        if jj == 4:
            mask = (k_in >= 64) & (q_in < 64)
        elif jj == 0:
            mask = (k_in < 64) & (q_in >= 64)
        else:
            mask = np.zeros((128, 128), bool)
        a_bias[:, :, :, jj * 128:(jj + 1) * 128] = np.where(mask[None, None], f(NEG), vals)
    t5 = np.asarray(inp["t5_table"], f)
    c_bias = np.empty((8, 128, 768), f)
    for bi, jd in enumerate([0, 1, 2, 2, 2, 2]):
        rel = (k_in - q_in - 128 * jd).astype(np.int32)
        vals = np.transpose(t5[_t5_bucket_np(rel)], (2, 0, 1))
        if jd == 0:
            vals = np.where(((k_in >= 64) & (q_in < 64))[None], f(NEG), vals)
        c_bias[:, :, bi * 128:(bi + 1) * 128] = vals
    assert (_t5_bucket_np(np.arange(-2048, -128)) == 15).all()
    lngb = np.stack([inp["ln1_g"], inp["ln1_b"], inp["ln2_g"], inp["ln2_b"]], axis=1).astype(f)
    lngb = np.ascontiguousarray(np.broadcast_to(lngb[:, :, None, :], (DEPTH, 4, 128, D)))
    lam = np.stack([inp["c_lam_q1"][0], inp["c_lam_k1"][0], inp["c_lam_q2"][0], inp["c_lam_k2"][0]], 0).astype(f)
    s = np.arange(128)
    U = ((s[:, None] // 64 == s[None, :] // 64) & (s[:, None] > s[None, :])).astype(f)
    return {
        "lngb": lngb,
        "ffn_gu": np.ascontiguousarray(inp["ffn_w_gate_up"], f),
        "ffn_d": np.ascontiguousarray(inp["ffn_w_down"], f),
        "a_qkv": np.ascontiguousarray(inp["a_w_qkv"], f),
        "a_o": np.ascontiguousarray(inp["a_w_o"], f),
        "a_bias": a_bias,
        "b_in": np.ascontiguousarray(inp["b_w_in"][0], f),
        "b_g1": np.ascontiguousarray(inp["b_w_g1"][0], f),
        "b_g2aug": np.ascontiguousarray(np.concatenate([inp["b_w_g2"][0], inp["b_b_g"][0][None, :]], 0), f),
        "b_gn": np.ascontiguousarray(np.broadcast_to(np.asarray(inp["b_g_norm"][0], f)[None, :], (128, 256))),
        "b_o": np.ascontiguousarray(inp["b_w_o"][0], f),
        "c_qkv": np.ascontiguousarray(inp["c_w_qkv"][0], f),
        "c_o": np.ascontiguousarray(inp["c_w_o"][0], f),
        "c_bias": c_bias,
        "c_cc": np.ascontiguousarray(np.broadcast_to(t5[15, :][None, :], (128, 8))),
        "c_gn": np.ascontiguousarray(np.broadcast_to(np.asarray(inp["c_g_norm"][0], f)[None, :], (128, 128))),
        "c_lam": np.ascontiguousarray(np.broadcast_to(lam[None], (128, 4, 64))),
        "k_ident": np.eye(128, dtype=f),
        "k_U": U,
    }


_NC_CACHE = {}


def kernel(_n_layers=DEPTH, _stop_mid=False, _cores=8, **inputs):
    inp = {k: np.asarray(v) for k, v in inputs.items()}
    shared = _prep_shared(inp)
    key = (_n_layers, _stop_mid)
    if key not in _NC_CACHE:
        _NC_CACHE[key] = build_program(_n_layers, _stop_mid)
    nc = _NC_CACHE[key]
    x = np.asarray(inp["x"], np.float32)
    in_maps = []
    for b in range(_cores):
        m = dict(shared)
        m["x"] = np.ascontiguousarray(x[b])
        in_maps.append(m)
    res = run_bass_kernel_spmd(nc, in_maps, core_ids=list(range(_cores)))
    return np.stack([res.results[b]["out"] for b in range(_cores)], axis=0)
```

```python
import math
import bisect
import contextlib
import numpy as np
import concourse.bass as bass
import concourse.mybir as mybir
from concourse.bass_utils import run_bass_kernel_spmd

F32 = mybir.dt.float32
BF16 = mybir.dt.bfloat16
AF = mybir.ActivationFunctionType
ALU = mybir.AluOpType

D = 1024
T = 2048
NT = 16
KC = 8
DEPTH = 4
FF = 2816
DN_ALPHA = (2.0 * DEPTH) ** 0.25
LN_EPS = 1e-5
RMS_EPS = 1e-6
NEG = -30000.0
DBG = set()

ENGS = ("pe", "act", "dve", "pool", "sp")


class Op:
    __slots__ = ("eng", "fn", "deps", "idx", "gidx", "signal", "cnt", "dma", "waits", "tag")


class Sched:
    def __init__(self):
        self.q = {e: [] for e in ENGS}
        self.lastw = {}
        self.readers = {}
        self.g = 0
        self.dma_gidx = {}
        self.tag = ""
        self.guard = None

    def last(self, eng):
        for op in reversed(self.q[eng]):
            if op.fn is not None and op.dma is None:
                return op
        return None

    def collect(self, keys):
        deps = set()
        for k in keys:
            d = self.lastw.get(k)
            if d is not None:
                deps.add(d)
            rd = self.readers.get(k)
            if rd:
                for x in rd.values():
                    if isinstance(x, list):
                        deps.update(x)
                    else:
                        deps.add(x)
        return deps

    def add(self, eng, fn, r=(), w=(), dma=None, extra=None):
        op = Op()
        op.eng = eng
        op.fn = fn
        op.idx = len(self.q[eng])
        op.gidx = self.g
        self.g += 1
        op.dma = dma
        op.signal = False
        op.cnt = 0
        op.tag = self.tag
        deps = set()
        for k in r:
            d = self.lastw.get(k)
            if d is not None:
                deps.add(d)
        for k in w:
            d = self.lastw.get(k)
            if d is not None:
                deps.add(d)
            rd = self.readers.get(k)
            if rd:
                for x in rd.values():
                    if isinstance(x, list):
                        deps.update(x)
                    else:
                        deps.add(x)
        if extra:
            deps |= extra
        if self.guard:
            deps |= self.guard
        op.deps = deps
        for k in w:
            self.lastw[k] = op
            self.readers[k] = {}
        for k in r:
            rd = self.readers.setdefault(k, {})
            if dma is not None:
                rd.setdefault("dma", []).append(op)
            else:
                rd[eng] = op
        self.q[eng].append(op)
        if dma is not None:
            self.dma_gidx.setdefault(dma, []).append(op.gidx)
        return op

    def fence(self, eng, keys):
        return self.add(eng, None, r=keys)

    def barrier(self, engs=("pe", "act", "dve")):
        lasts = []
        for e in engs:
            for op in reversed(self.q[e]):
                if op.fn is not None and op.dma is None:
                    lasts.append(op)
                    break
        for e in engs:
            op = self.add(e, None)
            op.deps = set(x for x in lasts if x.eng != e)

    def resolve(self):
        for e in ENGS:
            for op in self.q[e]:
                keep = []
                for d in op.deps:
                    if d.dma is not None:
                        keep.append(d)
                        continue
                    if d.fn is None:
                        continue
                    if d.eng == op.eng:
                        if op.eng == "pe":
                            continue
                    d.signal = True
                    keep.append(d)
                op.deps = keep
        for e in ENGS:
            c = 0
            for op in self.q[e]:
                if op.signal and op.dma is None:
                    c += 1
                op.cnt = c
        for e in ENGS:
            waited = {}
            for op in self.q[e]:
                need = {}
                for d in op.deps:
                    if d.dma is not None:
                        lst = self.dma_gidx[d.dma]
                        n = bisect.bisect_left(lst, op.gidx)
                        key = ("dma", d.dma)
                        val = 16 * n
                    else:
                        key = ("eng", d.eng)
                        val = d.cnt
                    if val > need.get(key, 0):
                        need[key] = val
                op.waits = []
                for key, val in need.items():
                    if waited.get(key, 0) >= val:
                        continue
                    waited[key] = val
                    op.waits.append((key, val))

    def sem_keys(self):
        keys = [("eng", e) for e in ENGS if e != "sp"]
        keys += [("dma", k) for k in self.dma_gidx]
        return keys

    def emit(self, block, sems):
        engmap = {"pe": block.tensor, "act": block.scalar, "dve": block.vector,
                  "pool": block.gpsimd, "sp": block.sync}
        for e in ENGS:
            ops = self.q[e]
            if not ops:
                continue

            def body(eng, ops=ops, e=e):
                for op in ops:
                    for key, val in op.waits:
                        eng.wait_ge(sems[key], val)
                    if op.fn is None:
                        continue
                    ins = op.fn(eng)
                    if op.dma is not None:
                        ins.then_inc(sems[("dma", op.dma)], 16)
                    elif op.signal:
                        ins.then_inc(sems[("eng", e)], 1)

            engmap[e](body)


class Ring:
    def __init__(self, n):
        self.n = n
        self.i = -1

    def next(self):
        self.i = (self.i + 1) % self.n
        return self.i


class Piece:
    def __init__(self, pid, nkc, ncols, parts):
        self.pid = pid
        self.nkc = nkc
        self.ncols = ncols
        self.parts = parts
        self.slot = None
        self.view = None

    def key(self, part):
        return ("w", self.slot, part)

    def ap(self, part, kc, a, b):
        _, k0, pk, c0, pw = self.parts[part]
        return self.view[:, k0 + kc, c0 + a:c0 + b]


class WLoader:
    NSLOT = 4
    SLOT_ELEMS = 8 * 512

    def __init__(self, S, wsl):
        self.S = S
        self.wsl = wsl
        self.plan = []
        self.byid = {}
        self.free = list(range(self.NSLOT))
        self.nxt = 0

    def add(self, piece):
        assert piece.nkc * piece.ncols <= self.SLOT_ELEMS
        self.plan.append(piece)
        self.byid[piece.pid] = piece

    def pump(self):
        while self.free and self.nxt < len(self.plan):
            s = self.free.pop(0)
            p = self.plan[self.nxt]
            self.nxt += 1
            p.slot = s
            p.view = self.wsl[:, s, 0:p.nkc * p.ncols].rearrange("p (k c) -> p k c", c=p.ncols)
            extra = self.S.collect([("w", s, 0), ("w", s, 1), ("w", s, 2)])
            for pi, (src, k0, pk, c0, pw) in enumerate(p.parts):
                dst = p.view[:, k0:k0 + pk, c0:c0 + pw]
                self._dma(dst, src, [("w", s, pi)], "w%d" % s, extra)

    def _dma(self, dst, src, wkeys, sem, extra):
        self.S.add("pool", lambda q: q.dma_start(out=dst, in_=src), w=wkeys, dma=sem, extra=set(extra))

    def get(self, pid):
        p = self.byid[pid]
        assert p.slot is not None, ("weight piece not yet issued", pid)
        return p

    def done(self, pid):
        p = self.byid[pid]
        self.free.append(p.slot)
        self.pump()


def build_program(n_layers=DEPTH, stop_mid=False):
    nc = bass.Bass("TRN2", target_bir_lowering=False)

    def din(name, shape):
        return nc.dram_tensor(name, list(shape), F32, kind="ExternalInput").ap()

    x_d = din("x", [T, D])
    lngb_d = din("lngb", [DEPTH, 4, 128, D])
    gu_d = din("ffn_gu", [DEPTH, D, 2 * FF])
    dn_d = din("ffn_d", [DEPTH, FF, D])
    aqkv_d = din("a_qkv", [2, D, 3 * D])
    ao_d = din("a_o", [2, D, D])
    abias_d = din("a_bias", [2, 16, 128, 640])
    bin_d = din("b_in", [D, 3 * D])
    bg1_d = din("b_g1", [D, 16])
    bg2_d = din("b_g2aug", [17, 512])
    bgn_d = din("b_gn", [128, 256])
    bo_d = din("b_o", [D, D])
    cqkv_d = din("c_qkv", [D, 3 * D])
    co_d = din("c_o", [D, D])
    cbias_d = din("c_bias", [8, 128, 768])
    cgn_d = din("c_gn", [128, 128])
    ccc_d = din("c_cc", [128, 8])
    clam_d = din("c_lam", [128, 4, 64])
    cident_d = din("k_ident", [128, 128])
    cU_d = din("k_U", [128, 128])
    out_d = nc.dram_tensor("out", [T, D], F32, kind="ExternalOutput").ap()

    S = Sched()
    es = contextlib.ExitStack()

    def sb(name, shape, dt):
        return es.enter_context(nc.sbuf_tensor(name, list(shape), dt))

    h = sb("h", [128, NT, D], F32)
    hT = sb("hT", [128, KC, T], BF16)
    wsl = sb("wsl", [128, WLoader.NSLOT, WLoader.SLOT_ELEMS], BF16)
    gb = sb("gb", [128, 2, D], F32)
    SCRN = 31872
    scr = sb("scr", [128, SCRN], BF16)
    ident = sb("ident", [128, 128], BF16)
    hb = sb("hb", [128, 2, D], BF16)
    ebuf = sb("ebuf", [128, 3, 640], BF16)
    onb = sb("onb", [128, 2, 256], BF16)
    stat = sb("stat", [128, 4, 8], F32)
    bst = sb("bst", [128, 4, 2, 6], F32)
    stat2 = sb("stat2", [128, 4, 8], F32)
    cst = sb("cst", [128, 4], F32)
    ps = es.enter_context(nc.psum_tensor("ps", [128, 4096], F32))
    psb = ps.bitcast(BF16)

    W = WLoader(S, wsl)

    def mm(out, lhsT, rhs, start, stop, r, w):
        S.add("pe", lambda q: q.matmul(out, lhsT=lhsT, rhs=rhs, start=start, stop=stop), r=r, w=w)

    def tp(out, in_, idn, r, w):
        S.add("pe", lambda q: q.transpose(out=out, in_=in_, identity=idn), r=r, w=w)

    def act(out, in_, func, r, w, scale=1.0, bias=None, accum=None):
        def f(q):
            kw = {}
            if bias is not None:
                kw["bias"] = bias
            if accum is not None:
                kw["accum_out"] = accum
            return q.activation(out=out, in_=in_, func=func, scale=scale, **kw)
        S.add("act", f, r=r, w=w)

    def tt(eng, out, in0, in1, op, r, w):
        S.add(eng, lambda q: q.tensor_tensor(out=out, in0=in0, in1=in1, op=op), r=r, w=w)

    def ts(eng, out, in0, s1, op0, r, w, s2=None, op1=None):
        if s2 is None:
            S.add(eng, lambda q: q.tensor_scalar(out=out, in0=in0, scalar1=s1, scalar2=None, op0=op0), r=r, w=w)
        else:
            S.add(eng, lambda q: q.tensor_scalar(out=out, in0=in0, scalar1=s1, scalar2=s2, op0=op0, op1=op1), r=r, w=w)

    def stt(eng, out, in0, scalar, in1, op0, op1, r, w):
        S.add(eng, lambda q: q.scalar_tensor_tensor(out=out, in0=in0, scalar=scalar, in1=in1, op0=op0, op1=op1), r=r, w=w)

    def cp(eng, out, in_, r, w):
        if eng == "act":
            S.add("act", lambda q: q.copy(out=out, in_=in_), r=r, w=w)
        else:
            S.add(eng, lambda q: q.tensor_copy(out=out, in_=in_), r=r, w=w)

    def dma(eng, out, in_, r, w, sem):
        S.add(eng, lambda q: q.dma_start(out=out, in_=in_), r=r, w=w, dma=sem)

    def bank(b, a=0, n=512):
        return ps[:, b * 512 + a:b * 512 + a + n]

    def pskeys(b0, nb=1):
        return [("ps", b0 + i) for i in range(nb)]

    def wsrc(mat, r0, nk, c0, w):
        return mat[r0:r0 + nk * 128, c0:c0 + w].rearrange("(k p) c -> p k c", p=128)

    def plan_ffn(i):
        for grp in range(2):
            for sp in range(6):
                nf = 2 if sp < 5 else 1
                f0 = grp * 11 + 2 * sp
                W.add(Piece((i, "gu", grp, sp), 8, 512, [
                    (wsrc(gu_d[i], 0, 8, f0 * 128, nf * 128), 0, 8, 0, nf * 128),
                    (wsrc(gu_d[i], 0, 8, FF + f0 * 128, nf * 128), 0, 8, 256, nf * 128)]))
            for half in range(2):
                W.add(Piece((i, "dA", grp, half), 8, 512, [
                    (wsrc(dn_d[i], grp * 1408, 8, half * 512, 512), 0, 8, 0, 512)]))
            W.add(Piece((i, "dB", grp), 6, 512, [
                (wsrc(dn_d[i], grp * 1408 + 1024, 3, 0, 512), 0, 3, 0, 512),
                (wsrc(dn_d[i], grp * 1408 + 1024, 3, 512, 512), 3, 3, 0, 512)]))

    def plan_o(i, mat):
        for half in range(2):
            W.add(Piece((i, "o", half), 8, 512, [(wsrc(mat, 0, 8, half * 512, 512), 0, 8, 0, 512)]))

    for i in range(n_layers):
        kind = i % 3
        j = i // 3
        if kind == 0:
            for g in range(4):
                W.add(Piece((i, "qk", g), 8, 512, [
                    (wsrc(aqkv_d[j], 0, 8, g * 256, 256), 0, 8, 0, 256),
                    (wsrc(aqkv_d[j], 0, 8, D + g * 256, 256), 0, 8, 256, 256)]))
                W.add(Piece((i, "v", g), 8, 256, [(wsrc(aqkv_d[j], 0, 8, 2 * D + g * 256, 256), 0, 8, 0, 256)]))
            plan_o(i, ao_d[j])
        elif kind == 1:
            W.add(Piece((i, "g1"), 8, 16, [(wsrc(bg1_d, 0, 8, 0, 16), 0, 8, 0, 16)]))
            for hd in range(4):
                W.add(Piece((i, "qkv", hd), 8, 512, [
                    (wsrc(bin_d, 0, 8, hd * 128, 128), 0, 8, 0, 128),
                    (wsrc(bin_d, 0, 8, 512 + hd * 128, 128), 0, 8, 128, 128),
                    (wsrc(bin_d, 0, 8, 1024 + hd * 256, 256), 0, 8, 256, 256)]))
                W.add(Piece((i, "r", hd), 8, 256, [(wsrc(bin_d, 0, 8, 2048 + hd * 256, 256), 0, 8, 0, 256)]))
            plan_o(i, bo_d)
        else:
            for g in range(4):
                W.add(Piece((i, "qk", g), 8, 512, [
                    (wsrc(cqkv_d, 0, 8, g * 256, 256), 0, 8, 0, 256),
                    (wsrc(cqkv_d, 0, 8, D + g * 256, 256), 0, 8, 256, 256)]))
                W.add(Piece((i, "v", g), 8, 256, [(wsrc(cqkv_d, 0, 8, 2 * D + g * 256, 256), 0, 8, 0, 256)]))
            plan_o(i, co_d)
        if not (stop_mid and i == n_layers - 1):
            plan_ffn(i)

    S.add("dve", lambda q: q.memset(cst[:, 0:1], -0.5), r=[], w=["cst"])
    dma("pool", ident[:], cident_d, [], ["ident"], "cst")
    W.pump()

    for t in range(NT):
        dma("sp", h[:, t, :], x_d[t * 128:(t + 1) * 128, :], [], [("h", t)], "xin%d" % (t // 2))

    hb_ring = Ring(2)
    tb_ring = Ring(2)
    evac_flip = [0]

    def to_featmajor(src_keys, t, nch, dst_fn, dst_keys, src_ap_fn, evac_eng=None):
        tb = 6 + tb_ring.next()
        for c in range(nch):
            tp(psb[:, tb * 1024 + c * 128: tb * 1024 + (c + 1) * 128], src_ap_fn(c), ident[:],
               list(src_keys) + ["ident"], pskeys(tb))
        src = psb[:, tb * 1024: tb * 1024 + nch * 128].rearrange("p (c t) -> p c t", t=128)
        evac_flip[0] ^= 1
        if evac_eng is None:
            evac_eng = "act" if evac_flip[0] else "dve"
        cp(evac_eng, dst_fn(), src, pskeys(tb), dst_keys)

    def make_hT(t):
        r = hb_ring.next()
        cp("act", hb[:, r, :], h[:, t, :], [("h", t)], [("hb", r)])
        to_featmajor([("hb", r)], t, 8, lambda: hT[:, :, t * 128:(t + 1) * 128], [("hT", t)],
                     lambda c: hb[:, r, c * 128:(c + 1) * 128])

    init_hT = []

    st_ring = Ring(4)
    st2_ring = Ring(4)

    ln_pipe = []

    ln_backlog = []

    def ln_step():
        if ln_backlog:
            ln_pipe.append(ln_backlog.pop(0))
        for ent in list(ln_pipe):
            if ent[1]:
                ent[1].pop(0)()
        ln_pipe[:] = [ent for ent in ln_pipe if ent[1]]

    def ln_drain():
        while ln_pipe or ln_backlog:
            ln_step()

    def need_hT(tiles):
        tiles = set(tiles)
        while any(ent[0] in tiles for ent in ln_pipe + ln_backlog):
            ln_step()

    def ln_tick():
        if ln_pipe or ln_backlog:
            ln_step()

    def _mk_init(t):
        loc = {}

        def sa():
            r = hb_ring.next()
            loc["r"] = r
            cp("act", hb[:, r, :], h[:, t, :], [("h", t)], [("hb", r)])

        def sb():
            r = loc["r"]
            to_featmajor([("hb", r)], t, 8, lambda: hT[:, :, t * 128:(t + 1) * 128], [("hT", t)],
                         lambda c: hb[:, r, c * 128:(c + 1) * 128])
        return [t, [sa, sb]]

    pend = None
    for t in range(NT + 1):
        cur = None
        if t < NT:
            r = hb_ring.next()
            eng = "act" if t % 2 == 0 else "dve"
            cp(eng, hb[:, r, :], h[:, t, :], [("h", t)], [("hb", r)])
            cur = (t, r, "dve" if eng == "act" else "act")
        if pend is not None:
            pt, pr, pe_ = pend
            to_featmajor([("hb", pr)], pt, 8, lambda pt=pt: hT[:, :, pt * 128:(pt + 1) * 128], [("hT", pt)],
                         lambda c, pr=pr: hb[:, pr, c * 128:(c + 1) * 128], evac_eng=pe_)
        pend = cur

    def layer_norm(t, y_ap, ykeys, alpha, final):
        hk = [("h", t)]
        hv = h[:, t, :]
        si = st_ring.next()
        mv = stat[:, si, 0:2]
        rstd = stat[:, si, 2:3]
        nmr = stat[:, si, 3:4]
        ve = stat[:, si, 4:5]

        def s1():
            S.tag = "ln.t%d.s1" % t
            stt("dve", hv, hv, float(alpha), y_ap, ALU.mult, ALU.add, hk + ykeys, hk)
            for k in range(2):
                S.add("dve", lambda q, k=k: q.bn_stats(out=bst[:, si, k, :], in_=h[:, t, k * 512:(k + 1) * 512]),
                      r=hk, w=[("bst", si, k)])
            S.add("dve", lambda q: q.bn_aggr(out=mv, in_=bst[:, si, :, :]), r=[("bst", si, 0), ("bst", si, 1)],
                  w=[("mv", si)])
            ts("dve", ve, stat[:, si, 1:2], LN_EPS, ALU.add, [("mv", si)], [("ve", si)])

        def s2():
            S.tag = "ln.t%d.s2" % t
            tt("pool", rstd, ve, cst[:, 0:1], ALU.pow, [("ve", si), "cst"], [("rstd", si)])
            stt("dve", nmr, stat[:, si, 0:1], -1.0, rstd, ALU.mult, ALU.mult, [("mv", si), ("rstd", si)], [("nmr", si)])
            act(hv, hv, AF.Identity, hk + [("rstd", si), ("nmr", si)], hk, scale=rstd, bias=nmr)

        def s3():
            S.tag = "ln.t%d.s3" % t
            tt("pool", hv, hv, gb[:, 0, :], ALU.mult, hk + [("gb", 0)], hk)

        def s4():
            S.tag = "ln.t%d.s4" % t
            tt("dve", hv, hv, gb[:, 1, :], ALU.add, hk + [("gb", 1)], hk)
            if final:
                dma("sp", out_d[t * 128:(t + 1) * 128, :], hv, hk, [("out", t)], "out")
            else:
                r = hb_ring.next()
                hbr[0] = r
                cp("act", hb[:, r, :], h[:, t, :], [("h", t)], [("hb", r)])

        def s5():
            S.tag = "ln.t%d.s5" % t
            if not final:
                r = hbr[0]
                to_featmajor([("hb", r)], t, 8, lambda: hT[:, :, t * 128:(t + 1) * 128], [("hT", t)],
                             lambda c: hb[:, r, c * 128:(c + 1) * 128])

        hbr = [0]
        ln_pipe.append([t, [s1, s2, s3, s4, s5]])
        ln_step()

    def load_gb(i, which):
        dma("sp", gb[:, 0, :], lngb_d[i, 2 * which, :, :], [], [("gb", 0)], "gb%d_%d" % (i, which))
        dma("sp", gb[:, 1, :], lngb_d[i, 2 * which + 1, :, :], [], [("gb", 1)], "gb%d_%d" % (i, which))

    yset_ring = Ring(3)

    def out_proj_ln(i, ao, final):
        pO = [W.get((i, "o", 0)), W.get((i, "o", 1))]
        for t in range(NT):
            S.tag = "L%d.oproj.t%d" % (i, t)
            ys = yset_ring.next()
            for half in range(2):
                b = 2 * ys + half
                for kc in range(KC):
                    mm(bank(b), ao[:, kc, t * 128:(t + 1) * 128], pO[half].ap(0, kc, 0, 512), kc == 0, kc == KC - 1,
                       [("ao", kc, t), pO[half].key(0)], pskeys(b))
            layer_norm(t, ps[:, ys * 1024:(ys + 1) * 1024], pskeys(2 * ys, 2), DN_ALPHA, final)
        if final:
            ln_drain()
        W.done((i, "o", 0))
        W.done((i, "o", 1))

    def ffn(i, final):
        actT = scr[:, 0:11 * T].rearrange("p (f t) -> p f t", t=T)
        sgt = scr[:, 11 * T:11 * T + 2048].bitcast(F32).rearrange("p (a d) -> p a d", d=512)
        sg_ring = Ring(2)
        nsg = [0]
        ffn_guard = {S.last("pe")}
        for grp in range(2):
            for sp in range(6):
                nf = 2 if sp < 5 else 1
                pc = W.get((i, "gu", grp, sp))
                S.tag = "L%d.ffn.gu%d.%d" % (i, grp, sp)
                for tq in range(4):
                  for fi in range(nf):
                        fc = 2 * sp + fi
                        need_hT(range(4 * tq, 4 * tq + 4))
                        ys = yset_ring.next()
                        bg, bu = 2 * ys, 2 * ys + 1
                        hk = [("hT", 4 * tq + u) for u in range(4)]
                        for kc in range(KC):
                            mm(bank(bg), pc.ap(0, kc, fi * 128, fi * 128 + 128), hT[:, kc, tq * 512:(tq + 1) * 512],
                               kc == 0, kc == KC - 1, hk + [pc.key(0)], pskeys(bg))
                        for kc in range(KC):
                            mm(bank(bu), pc.ap(1, kc, fi * 128, fi * 128 + 128), hT[:, kc, tq * 512:(tq + 1) * 512],
                               kc == 0, kc == KC - 1, hk + [pc.key(1)], pskeys(bu))
                        si = sg_ring.next()
                        if nsg[0] < 2:
                            S.guard = ffn_guard
                        nsg[0] += 1
                        act(sgt[:, si, :], bank(bg), AF.Silu, pskeys(bg), [("sgt", si)])
                        S.guard = None
                        tt("dve", actT[:, fc, tq * 512:(tq + 1) * 512], sgt[:, si, :], bank(bu), ALU.mult,
                           [("sgt", si)] + pskeys(bu), [("actT", fc, tq)])
                        ln_tick()
                W.done((i, "gu", grp, sp))
            if "ffn_a" in DBG:
                for t in range(NT):
                    dma("sp", out_d[t * 128:(t + 1) * 128, :], h[:, t, :], [("h", t)], [("out", t)], "out")
                return
            pA = [W.get((i, "dA", grp, 0)), W.get((i, "dA", grp, 1))]
            pB = W.get((i, "dB", grp))
            if grp == 1 and "ffn_e" not in DBG:
                load_gb(i, 1)
            for t in range(NT):
                S.tag = "L%d.ffn.dn%d.t%d" % (i, grp, t)
                ys = yset_ring.next()
                for half in range(2):
                    b = 2 * ys + half
                    for kc in range(11):
                        if kc < 8:
                            rhs = pA[half].ap(0, kc, 0, 512)
                            wk = pA[half].key(0)
                        else:
                            rhs = pB.ap(half, kc - 8, 0, 512)
                            wk = pB.key(half)
                        mm(bank(b), actT[:, kc, t * 128:(t + 1) * 128], rhs, kc == 0, kc == 10,
                           [("actT", kc, t // 4), wk], pskeys(b))
                y_ap = ps[:, ys * 1024:(ys + 1) * 1024]
                if grp == 0 or "ffn_c" in DBG:
                    hk = [("h", t)]
                    stt("dve", h[:, t, :], h[:, t, :], float(DN_ALPHA if grp == 0 else 1.0), y_ap, ALU.mult, ALU.add,
                        hk + pskeys(2 * ys, 2), hk)
                else:
                    layer_norm(t, y_ap, pskeys(2 * ys, 2), 1.0, final and "ffn_d" not in DBG)
            if final or grp == 0:
                ln_drain()
            if "ffn_d" in DBG and grp == 1:
                for t in range(NT):
                    dma("sp", out_d[t * 128:(t + 1) * 128, :], h[:, t, :], [("h", t)], [("out", t)], "out")
            if "ffn_b" in DBG or ("ffn_c" in DBG and grp == 1):
                for t in range(NT):
                    dma("sp", out_d[t * 128:(t + 1) * 128, :], h[:, t, :], [("h", t)], [("out", t)], "out")
                return
            W.done((i, "dA", grp, 0))
            W.done((i, "dA", grp, 1))
            W.done((i, "dB", grp))

    pj_ring = Ring(4)

    def proj_fm(dst_fn, dkey_fn, pc, part, c0, scale):
        for tq in range(4):
            need_hT(range(4 * tq, 4 * tq + 4))
            b = pj_ring.next()
            hk = [("hT", 4 * tq + u) for u in range(4)]
            for kc in range(KC):
                mm(bank(b), pc.ap(part, kc, c0, c0 + 128), hT[:, kc, tq * 512:(tq + 1) * 512], kc == 0, kc == KC - 1,
                   hk + [pc.key(part)], pskeys(b))
            ln_tick()
            if scale is None:
                cp("dve", dst_fn(tq), bank(b), pskeys(b), [dkey_fn(tq)])
            else:
                S.add("act", lambda q, o=dst_fn(tq), b=b: q.mul(out=o, in_=bank(b), mul=float(scale)),
                      r=pskeys(b), w=[dkey_fn(tq)])

    def proj_tm(t, pc, part, c0, ncols):
        need_hT([t])
        b = pj_ring.next()
        for kc in range(KC):
            mm(bank(b, 0, ncols), hT[:, kc, t * 128:(t + 1) * 128], pc.ap(part, kc, c0, c0 + ncols), kc == 0,
               kc == KC - 1, [("hT", t), pc.key(part)], pskeys(b))
        return b

    class Pipe:
        def __init__(self):
            self.fl = []

        def push(self, stages):
            self.fl.append(list(stages))
            self.step()

        def step(self):
            for st_ in list(self.fl):
                if st_:
                    st_.pop(0)()
            self.fl = [s_ for s_ in self.fl if s_]

        def drain(self):
            while self.fl:
                self.step()

    bstage = hb[:].rearrange("p a d -> p (a d)").bitcast(F32)

    def chunk_attention(i, j):
        qT = scr[:, 0:4096].rearrange("p (c t) -> p c t", t=T)
        kT = scr[:, 4096:8192].rearrange("p (c t) -> p c t", t=T)
        v = scr[:, 8192:8192 + 4160].rearrange("p (t h d) -> p t h d", h=4, d=65)
        ao = scr[:, 12352:12352 + 16384].rearrange("p (c t) -> p c t", t=T)
        expB = scr[:, 28736:28736 + 2560].rearrange("p (h q) -> p h q", q=640)
        gb_loaded = [False]
        mix_guard = {S.last("pe")} if S.last("pe") is not None else None
        v_init = [False]
        e_ring = Ring(3)
        s_ring = Ring(2)
        on_ring = Ring(2)
        for g in range(4):
            S.tag = "L%d.g%d.proj" % (i, g)
            pqk = W.get((i, "qk", g))
            pv = W.get((i, "v", g))

            def build_expb(hh):
                dma("pool", bstage[:, 0:640], abias_d[j, 4 * g + hh], [], ["bstage", ("hb", 0), ("hb", 1)], "bias")
                act(expB[:, hh, :], bstage[:, 0:640], AF.Exp, ["bstage", ("hb", 0), ("hb", 1)], [("expB", hh)])

            for c in range(2):
                proj_fm(lambda tq, c=c: qT[:, c, tq * 512:(tq + 1) * 512], lambda tq, c=c: ("qT", c, tq), pqk, 0, c * 128, None)
                ln_drain()
                if not gb_loaded[0]:
                    gb_loaded[0] = True
                    load_gb(i, 0)
                build_expb(2 * c)
                build_expb(2 * c + 1)
                proj_fm(lambda tq, c=c: kT[:, c, tq * 512:(tq + 1) * 512], lambda tq, c=c: ("kT", c, tq), pqk, 1, c * 128, None)
            if not v_init[0]:
                v_init[0] = True
                S.guard = mix_guard
                S.add("dve", lambda q: q.memset(scr[:, 8192:8192 + 4160], 1.0), r=[], w=[("v", t) for t in range(NT)])
                S.guard = None
            for t in range(NT):
                b = proj_tm(t, pv, 0, 0, 256)
                cp("act" if t % 2 else "dve", v[:, t, :, 0:64], bank(b, 0, 256).rearrange("p (h d) -> p h d", d=64),
                   pskeys(b), [("v", t)])
            W.done((i, "qk", g))
            W.done((i, "v", g))

            pipe = Pipe()
            for qb in range(NT):
                for hh in range(4):
                    def mk(qb=qb, hh=hh):
                        c, po = hh // 2, (hh % 2) * 64
                        nkb = min(qb, 4) + 1
                        jlo = 5 - nkb
                        loc = {}
                        ob = 4 + (qb % 2)
                        tagb = "L%d.g%d.q%d.h%d." % (i, g, qb, hh)

                        def sA():
                            S.tag = tagb + "A"
                            ss = s_ring.next()
                            loc["ss"] = ss
                            base = ss * 1024
                            for jj in range(jlo, 5):
                                kb = qb - 4 + jj
                                mm(ps[:, base + jj * 128:base + (jj + 1) * 128], kT[po:po + 64, c, kb * 128:(kb + 1) * 128],
                                   qT[po:po + 64, c, qb * 128:(qb + 1) * 128], True, True,
                                   [("kT", c, kb // 4), ("qT", c, qb // 4)], pskeys(2 * ss + (1 if jj == 4 else 0)))

                        def sB():
                            S.tag = tagb + "B"
                            ss = loc["ss"]
                            base = ss * 1024
                            e = e_ring.next()
                            loc["e"] = e
                            act(ebuf[:, e, jlo * 128:640], ps[:, base + jlo * 128:base + 640], AF.Exp, pskeys(2 * ss, 2), [("E", e)],
                                scale=0.125)

                        def sB2():
                            S.tag = tagb + "B2"
                            e = loc["e"]
                            tt("dve", ebuf[:, e, jlo * 128:640], ebuf[:, e, jlo * 128:640], expB[:, hh, jlo * 128:640], ALU.mult,
                               [("E", e), ("expB", hh)], [("E", e)])

                        def sC():
                            S.tag = tagb + "C"
                            e = loc["e"]
                            for jj in range(jlo, 5):
                                kb = qb - 4 + jj
                                mm(bank(ob, hh * 65, 65), ebuf[:, e, jj * 128:(jj + 1) * 128], v[:, kb, hh, :], jj == jlo, jj == 4,
                                   [("E", e), ("v", kb)], pskeys(ob))

                        def sD1():
                            S.tag = tagb + "D1"
                            o3 = bank(ob, 0, 260).rearrange("p (h d) -> p h d", d=65)
                            si = st2_ring.next()
                            rec = stat2[:, si, 0:4]
                            S.add("dve", lambda q: q.reciprocal(out=rec, in_=o3[:, :, 64]), r=pskeys(ob), w=[("rec", si)])
                            r = on_ring.next()
                            loc["r"] = r
                            tt("dve", onb[:, r, :].rearrange("p (h d) -> p h d", d=64), o3[:, :, 0:64],
                               rec.unsqueeze(2).to_broadcast([128, 4, 64]), ALU.mult, pskeys(ob) + [("rec", si)], [("on", r)])

                        def sD2():
                            S.tag = tagb + "D2"
                            r = loc["r"]
                            tb = 6 + tb_ring.next()
                            loc["tb"] = tb
                            for cc in range(2):
                                tp(psb[:, tb * 1024 + cc * 128: tb * 1024 + (cc + 1) * 128], onb[:, r, cc * 128:(cc + 1) * 128],
                                   ident[:], [("on", r), "ident"], pskeys(tb))

                        def sD3():
                            S.tag = tagb + "D3"
                            tb = loc["tb"]
                            srcp = psb[:, tb * 1024: tb * 1024 + 256].rearrange("p (c t) -> p c t", t=128)
                            cp("dve", ao[:, 2 * g:2 * g + 2, qb * 128:(qb + 1) * 128], srcp, pskeys(tb),
                               [("ao", 2 * g, qb), ("ao", 2 * g + 1, qb)])

                        stages = [sA, sB, sB2, sC]
                        if hh == 3:
                            stages += [sD1, sD2, sD3]
                        return stages
                    pipe.push(mk())
            pipe.drain()
        return ao

    def diff_attention(i, j):
        lam_init = 0.8 - 0.6 * math.exp(-0.3 * i)
        qT = scr[:, 0:4096].rearrange("p (c t) -> p c t", t=T)
        kT = scr[:, 4096:8192].rearrange("p (c t) -> p c t", t=T)
        v = scr[:, 8192:8192 + 4128].rearrange("p (t h d) -> p t h d", h=2, d=129)
        ao = scr[:, 12352:12352 + 16384].rearrange("p (c t) -> p c t", t=T)
        expBp = scr[:, 28736:28736 + 512].rearrange("p (h q) -> p h q", q=256)
        f32z = scr[:, 30272:30272 + 1600].bitcast(F32)
        gC = f32z[:, 0:128]
        lamv = f32z[:, 128:384].rearrange("p (a d) -> p a d", d=64)
        t1 = f32z[:, 384:512]
        dtm3 = scr[:, 29248:29248 + 768].bitcast(F32).rearrange("p (a d) -> p a d", d=128)
        dt_ring = Ring(3)
        ltmp = f32z[:, 640:768]
        sc = f32z[:, 768:784]
        cc = f32z[:, 784:792]
        ncc = f32z[:, 792:800]
        gb_loaded = [False]
        dma("sp", gC, cgn_d, [], ["gC"], "cst2_%d" % i)
        dma("sp", lamv, clam_d, [], ["lamv"], "cst2_%d" % i)
        dma("sp", cc, ccc_d, [], ["cc"], "cst2_%d" % i)
        ts("dve", ncc, cc, -1.0, ALU.mult, ["cc"], ["ncc"])
        mix_guard = {S.last("pe")} if S.last("pe") is not None else None
        v_init = [False]
        for m in range(2):
            tt("dve", ltmp[:, m * 64:(m + 1) * 64], lamv[:, 2 * m, :], lamv[:, 2 * m + 1, :], ALU.mult, ["lamv"], [("ltmp", m)])
            S.add("dve", lambda q, m=m: q.reduce_sum(out=sc[:, m:m + 1], in_=ltmp[:, m * 64:(m + 1) * 64],
                                                     axis=mybir.AxisListType.X), r=[("ltmp", m)], w=[("sc", m)])
            act(sc[:, 2 + m:3 + m], sc[:, m:m + 1], AF.Exp, [("sc", m)], [("sc", 2 + m)])
        tt("dve", sc[:, 4:5], sc[:, 3:4], sc[:, 2:3], ALU.subtract, [("sc", 2), ("sc", 3)], [("sc", 4)])
        ts("dve", sc[:, 5:6], sc[:, 4:5], -float(lam_init), ALU.add, [("sc", 4)], ["nlam"])
        nlam = sc[:, 5:6]
        e_ring = Ring(3)
        s_ring = Ring(4)
        on_ring = Ring(2)
        ccn = float(1.0 - lam_init) ** 2
        for g in range(4):
            S.tag = "L%d.g%d.proj" % (i, g)
            pqk = W.get((i, "qk", g))
            pv = W.get((i, "v", g))

            def build_expb(hl):
                hd = 2 * g + hl
                dma("pool", bstage[:, 0:256], cbias_d[hd][:, 0:256], [], ["bstage", ("hb", 0), ("hb", 1)], "bias")
                act(expBp[:, hl, :], bstage[:, 0:256], AF.Exp, ["bstage", ("hb", 0), ("hb", 1), "ncc"], [("expB", hl)],
                    bias=ncc[:, hd:hd + 1])

            for c in range(2):
                proj_fm(lambda tq, c=c: qT[:, c, tq * 512:(tq + 1) * 512], lambda tq, c=c: ("qT", c, tq), pqk, 0, c * 128, 0.125)
                ln_drain()
                if not gb_loaded[0]:
                    gb_loaded[0] = True
                    load_gb(i, 0)
                build_expb(c)
                proj_fm(lambda tq, c=c: kT[:, c, tq * 512:(tq + 1) * 512], lambda tq, c=c: ("kT", c, tq), pqk, 1, c * 128, None)
            if not v_init[0]:
                v_init[0] = True
                S.guard = mix_guard
                S.add("dve", lambda q: q.memset(scr[:, 8192:8192 + 4128], 1.0), r=[], w=[("v", t) for t in range(NT)])
                S.guard = None
            for t in range(NT):
                b = proj_tm(t, pv, 0, 0, 256)
                cp("act" if t % 2 else "dve", v[:, t, :, 0:128], bank(b, 0, 256).rearrange("p (h d) -> p h d", d=128),
                   pskeys(b), [("v", t)])
            W.done((i, "qk", g))
            W.done((i, "v", g))

            pipe = Pipe()
            shared = {}
            for qb in range(NT):
                for hl in range(2):
                    for m in range(2):
                        nk4 = qb // 4 + 1
                        for k4 in range(nk4):
                            def mk(qb=qb, hl=hl, m=m, k4=k4, last=(k4 == nk4 - 1)):
                                hd = 2 * g + hl
                                po = m * 64
                                nb = min(4, qb + 1 - 4 * k4)
                                ob = 4 + hl
                                loc = {}
                                tagb = "L%d.g%d.q%d.h%d.m%d.k%d." % (i, g, qb, hl, m, k4)

                                def sA():
                                    S.tag = tagb + "A"
                                    sbk = s_ring.next()
                                    loc["s"] = sbk
                                    for jj in range(nb):
                                        kb = qb - (4 * k4 + jj)
                                        mm(bank(sbk, jj * 128, 128), kT[po:po + 64, hl, kb * 128:(kb + 1) * 128],
                                           qT[po:po + 64, hl, qb * 128:(qb + 1) * 128], True, True,
                                           [("kT", hl, kb // 4), ("qT", hl, qb // 4)], pskeys(sbk))
                                    if "warm" in DBG:
                                        mm(bank(7, 0, 512), ident[:], hT[:, 0, 0:512], True, True, ["ident"], pskeys(7))

                                def sB():
                                    S.tag = tagb + "B"
                                    sbk = loc["s"]
                                    e = e_ring.next()
                                    loc["e"] = e
                                    act(ebuf[:, e, 0:nb * 128], bank(sbk, 0, nb * 128), AF.Exp, pskeys(sbk) + ["cc"], [("E", e)],
                                        bias=cc[:, hd:hd + 1])

                                def sB2():
                                    S.tag = tagb + "B2"
                                    e = loc["e"]
                                    if k4 == 0:
                                        nmul = min(nb, 2)
                                        tt("dve", ebuf[:, e, 0:nmul * 128], ebuf[:, e, 0:nmul * 128], expBp[:, hl, 0:nmul * 128],
                                           ALU.mult, [("E", e), ("expB", hl)], [("E", e)])

                                def sC():
                                    S.tag = tagb + "C"
                                    e = loc["e"]
                                    for jj in range(nb):
                                        kb = qb - (4 * k4 + jj)
                                        mm(bank(ob, m * 129, 129), ebuf[:, e, jj * 128:(jj + 1) * 128], v[:, kb, hl, :],
                                           k4 == 0 and jj == 0, last and jj == nb - 1, [("E", e), ("v", kb)], pskeys(ob))

                                o3 = bank(ob, 0, 258).rearrange("p (h d) -> p h d", d=129)

                                def sD1a():
                                    S.tag = tagb + "D1a"
                                    si = st2_ring.next()
                                    loc["si"] = si
                                    di = dt_ring.next()
                                    loc["di"] = di
                                    dtm = dtm3[:, di, :]
                                    rec = stat2[:, si, 0:2]
                                    S.add("dve", lambda q: q.reciprocal(out=rec, in_=o3[:, :, 128]), r=pskeys(ob), w=[("rec", si)])
                                    tt("dve", stat2[:, si, 2:3], stat2[:, si, 1:2], nlam, ALU.mult, [("rec", si), "nlam"], [("rec2", si)])
                                    ts("dve", t1, o3[:, 0, 0:128], stat2[:, si, 0:1], ALU.mult, pskeys(ob) + [("rec", si)], ["t1"])
                                    stt("dve", dtm, o3[:, 1, 0:128], stat2[:, si, 2:3], t1, ALU.mult, ALU.add,
                                        pskeys(ob) + [("rec2", si), "t1"], [("dtm", di)])

                                def sD1b():
                                    S.tag = tagb + "D1b"
                                    si = loc["si"]
                                    di = loc["di"]
                                    act(ltmp, dtm3[:, di, :], AF.Square, [("dtm", di)], ["sqj", ("ss", si)], accum=stat2[:, si, 3:4])

                                def sD1c():
                                    S.tag = tagb + "D1c"
                                    si = loc["si"]
                                    ts("dve", stat2[:, si, 4:5], stat2[:, si, 3:4], 1.0 / (128.0 * ccn), ALU.mult, [("ss", si)],
                                       [("ms", si)], s2=RMS_EPS / ccn, op1=ALU.add)
                                    tt("pool", stat2[:, si, 5:6], stat2[:, si, 4:5], cst[:, 0:1], ALU.pow, [("ms", si), "cst"],
                                       [("rs", si)])

                                def sD1d():
                                    S.tag = tagb + "D1d"
                                    si = loc["si"]
                                    if hl == 0:
                                        shared[qb] = on_ring.next()
                                    r = shared[qb]
                                    di = loc["di"]
                                    stt("dve", onb[:, r, hl * 128:(hl + 1) * 128], dtm3[:, di, :], stat2[:, si, 5:6], gC, ALU.mult, ALU.mult,
                                        [("dtm", di), ("rs", si), "gC"], [("on", r, hl)])

                                def sD2():
                                    S.tag = tagb + "D2"
                                    r = shared[qb]
                                    tb = 6 if "warm" in DBG else 6 + tb_ring.next()
                                    loc["tb"] = tb
                                    for cc_ in range(2):
                                        tp(psb[:, tb * 1024 + cc_ * 128: tb * 1024 + (cc_ + 1) * 128],
                                           onb[:, r, cc_ * 128:(cc_ + 1) * 128], ident[:],
                                           [("on", r, 0), ("on", r, 1), "ident"], pskeys(tb))

                                def sD3():
                                    S.tag = tagb + "D3"
                                    tb = loc["tb"]
                                    srcp = psb[:, tb * 1024: tb * 1024 + 256].rearrange("p (c t) -> p c t", t=128)
                                    cp("act", ao[:, 2 * g:2 * g + 2, qb * 128:(qb + 1) * 128], srcp, pskeys(tb),
                                       [("ao", 2 * g, qb), ("ao", 2 * g + 1, qb)])

                                stages = [sA, sB, sB2, sC]
                                if last and m == 1:
                                    stages += [sD1a, sD1b, sD1c, sD1d]
                                    if hl == 1:
                                        stages += [sD2, sD3]
                                return stages
                            pipe.push(mk())
            pipe.drain()
        return ao

    def gla(i):
        qT2 = scr[:, 0:4096].rearrange("p (a t) -> p a t", t=T)
        g1T = scr[:, 4096:6144]
        w2 = scr[:, 6144:6656]
        fz = scr[:, 6656:6656 + 5456].bitcast(F32)
        gB = fz[:, 0:256]
        U = fz[:, 256:384]
        ones2 = fz[:, 384:386]

        def hbuf(k, off, n):
            base = 392 + k * 1168 + off
            return fz[:, base:base + n]
        bz = scr[:, 28736:28736 + 2048]
        ao = scr[:, 12352:12352 + 16384].rearrange("p (c t) -> p c t", t=T)
        S.guard = {S.last("pe")} if S.last("pe") is not None else None
        dma("sp", gB, bgn_d, [], ["gB"], "cst2_%d" % i)
        dma("sp", U, cU_d, [], ["U"], "cst2_%d" % i)
        dma("pool", w2[0:17, :], bg2_d, [], ["w2"], "bias")
        S.add("dve", lambda q: q.memset(ones2, 0.0), r=[], w=["ones2"])
        S.add("dve", lambda q: q.memset(ones2[0:64, 0:1], 1.0), r=["ones2"], w=["ones2"])
        S.add("dve", lambda q: q.memset(ones2[64:128, 1:2], 1.0), r=["ones2"], w=["ones2"])
        S.add("dve", lambda q: q.memset(g1T[0:32, :], 1.0), r=[], w=["g1T"])
        S.add("dve", lambda q: q.memset(bz[:, 0:512], 0.0), r=[], w=[("kdec", 0), ("kdec", 1)])
        S.guard = None
        pg1 = W.get((i, "g1"))
        for tq in range(4):
            need_hT(range(4 * tq, 4 * tq + 4))
            b = pj_ring.next()
            hk = [("hT", 4 * tq + u) for u in range(4)]
            for kc in range(KC):
                mm(ps[0:16, b * 512:(b + 1) * 512], pg1.ap(0, kc, 0, 16), hT[:, kc, tq * 512:(tq + 1) * 512],
                   kc == 0, kc == KC - 1, hk + [pg1.key(0)], pskeys(b))
            cp("dve", g1T[0:16, tq * 512:(tq + 1) * 512], ps[0:16, b * 512:(b + 1) * 512], pskeys(b) + ["g1T"], ["g1T"])
        W.done((i, "g1"))
        ln_drain()
        load_gb(i, 0)

        def head_gen(hd, k):
            P1, P2, P3 = 3 * k, 3 * k + 1, 3 * k + 2
            spb = hbuf(k, 0, 128)
            wdec = hbuf(k, 128, 128)
            decay = hbuf(k, 256, 8)
            state = hbuf(k, 264, 256)
            er = hbuf(k, 520, 256)
            onf = hbuf(k, 776, 256)
            e1 = hbuf(k, 1032, 128)
            ssq = hbuf(k, 1160, 8)
            kdec2 = bz[:, k * 256:(k + 1) * 256].rearrange("p (a d) -> p a d", d=128)
            vbf = bz[:, 512 + k * 256:512 + (k + 1) * 256]
            stbf = bz[:, 1024 + k * 512:1024 + (k + 1) * 512].rearrange("p (c d) -> p c d", d=256)
            K_ = lambda name: (name, k)
            pA = W.get((i, "qkv", hd))
            pR = W.get((i, "r", hd))
            proj_fm(lambda tq: qT2[:, k, tq * 512:(tq + 1) * 512], lambda tq: ("qT", k, tq), pA, 0, 0, 128.0 ** -0.5)
            S.add("dve", lambda q: q.memset(state, 0.0), r=[], w=[K_("state")])
            yield
            def proj_kgv(t):
                tok = slice(t * 128, (t + 1) * 128)
                need_hT([t])
                for kc in range(KC):
                    mm(bank(P1, 0, 128), hT[:, kc, tok], pA.ap(1, kc, 0, 128), kc == 0, kc == KC - 1,
                       [("hT", t), pA.key(1)], pskeys(P1))
                mm(bank(P1, 128, 128), g1T[0:17, tok], w2[0:17, hd * 128:(hd + 1) * 128], True, True, ["g1T", "w2"], pskeys(P1))
                for kc in range(KC):
                    mm(bank(P2, 0, 256), hT[:, kc, tok], pA.ap(2, kc, 0, 256), kc == 0, kc == KC - 1,
                       [("hT", t), pA.key(2)], pskeys(P2))

            P4 = 6 + k
            proj_kgv(0)
            for t in range(NT):
                S.tag = "L%d.gla.h%d.t%d" % (i, hd, t)
                tok = slice(t * 128, (t + 1) * 128)
                for kc in range(KC):
                    mm(bank(P3, 0, 256), hT[:, kc, tok], pR.ap(0, kc, 0, 256), kc == 0, kc == KC - 1,
                       [("hT", t), pR.key(0)], pskeys(P3))
                yield
                act(e1, bank(P1, 128, 128), AF.Exp, pskeys(P1), [K_("e1")], scale=-1.0)
                cp("act", vbf, bank(P2, 0, 256), pskeys(P2), [K_("vbf")])
                act(er, bank(P3, 0, 256), AF.Exp, pskeys(P3), [K_("er")], scale=-1.0)
                yield
                act(spb, e1, AF.Ln, [K_("e1")], [K_("spb")], bias=1.0)
                act(er, er, AF.Ln, [K_("er")], [K_("er")], bias=1.0)
                yield
                mm(bank(P1, 256, 128), U, spb, True, True, ["U", K_("spb")], pskeys(P1))
                mm(bank(P1, 384, 2), spb, ones2, True, True, ["ones2", K_("spb")], pskeys(P1))
                act(er, er, AF.Exp, [K_("er")], [K_("er")], scale=-1.0)
                yield
                act(hbuf(k, 128, 130), bank(P1, 256, 130), AF.Exp, pskeys(P1), [K_("wdec"), K_("decay")], scale=-1.0 / 16.0)
                yield
                for c in range(2):
                    lo, hi = c * 64, (c + 1) * 64
                    tt("dve", kdec2[lo:hi, c, :], ps[lo:hi, P1 * 512:P1 * 512 + 128], wdec[lo:hi, :], ALU.mult,
                       pskeys(P1) + [K_("wdec")], [("kdec", k)])
                yield
                mm(bank(P1, 0, 256), kdec2[:, 0, :], vbf, True, True, [("kdec", k), K_("vbf")], pskeys(P1))
                mm(bank(P2, 256, 256), kdec2[:, 1, :], vbf, True, True, [("kdec", k), K_("vbf")], pskeys(P2))
                yield
                kvs = [bank(P1, 0, 256), bank(P2, 256, 256)]
                kvk = [pskeys(P1), pskeys(P2)]
                for c in range(2):
                    stt("dve", state, state, decay[:, c:c + 1], kvs[c], ALU.mult, ALU.add,
                        [K_("state"), K_("decay")] + kvk[c], [K_("state")])
                    cp("dve", stbf[:, c, :], state, [K_("state")], [("stbf", k, c)])
                yield
                mm(bank(P4, 0, 256), qT2[:, k, tok], stbf[:, 0, :], True, True, [("qT", k, t // 4), ("stbf", k, 0)], pskeys(P4))
                mm(bank(P4, 256, 256), qT2[:, k, tok], stbf[:, 1, :], True, True, [("qT", k, t // 4), ("stbf", k, 1)], pskeys(P4))
                if t + 1 < NT:
                    proj_kgv(t + 1)
                yield
                ocol = [0, 256]
                for c in range(2):
                    lo, hi = c * 64, (c + 1) * 64
                    act(onf[lo:hi, :], ps[lo:hi, P4 * 512 + ocol[c]:P4 * 512 + ocol[c] + 256], AF.Square, pskeys(P4),
                        [("onf", k, c), ("ssq", k, c)], accum=ssq[lo:hi, 0:1])
                yield
                ts("dve", ssq[:, 1:2], ssq[:, 0:1], 1.0 / 256.0, ALU.mult, [("ssq", k, 0), ("ssq", k, 1)], [K_("rs1")],
                   s2=RMS_EPS, op1=ALU.add)
                tt("pool", ssq[:, 2:3], ssq[:, 1:2], cst[:, 0:1], ALU.pow, [K_("rs1"), "cst"], [K_("rs2")])
                yield
                for c in range(2):
                    lo, hi = c * 64, (c + 1) * 64
                    stt("dve", onf[lo:hi, :], ps[lo:hi, P4 * 512 + ocol[c]:P4 * 512 + ocol[c] + 256], ssq[lo:hi, 2:3],
                        gB[lo:hi, :], ALU.mult, ALU.mult, pskeys(P4) + [K_("rs2"), "gB", ("onf", k, c)], [("onf", k, c)])
                tt("dve", er, er, bank(P3, 0, 256), ALU.mult, [K_("er")] + pskeys(P3), [K_("er")])
                tt("dve", onb[:, k, :], onf, er, ALU.mult, [("onf", k, 0), ("onf", k, 1), K_("er")], [("on", k)])
                yield
                for cc_ in range(2):
                    tp(psb[:, P3 * 1024 + 512 + cc_ * 128: P3 * 1024 + 512 + (cc_ + 1) * 128], onb[:, k, cc_ * 128:(cc_ + 1) * 128],
                       ident[:], [("on", k), "ident"], pskeys(P3))
                yield
                srcp = psb[:, P3 * 1024 + 512: P3 * 1024 + 768].rearrange("p (c t) -> p c t", t=128)
                cp("act", ao[:, 2 * hd:2 * hd + 2, t * 128:(t + 1) * 128], srcp, pskeys(P3),
                   [("ao", 2 * hd, t), ("ao", 2 * hd + 1, t)])
                yield
            W.done((i, "qkv", hd))
            W.done((i, "r", hd))

        for pair in ((0, 1), (2, 3)):
            gens = [head_gen(hd, k) for k, hd in enumerate(pair)]
            alive = list(gens)
            for gen in gens:
                next(gen)
            for _ in range(7):
                next(gens[0])
            while alive:
                for gi, gen in enumerate(list(alive)):
                    try:
                        next(gen)
                    except StopIteration:
                        alive.remove(gen)
        return ao

    for i in range(n_layers):
        kind = i % 3
        j = i // 3
        last_layer = (i == n_layers - 1)
        if kind == 0:
            ao = chunk_attention(i, j)
        elif kind == 1:
            ao = gla(i)
        else:
            ao = diff_attention(i, j)
        if ao is None:
            for t in range(NT):
                dma("sp", out_d[t * 128:(t + 1) * 128, :], h[:, t, :], [("h", t)], [("out", t)], "out")
            break
        out_proj_ln(i, ao, final=(last_layer and stop_mid))
        if not (last_layer and stop_mid):
            ffn(i, final=last_layer)

    ln_drain()
    S.fence("sp", [("out", t) for t in range(NT)])
    S.resolve()
    sems = {k: es.enter_context(nc.semaphore("s_%s_%s" % k)) for k in S.sem_keys()}
    with nc.Block() as block:
        S.emit(block, sems)
    es.close()
    return nc


def _t5_bucket_np(rel):
    nb = 16
    max_exact = 8
    ret = (rel > 0).astype(np.int32) * nb
    n = np.abs(rel)
    is_small = n < max_exact
    n_f = np.maximum(n, 1).astype(np.float32)
    large = max_exact + (np.log(n_f / np.float32(max_exact)) / np.float32(math.log(128 / max_exact))
                         * np.float32(nb - max_exact)).astype(np.int32)
    large = np.minimum(large, nb - 1)
    return ret + np.where(is_small, n, large)


def _prep_shared(inp):
    f = np.float32
    k_in = np.arange(128)[:, None]
    q_in = np.arange(128)[None, :]
    a_rel = np.asarray(inp["a_rel_bias"], f)
    a_bias = np.empty((2, 16, 128, 640), f)
    for jj in range(5):
        rel = q_in - k_in + (4 - jj) * 128
        idx = np.clip(rel, -128, 128) + 128
        vals = a_rel[:, idx, :]
        vals = np.transpose(vals, (0, 3, 1, 2))
        if jj == 4:
            mask = (k_in >= 64) & (q_in < 64)
        elif jj == 0:
            mask = (k_in < 64) & (q_in >= 64)
        else:
            mask = np.zeros((128, 128), bool)
        a_bias[:, :, :, jj * 128:(jj + 1) * 128] = np.where(mask[None, None], f(NEG), vals)
    t5 = np.asarray(inp["t5_table"], f)
    c_bias = np.empty((8, 128, 768), f)
    for bi, jd in enumerate([0, 1, 2, 2, 2, 2]):
        rel = (k_in - q_in - 128 * jd).astype(np.int32)
        vals = np.transpose(t5[_t5_bucket_np(rel)], (2, 0, 1))
        if jd == 0:
            vals = np.where(((k_in >= 64) & (q_in < 64))[None], f(NEG), vals)
        c_bias[:, :, bi * 128:(bi + 1) * 128] = vals
    assert (_t5_bucket_np(np.arange(-2048, -128)) == 15).all()
    lngb = np.stack([inp["ln1_g"], inp["ln1_b"], inp["ln2_g"], inp["ln2_b"]], axis=1).astype(f)
    lngb = np.ascontiguousarray(np.broadcast_to(lngb[:, :, None, :], (DEPTH, 4, 128, D)))
    lam = np.stack([inp["c_lam_q1"][0], inp["c_lam_k1"][0], inp["c_lam_q2"][0], inp["c_lam_k2"][0]], 0).astype(f)
    s = np.arange(128)
    U = ((s[:, None] // 64 == s[None, :] // 64) & (s[:, None] > s[None, :])).astype(f)
    return {
        "lngb": lngb,
        "ffn_gu": np.ascontiguousarray(inp["ffn_w_gate_up"], f),
        "ffn_d": np.ascontiguousarray(inp["ffn_w_down"], f),
        "a_qkv": np.ascontiguousarray(inp["a_w_qkv"], f),
        "a_o": np.ascontiguousarray(inp["a_w_o"], f),
        "a_bias": a_bias,
        "b_in": np.ascontiguousarray(inp["b_w_in"][0], f),
        "b_g1": np.ascontiguousarray(inp["b_w_g1"][0], f),
        "b_g2aug": np.ascontiguousarray(np.concatenate([inp["b_w_g2"][0], inp["b_b_g"][0][None, :]], 0), f),
        "b_gn": np.ascontiguousarray(np.broadcast_to(np.asarray(inp["b_g_norm"][0], f)[None, :], (128, 256))),
        "b_o": np.ascontiguousarray(inp["b_w_o"][0], f),
        "c_qkv": np.ascontiguousarray(inp["c_w_qkv"][0], f),
        "c_o": np.ascontiguousarray(inp["c_w_o"][0], f),
        "c_bias": c_bias,
        "c_cc": np.ascontiguousarray(np.broadcast_to(t5[15, :][None, :], (128, 8))),
        "c_gn": np.ascontiguousarray(np.broadcast_to(np.asarray(inp["c_g_norm"][0], f)[None, :], (128, 128))),
        "c_lam": np.ascontiguousarray(np.broadcast_to(lam[None], (128, 4, 64))),
        "k_ident": np.eye(128, dtype=f),
        "k_U": U,
    }


_NC_CACHE = {}


def kernel(_n_layers=DEPTH, _stop_mid=False, _cores=8, **inputs):
    inp = {k: np.asarray(v) for k, v in inputs.items()}
    shared = _prep_shared(inp)
    key = (_n_layers, _stop_mid)
    if key not in _NC_CACHE:
        _NC_CACHE[key] = build_program(_n_layers, _stop_mid)
    nc = _NC_CACHE[key]
    x = np.asarray(inp["x"], np.float32)
    in_maps = []
    for b in range(_cores):
        m = dict(shared)
        m["x"] = np.ascontiguousarray(x[b])
        in_maps.append(m)
    res = run_bass_kernel_spmd(nc, in_maps, core_ids=list(range(_cores)))
    return np.stack([res.results[b]["out"] for b in range(_cores)], axis=0)
```

```python
import math
import bisect
import contextlib
import numpy as np
import concourse.bass as bass
import concourse.mybir as mybir
from concourse.bass_utils import run_bass_kernel_spmd

F32 = mybir.dt.float32
BF16 = mybir.dt.bfloat16
AF = mybir.ActivationFunctionType
ALU = mybir.AluOpType

D = 1024
T = 2048
NT = 16
KC = 8
DEPTH = 4
FF = 2816
DN_ALPHA = (2.0 * DEPTH) ** 0.25
LN_EPS = 1e-5
RMS_EPS = 1e-6
NEG = -30000.0
DBG = set()

ENGS = ("pe", "act", "dve", "pool", "sp")


class Op:
    __slots__ = ("eng", "fn", "deps", "idx", "gidx", "signal", "cnt", "dma", "waits", "tag")


class Sched:
    def __init__(self):
        self.q = {e: [] for e in ENGS}
        self.lastw = {}
        self.readers = {}
        self.g = 0
        self.dma_gidx = {}
        self.tag = ""
        self.guard = None

    def last(self, eng):
        for op in reversed(self.q[eng]):
            if op.fn is not None and op.dma is None:
                return op
        return None

    def collect(self, keys):
        deps = set()
        for k in keys:
            d = self.lastw.get(k)
            if d is not None:
                deps.add(d)
            rd = self.readers.get(k)
            if rd:
                for x in rd.values():
                    if isinstance(x, list):
                        deps.update(x)
                    else:
                        deps.add(x)
        return deps

    def add(self, eng, fn, r=(), w=(), dma=None, extra=None):
        op = Op()
        op.eng = eng
        op.fn = fn
        op.idx = len(self.q[eng])
        op.gidx = self.g
        self.g += 1
        op.dma = dma
        op.signal = False
        op.cnt = 0
        op.tag = self.tag
        deps = set()
        for k in r:
            d = self.lastw.get(k)
            if d is not None:
                deps.add(d)
        for k in w:
            d = self.lastw.get(k)
            if d is not None:
                deps.add(d)
            rd = self.readers.get(k)
            if rd:
                for x in rd.values():
                    if isinstance(x, list):
                        deps.update(x)
                    else:
                        deps.add(x)
        if extra:
            deps |= extra
        if self.guard:
            deps |= self.guard
        op.deps = deps
        for k in w:
            self.lastw[k] = op
            self.readers[k] = {}
        for k in r:
            rd = self.readers.setdefault(k, {})
            if dma is not None:
                rd.setdefault("dma", []).append(op)
            else:
                rd[eng] = op
        self.q[eng].append(op)
        if dma is not None:
            self.dma_gidx.setdefault(dma, []).append(op.gidx)
        return op

    def fence(self, eng, keys):
        return self.add(eng, None, r=keys)

    def barrier(self, engs=("pe", "act", "dve")):
        lasts = []
        for e in engs:
            for op in reversed(self.q[e]):
                if op.fn is not None and op.dma is None:
                    lasts.append(op)
                    break
        for e in engs:
            op = self.add(e, None)
            op.deps = set(x for x in lasts if x.eng != e)

    def resolve(self):
        for e in ENGS:
            for op in self.q[e]:
                keep = []
                for d in op.deps:
                    if d.dma is not None:
                        keep.append(d)
                        continue
                    if d.fn is None:
                        continue
                    if d.eng == op.eng:
                        if op.eng == "pe":
                            continue
                    d.signal = True
                    keep.append(d)
                op.deps = keep
        for e in ENGS:
            c = 0
            for op in self.q[e]:
                if op.signal and op.dma is None:
                    c += 1
                op.cnt = c
        for e in ENGS:
            waited = {}
            for op in self.q[e]:
                need = {}
                for d in op.deps:
                    if d.dma is not None:
                        lst = self.dma_gidx[d.dma]
                        n = bisect.bisect_left(lst, op.gidx)
                        key = ("dma", d.dma)
                        val = 16 * n
                    else:
                        key = ("eng", d.eng)
                        val = d.cnt
                    if val > need.get(key, 0):
                        need[key] = val
                op.waits = []
                for key, val in need.items():
                    if waited.get(key, 0) >= val:
                        continue
                    waited[key] = val
                    op.waits.append((key, val))

    def sem_keys(self):
        keys = [("eng", e) for e in ENGS if e != "sp"]
        keys += [("dma", k) for k in self.dma_gidx]
        return keys

    def emit(self, block, sems):
        engmap = {"pe": block.tensor, "act": block.scalar, "dve": block.vector,
                  "pool": block.gpsimd, "sp": block.sync}
        for e in ENGS:
            ops = self.q[e]
            if not ops:
                continue

            def body(eng, ops=ops, e=e):
                for op in ops:
                    for key, val in op.waits:
                        eng.wait_ge(sems[key], val)
                    if op.fn is None:
                        continue
                    ins = op.fn(eng)
                    if op.dma is not None:
                        ins.then_inc(sems[("dma", op.dma)], 16)
                    elif op.signal:
                        ins.then_inc(sems[("eng", e)], 1)

            engmap[e](body)


class Ring:
    def __init__(self, n):
        self.n = n
        self.i = -1

    def next(self):
        self.i = (self.i + 1) % self.n
        return self.i


class Piece:
    def __init__(self, pid, nkc, ncols, parts):
        self.pid = pid
        self.nkc = nkc
        self.ncols = ncols
        self.parts = parts
        self.slot = None
        self.view = None

    def key(self, part):
        return ("w", self.slot, part)

    def ap(self, part, kc, a, b):
        _, k0, pk, c0, pw = self.parts[part]
        return self.view[:, k0 + kc, c0 + a:c0 + b]


class WLoader:
    NSLOT = 4
    SLOT_ELEMS = 8 * 512

    def __init__(self, S, wsl):
        self.S = S
        self.wsl = wsl
        self.plan = []
        self.byid = {}
        self.free = list(range(self.NSLOT))
        self.nxt = 0

    def add(self, piece):
        assert piece.nkc * piece.ncols <= self.SLOT_ELEMS
        self.plan.append(piece)
        self.byid[piece.pid] = piece

    def pump(self):
        while self.free and self.nxt < len(self.plan):
            s = self.free.pop(0)
            p = self.plan[self.nxt]
            self.nxt += 1
            p.slot = s
            p.view = self.wsl[:, s, 0:p.nkc * p.ncols].rearrange("p (k c) -> p k c", c=p.ncols)
            extra = self.S.collect([("w", s, 0), ("w", s, 1), ("w", s, 2)])
            for pi, (src, k0, pk, c0, pw) in enumerate(p.parts):
                dst = p.view[:, k0:k0 + pk, c0:c0 + pw]
                self._dma(dst, src, [("w", s, pi)], "w%d" % s, extra)

    def _dma(self, dst, src, wkeys, sem, extra):
        self.S.add("pool", lambda q: q.dma_start(out=dst, in_=src), w=wkeys, dma=sem, extra=set(extra))

    def get(self, pid):
        p = self.byid[pid]
        assert p.slot is not None, ("weight piece not yet issued", pid)
        return p

    def done(self, pid):
        p = self.byid[pid]
        self.free.append(p.slot)
        self.pump()


def build_program(n_layers=DEPTH, stop_mid=False):
    nc = bass.Bass("TRN2", target_bir_lowering=False)

    def din(name, shape):
        return nc.dram_tensor(name, list(shape), F32, kind="ExternalInput").ap()

    x_d = din("x", [T, D])
    lngb_d = din("lngb", [DEPTH, 4, 128, D])
    gu_d = din("ffn_gu", [DEPTH, D, 2 * FF])
    dn_d = din("ffn_d", [DEPTH, FF, D])
    aqkv_d = din("a_qkv", [2, D, 3 * D])
    ao_d = din("a_o", [2, D, D])
    abias_d = din("a_bias", [2, 16, 128, 640])
    bin_d = din("b_in", [D, 3 * D])
    bg1_d = din("b_g1", [D, 16])
    bg2_d = din("b_g2aug", [17, 512])
    bgn_d = din("b_gn", [128, 256])
    bo_d = din("b_o", [D, D])
    cqkv_d = din("c_qkv", [D, 3 * D])
    co_d = din("c_o", [D, D])
    cbias_d = din("c_bias", [8, 128, 768])
    cgn_d = din("c_gn", [128, 128])
    ccc_d = din("c_cc", [128, 8])
    clam_d = din("c_lam", [128, 4, 64])
    cident_d = din("k_ident", [128, 128])
    cU_d = din("k_U", [128, 128])
    out_d = nc.dram_tensor("out", [T, D], F32, kind="ExternalOutput").ap()

    S = Sched()
    es = contextlib.ExitStack()

    def sb(name, shape, dt):
        return es.enter_context(nc.sbuf_tensor(name, list(shape), dt))

    h = sb("h", [128, NT, D], F32)
    hT = sb("hT", [128, KC, T], BF16)
    wsl = sb("wsl", [128, WLoader.NSLOT, WLoader.SLOT_ELEMS], BF16)
    gb = sb("gb", [128, 2, D], F32)
    SCRN = 31872
    scr = sb("scr", [128, SCRN], BF16)
    ident = sb("ident", [128, 128], BF16)
    hb = sb("hb", [128, 2, D], BF16)
    ebuf = sb("ebuf", [128, 3, 640], BF16)
    onb = sb("onb", [128, 2, 256], BF16)
    stat = sb("stat", [128, 4, 8], F32)
    bst = sb("bst", [128, 4, 2, 6], F32)
    stat2 = sb("stat2", [128, 4, 8], F32)
    cst = sb("cst", [128, 4], F32)
    ps = es.enter_context(nc.psum_tensor("ps", [128, 4096], F32))
    psb = ps.bitcast(BF16)

    W = WLoader(S, wsl)

    def mm(out, lhsT, rhs, start, stop, r, w):
        S.add("pe", lambda q: q.matmul(out, lhsT=lhsT, rhs=rhs, start=start, stop=stop), r=r, w=w)

    def tp(out, in_, idn, r, w):
        S.add("pe", lambda q: q.transpose(out=out, in_=in_, identity=idn), r=r, w=w)

    def act(out, in_, func, r, w, scale=1.0, bias=None, accum=None):
        def f(q):
            kw = {}
            if bias is not None:
                kw["bias"] = bias
            if accum is not None:
                kw["accum_out"] = accum
            return q.activation(out=out, in_=in_, func=func, scale=scale, **kw)
        S.add("act", f, r=r, w=w)

    def tt(eng, out, in0, in1, op, r, w):
        S.add(eng, lambda q: q.tensor_tensor(out=out, in0=in0, in1=in1, op=op), r=r, w=w)

    def ts(eng, out, in0, s1, op0, r, w, s2=None, op1=None):
        if s2 is None:
            S.add(eng, lambda q: q.tensor_scalar(out=out, in0=in0, scalar1=s1, scalar2=None, op0=op0), r=r, w=w)
        else:
            S.add(eng, lambda q: q.tensor_scalar(out=out, in0=in0, scalar1=s1, scalar2=s2, op0=op0, op1=op1), r=r, w=w)

    def stt(eng, out, in0, scalar, in1, op0, op1, r, w):
        S.add(eng, lambda q: q.scalar_tensor_tensor(out=out, in0=in0, scalar=scalar, in1=in1, op0=op0, op1=op1), r=r, w=w)

    def cp(eng, out, in_, r, w):
        if eng == "act":
            S.add("act", lambda q: q.copy(out=out, in_=in_), r=r, w=w)
        else:
            S.add(eng, lambda q: q.tensor_copy(out=out, in_=in_), r=r, w=w)

    def dma(eng, out, in_, r, w, sem):
        S.add(eng, lambda q: q.dma_start(out=out, in_=in_), r=r, w=w, dma=sem)

    def bank(b, a=0, n=512):
        return ps[:, b * 512 + a:b * 512 + a + n]

    def pskeys(b0, nb=1):
        return [("ps", b0 + i) for i in range(nb)]

    def wsrc(mat, r0, nk, c0, w):
        return mat[r0:r0 + nk * 128, c0:c0 + w].rearrange("(k p) c -> p k c", p=128)

    def plan_ffn(i):
        for grp in range(2):
            for sp in range(6):
                nf = 2 if sp < 5 else 1
                f0 = grp * 11 + 2 * sp
                W.add(Piece((i, "gu", grp, sp), 8, 512, [
                    (wsrc(gu_d[i], 0, 8, f0 * 128, nf * 128), 0, 8, 0, nf * 128),
                    (wsrc(gu_d[i], 0, 8, FF + f0 * 128, nf * 128), 0, 8, 256, nf * 128)]))
            for half in range(2):
                W.add(Piece((i, "dA", grp, half), 8, 512, [
                    (wsrc(dn_d[i], grp * 1408, 8, half * 512, 512), 0, 8, 0, 512)]))
            W.add(Piece((i, "dB", grp), 6, 512, [
                (wsrc(dn_d[i], grp * 1408 + 1024, 3, 0, 512), 0, 3, 0, 512),
                (wsrc(dn_d[i], grp * 1408 + 1024, 3, 512, 512), 3, 3, 0, 512)]))

    def plan_o(i, mat):
        for half in range(2):
            W.add(Piece((i, "o", half), 8, 512, [(wsrc(mat, 0, 8, half * 512, 512), 0, 8, 0, 512)]))

    for i in range(n_layers):
        kind = i % 3
        j = i // 3
        if kind == 0:
            for g in range(4):
                W.add(Piece((i, "qk", g), 8, 512, [
                    (wsrc(aqkv_d[j], 0, 8, g * 256, 256), 0, 8, 0, 256),
                    (wsrc(aqkv_d[j], 0, 8, D + g * 256, 256), 0, 8, 256, 256)]))
                W.add(Piece((i, "v", g), 8, 256, [(wsrc(aqkv_d[j], 0, 8, 2 * D + g * 256, 256), 0, 8, 0, 256)]))
            plan_o(i, ao_d[j])
        elif kind == 1:
            W.add(Piece((i, "g1"), 8, 16, [(wsrc(bg1_d, 0, 8, 0, 16), 0, 8, 0, 16)]))
            for hd in range(4):
                W.add(Piece((i, "qkv", hd), 8, 512, [
                    (wsrc(bin_d, 0, 8, hd * 128, 128), 0, 8, 0, 128),
                    (wsrc(bin_d, 0, 8, 512 + hd * 128, 128), 0, 8, 128, 128),
                    (wsrc(bin_d, 0, 8, 1024 + hd * 256, 256), 0, 8, 256, 256)]))
                W.add(Piece((i, "r", hd), 8, 256, [(wsrc(bin_d, 0, 8, 2048 + hd * 256, 256), 0, 8, 0, 256)]))
            plan_o(i, bo_d)
        else:
            for g in range(4):
                W.add(Piece((i, "qk", g), 8, 512, [
                    (wsrc(cqkv_d, 0, 8, g * 256, 256), 0, 8, 0, 256),
                    (wsrc(cqkv_d, 0, 8, D + g * 256, 256), 0, 8, 256, 256)]))
                W.add(Piece((i, "v", g), 8, 256, [(wsrc(cqkv_d, 0, 8, 2 * D + g * 256, 256), 0, 8, 0, 256)]))
            plan_o(i, co_d)
        if not (stop_mid and i == n_layers - 1):
            plan_ffn(i)

    S.add("dve", lambda q: q.memset(cst[:, 0:1], -0.5), r=[], w=["cst"])
    dma("pool", ident[:], cident_d, [], ["ident"], "cst")
    W.pump()

    for t in range(NT):
        dma("sp", h[:, t, :], x_d[t * 128:(t + 1) * 128, :], [], [("h", t)], "xin%d" % (t // 2))

    hb_ring = Ring(2)
    tb_ring = Ring(2)
    evac_flip = [0]

    def to_featmajor(src_keys, t, nch, dst_fn, dst_keys, src_ap_fn, evac_eng=None):
        tb = 6 + tb_ring.next()
        for c in range(nch):
            tp(psb[:, tb * 1024 + c * 128: tb * 1024 + (c + 1) * 128], src_ap_fn(c), ident[:],
               list(src_keys) + ["ident"], pskeys(tb))
        src = psb[:, tb * 1024: tb * 1024 + nch * 128].rearrange("p (c t) -> p c t", t=128)
        evac_flip[0] ^= 1
        if evac_eng is None:
            evac_eng = "act" if evac_flip[0] else "dve"
        cp(evac_eng, dst_fn(), src, pskeys(tb), dst_keys)

    def make_hT(t):
        r = hb_ring.next()
        cp("act", hb[:, r, :], h[:, t, :], [("h", t)], [("hb", r)])
        to_featmajor([("hb", r)], t, 8, lambda: hT[:, :, t * 128:(t + 1) * 128], [("hT", t)],
                     lambda c: hb[:, r, c * 128:(c + 1) * 128])

    init_hT = []

    st_ring = Ring(4)
    st2_ring = Ring(4)

    ln_pipe = []

    ln_backlog = []

    def ln_step():
        if ln_backlog:
            ln_pipe.append(ln_backlog.pop(0))
        for ent in list(ln_pipe):
            if ent[1]:
                ent[1].pop(0)()
        ln_pipe[:] = [ent for ent in ln_pipe if ent[1]]

    def ln_drain():
        while ln_pipe or ln_backlog:
            ln_step()

    def need_hT(tiles):
        tiles = set(tiles)
        while any(ent[0] in tiles for ent in ln_pipe + ln_backlog):
            ln_step()

    def ln_tick():
        if ln_pipe or ln_backlog:
            ln_step()

    def _mk_init(t):
        loc = {}

        def sa():
            r = hb_ring.next()
            loc["r"] = r
            cp("act", hb[:, r, :], h[:, t, :], [("h", t)], [("hb", r)])

        def sb():
            r = loc["r"]
            to_featmajor([("hb", r)], t, 8, lambda: hT[:, :, t * 128:(t + 1) * 128], [("hT", t)],
                         lambda c: hb[:, r, c * 128:(c + 1) * 128])
        return [t, [sa, sb]]

    pend = None
    for t in range(NT + 1):
        cur = None
        if t < NT:
            r = hb_ring.next()
            eng = "act" if t % 2 == 0 else "dve"
            cp(eng, hb[:, r, :], h[:, t, :], [("h", t)], [("hb", r)])
            cur = (t, r, "dve" if eng == "act" else "act")
        if pend is not None:
            pt, pr, pe_ = pend
            to_featmajor([("hb", pr)], pt, 8, lambda pt=pt: hT[:, :, pt * 128:(pt + 1) * 128], [("hT", pt)],
                         lambda c, pr=pr: hb[:, pr, c * 128:(c + 1) * 128], evac_eng=pe_)
        pend = cur

    def layer_norm(t, y_ap, ykeys, alpha, final):
        hk = [("h", t)]
        hv = h[:, t, :]
        si = st_ring.next()
        mv = stat[:, si, 0:2]
        rstd = stat[:, si, 2:3]
        nmr = stat[:, si, 3:4]
        ve = stat[:, si, 4:5]

        def s1():
            S.tag = "ln.t%d.s1" % t
            stt("dve", hv, hv, float(alpha), y_ap, ALU.mult, ALU.add, hk + ykeys, hk)
            for k in range(2):
                S.add("dve", lambda q, k=k: q.bn_stats(out=bst[:, si, k, :], in_=h[:, t, k * 512:(k + 1) * 512]),
                      r=hk, w=[("bst", si, k)])
            S.add("dve", lambda q: q.bn_aggr(out=mv, in_=bst[:, si, :, :]), r=[("bst", si, 0), ("bst", si, 1)],
                  w=[("mv", si)])
            ts("dve", ve, stat[:, si, 1:2], LN_EPS, ALU.add, [("mv", si)], [("ve", si)])

        def s2():
            S.tag = "ln.t%d.s2" % t
            tt("pool", rstd, ve, cst[:, 0:1], ALU.pow, [("ve", si), "cst"], [("rstd", si)])
            stt("dve", nmr, stat[:, si, 0:1], -1.0, rstd, ALU.mult, ALU.mult, [("mv", si), ("rstd", si)], [("nmr", si)])
            act(hv, hv, AF.Identity, hk + [("rstd", si), ("nmr", si)], hk, scale=rstd, bias=nmr)

        def s3():
            S.tag = "ln.t%d.s3" % t
            tt("pool", hv, hv, gb[:, 0, :], ALU.mult, hk + [("gb", 0)], hk)

        def s4():
            S.tag = "ln.t%d.s4" % t
            tt("dve", hv, hv, gb[:, 1, :], ALU.add, hk + [("gb", 1)], hk)
            if final:
                dma("sp", out_d[t * 128:(t + 1) * 128, :], hv, hk, [("out", t)], "out")
            else:
                r = hb_ring.next()
                hbr[0] = r
                cp("act", hb[:, r, :], h[:, t, :], [("h", t)], [("hb", r)])

        def s5():
            S.tag = "ln.t%d.s5" % t
            if not final:
                r = hbr[0]
                to_featmajor([("hb", r)], t, 8, lambda: hT[:, :, t * 128:(t + 1) * 128], [("hT", t)],
                             lambda c: hb[:, r, c * 128:(c + 1) * 128])

        hbr = [0]
        ln_pipe.append([t, [s1, s2, s3, s4, s5]])
        ln_step()

    def load_gb(i, which):
        dma("sp", gb[:, 0, :], lngb_d[i, 2 * which, :, :], [], [("gb", 0)], "gb%d_%d" % (i, which))
        dma("sp", gb[:, 1, :], lngb_d[i, 2 * which + 1, :, :], [], [("gb", 1)], "gb%d_%d" % (i, which))

    yset_ring = Ring(3)

    def out_proj_ln(i, ao, final):
        pO = [W.get((i, "o", 0)), W.get((i, "o", 1))]
        for t in range(NT):
            S.tag = "L%d.oproj.t%d" % (i, t)
            ys = yset_ring.next()
            for half in range(2):
                b = 2 * ys + half
                for kc in range(KC):
                    mm(bank(b), ao[:, kc, t * 128:(t + 1) * 128], pO[half].ap(0, kc, 0, 512), kc == 0, kc == KC - 1,
                       [("ao", kc, t), pO[half].key(0)], pskeys(b))
            layer_norm(t, ps[:, ys * 1024:(ys + 1) * 1024], pskeys(2 * ys, 2), DN_ALPHA, final)
        if final:
            ln_drain()
        W.done((i, "o", 0))
        W.done((i, "o", 1))

    def ffn(i, final):
        actT = scr[:, 0:11 * T].rearrange("p (f t) -> p f t", t=T)
        sgt = scr[:, 11 * T:11 * T + 2048].bitcast(F32).rearrange("p (a d) -> p a d", d=512)
        sg_ring = Ring(2)
        nsg = [0]
        ffn_guard = {S.last("pe")}
        for grp in range(2):
            for sp in range(6):
                nf = 2 if sp < 5 else 1
                pc = W.get((i, "gu", grp, sp))
                S.tag = "L%d.ffn.gu%d.%d" % (i, grp, sp)
                for tq in range(4):
                  for fi in range(nf):
                        fc = 2 * sp + fi
                        need_hT(range(4 * tq, 4 * tq + 4))
                        ys = yset_ring.next()
                        bg, bu = 2 * ys, 2 * ys + 1
                        hk = [("hT", 4 * tq + u) for u in range(4)]
                        for kc in range(KC):
                            mm(bank(bg), pc.ap(0, kc, fi * 128, fi * 128 + 128), hT[:, kc, tq * 512:(tq + 1) * 512],
                               kc == 0, kc == KC - 1, hk + [pc.key(0)], pskeys(bg))
                        for kc in range(KC):
                            mm(bank(bu), pc.ap(1, kc, fi * 128, fi * 128 + 128), hT[:, kc, tq * 512:(tq + 1) * 512],
                               kc == 0, kc == KC - 1, hk + [pc.key(1)], pskeys(bu))
                        si = sg_ring.next()
                        if nsg[0] < 2:
                            S.guard = ffn_guard
                        nsg[0] += 1
                        act(sgt[:, si, :], bank(bg), AF.Silu, pskeys(bg), [("sgt", si)])
                        S.guard = None
                        tt("dve", actT[:, fc, tq * 512:(tq + 1) * 512], sgt[:, si, :], bank(bu), ALU.mult,
                           [("sgt", si)] + pskeys(bu), [("actT", fc, tq)])
                        ln_tick()
                W.done((i, "gu", grp, sp))
            if "ffn_a" in DBG:
                for t in range(NT):
                    dma("sp", out_d[t * 128:(t + 1) * 128, :], h[:, t, :], [("h", t)], [("out", t)], "out")
                return
            pA = [W.get((i, "dA", grp, 0)), W.get((i, "dA", grp, 1))]
            pB = W.get((i, "dB", grp))
            if grp == 1 and "ffn_e" not in DBG:
                load_gb(i, 1)
            for t in range(NT):
                S.tag = "L%d.ffn.dn%d.t%d" % (i, grp, t)
                ys = yset_ring.next()
                for half in range(2):
                    b = 2 * ys + half
                    for kc in range(11):
                        if kc < 8:
                            rhs = pA[half].ap(0, kc, 0, 512)
                            wk = pA[half].key(0)
                        else:
                            rhs = pB.ap(half, kc - 8, 0, 512)
                            wk = pB.key(half)
                        mm(bank(b), actT[:, kc, t * 128:(t + 1) * 128], rhs, kc == 0, kc == 10,
                           [("actT", kc, t // 4), wk], pskeys(b))
                y_ap = ps[:, ys * 1024:(ys + 1) * 1024]
                if grp == 0 or "ffn_c" in DBG:
                    hk = [("h", t)]
                    stt("dve", h[:, t, :], h[:, t, :], float(DN_ALPHA if grp == 0 else 1.0), y_ap, ALU.mult, ALU.add,
                        hk + pskeys(2 * ys, 2), hk)
                else:
                    layer_norm(t, y_ap, pskeys(2 * ys, 2), 1.0, final and "ffn_d" not in DBG)
            if final or grp == 0:
                ln_drain()
            if "ffn_d" in DBG and grp == 1:
                for t in range(NT):
                    dma("sp", out_d[t * 128:(t + 1) * 128, :], h[:, t, :], [("h", t)], [("out", t)], "out")
            if "ffn_b" in DBG or ("ffn_c" in DBG and grp == 1):
                for t in range(NT):
                    dma("sp", out_d[t * 128:(t + 1) * 128, :], h[:, t, :], [("h", t)], [("out", t)], "out")
                return
            W.done((i, "dA", grp, 0))
            W.done((i, "dA", grp, 1))
            W.done((i, "dB", grp))

    pj_ring = Ring(4)

    def proj_fm(dst_fn, dkey_fn, pc, part, c0, scale):
        for tq in range(4):
            need_hT(range(4 * tq, 4 * tq + 4))
            b = pj_ring.next()
            hk = [("hT", 4 * tq + u) for u in range(4)]
            for kc in range(KC):
                mm(bank(b), pc.ap(part, kc, c0, c0 + 128), hT[:, kc, tq * 512:(tq + 1) * 512], kc == 0, kc == KC - 1,
                   hk + [pc.key(part)], pskeys(b))
            ln_tick()
            if scale is None:
                cp("dve", dst_fn(tq), bank(b), pskeys(b), [dkey_fn(tq)])
            else:
                S.add("act", lambda q, o=dst_fn(tq), b=b: q.mul(out=o, in_=bank(b), mul=float(scale)),
                      r=pskeys(b), w=[dkey_fn(tq)])

    def proj_tm(t, pc, part, c0, ncols):
        need_hT([t])
        b = pj_ring.next()
        for kc in range(KC):
            mm(bank(b, 0, ncols), hT[:, kc, t * 128:(t + 1) * 128], pc.ap(part, kc, c0, c0 + ncols), kc == 0,
               kc == KC - 1, [("hT", t), pc.key(part)], pskeys(b))
        return b

    class Pipe:
        def __init__(self):
            self.fl = []

        def push(self, stages):
            self.fl.append(list(stages))
            self.step()

        def step(self):
            for st_ in list(self.fl):
                if st_:
                    st_.pop(0)()
            self.fl = [s_ for s_ in self.fl if s_]

        def drain(self):
            while self.fl:
                self.step()

    bstage = hb[:].rearrange("p a d -> p (a d)").bitcast(F32)

    def chunk_attention(i, j):
        qT = scr[:, 0:4096].rearrange("p (c t) -> p c t", t=T)
        kT = scr[:, 4096:8192].rearrange("p (c t) -> p c t", t=T)
        v = scr[:, 8192:8192 + 4160].rearrange("p (t h d) -> p t h d", h=4, d=65)
        ao = scr[:, 12352:12352 + 16384].rearrange("p (c t) -> p c t", t=T)
        expB = scr[:, 28736:28736 + 2560].rearrange("p (h q) -> p h q", q=640)
        gb_loaded = [False]
        mix_guard = {S.last("pe")} if S.last("pe") is not None else None
        v_init = [False]
        e_ring = Ring(3)
        s_ring = Ring(2)
        on_ring = Ring(2)
        for g in range(4):
            S.tag = "L%d.g%d.proj" % (i, g)
            pqk = W.get((i, "qk", g))
            pv = W.get((i, "v", g))

            def build_expb(hh):
                dma("pool", bstage[:, 0:640], abias_d[j, 4 * g + hh], [], ["bstage", ("hb", 0), ("hb", 1)], "bias")
                act(expB[:, hh, :], bstage[:, 0:640], AF.Exp, ["bstage", ("hb", 0), ("hb", 1)], [("expB", hh)])

            for c in range(2):
                proj_fm(lambda tq, c=c: qT[:, c, tq * 512:(tq + 1) * 512], lambda tq, c=c: ("qT", c, tq), pqk, 0, c * 128, None)
                ln_drain()
                if not gb_loaded[0]:
                    gb_loaded[0] = True
                    load_gb(i, 0)
                build_expb(2 * c)
                build_expb(2 * c + 1)
                proj_fm(lambda tq, c=c: kT[:, c, tq * 512:(tq + 1) * 512], lambda tq, c=c: ("kT", c, tq), pqk, 1, c * 128, None)
            if not v_init[0]:
                v_init[0] = True
                S.guard = mix_guard
                S.add("dve", lambda q: q.memset(scr[:, 8192:8192 + 4160], 1.0), r=[], w=[("v", t) for t in range(NT)])
                S.guard = None
            for t in range(NT):
                b = proj_tm(t, pv, 0, 0, 256)
                cp("act" if t % 2 else "dve", v[:, t, :, 0:64], bank(b, 0, 256).rearrange("p (h d) -> p h d", d=64),
                   pskeys(b), [("v", t)])
            W.done((i, "qk", g))
            W.done((i, "v", g))

            pipe = Pipe()
            for qb in range(NT):
                for hh in range(4):
                    def mk(qb=qb, hh=hh):
                        c, po = hh // 2, (hh % 2) * 64
                        nkb = min(qb, 4) + 1
                        jlo = 5 - nkb
                        loc = {}
                        ob = 4 + (qb % 2)
                        tagb = "L%d.g%d.q%d.h%d." % (i, g, qb, hh)

                        def sA():
                            S.tag = tagb + "A"
                            ss = s_ring.next()
                            loc["ss"] = ss
                            base = ss * 1024
                            for jj in range(jlo, 5):
                                kb = qb - 4 + jj
                                mm(ps[:, base + jj * 128:base + (jj + 1) * 128], kT[po:po + 64, c, kb * 128:(kb + 1) * 128],
                                   qT[po:po + 64, c, qb * 128:(qb + 1) * 128], True, True,
                                   [("kT", c, kb // 4), ("qT", c, qb // 4)], pskeys(2 * ss + (1 if jj == 4 else 0)))

                        def sB():
                            S.tag = tagb + "B"
                            ss = loc["ss"]
                            base = ss * 1024
                            e = e_ring.next()
                            loc["e"] = e
                            act(ebuf[:, e, jlo * 128:640], ps[:, base + jlo * 128:base + 640], AF.Exp, pskeys(2 * ss, 2), [("E", e)],
                                scale=0.125)

                        def sB2():
                            S.tag = tagb + "B2"
                            e = loc["e"]
                            tt("dve", ebuf[:, e, jlo * 128:640], ebuf[:, e, jlo * 128:640], expB[:, hh, jlo * 128:640], ALU.mult,
                               [("E", e), ("expB", hh)], [("E", e)])

                        def sC():
                            S.tag = tagb + "C"
                            e = loc["e"]
                            for jj in range(jlo, 5):
                                kb = qb - 4 + jj
                                mm(bank(ob, hh * 65, 65), ebuf[:, e, jj * 128:(jj + 1) * 128], v[:, kb, hh, :], jj == jlo, jj == 4,
                                   [("E", e), ("v", kb)], pskeys(ob))

                        def sD1():
                            S.tag = tagb + "D1"
                            o3 = bank(ob, 0, 260).rearrange("p (h d) -> p h d", d=65)
                            si = st2_ring.next()
                            rec = stat2[:, si, 0:4]
                            S.add("dve", lambda q: q.reciprocal(out=rec, in_=o3[:, :, 64]), r=pskeys(ob), w=[("rec", si)])
                            r = on_ring.next()
                            loc["r"] = r
                            tt("dve", onb[:, r, :].rearrange("p (h d) -> p h d", d=64), o3[:, :, 0:64],
                               rec.unsqueeze(2).to_broadcast([128, 4, 64]), ALU.mult, pskeys(ob) + [("rec", si)], [("on", r)])

                        def sD2():
                            S.tag = tagb + "D2"
                            r = loc["r"]
                            tb = 6 + tb_ring.next()
                            loc["tb"] = tb
                            for cc in range(2):
                                tp(psb[:, tb * 1024 + cc * 128: tb * 1024 + (cc + 1) * 128], onb[:, r, cc * 128:(cc + 1) * 128],
                                   ident[:], [("on", r), "ident"], pskeys(tb))

                        def sD3():
                            S.tag = tagb + "D3"
                            tb = loc["tb"]
                            srcp = psb[:, tb * 1024: tb * 1024 + 256].rearrange("p (c t) -> p c t", t=128)
                            cp("dve", ao[:, 2 * g:2 * g + 2, qb * 128:(qb + 1) * 128], srcp, pskeys(tb),
                               [("ao", 2 * g, qb), ("ao", 2 * g + 1, qb)])

                        stages = [sA, sB, sB2, sC]
                        if hh == 3:
                            stages += [sD1, sD2, sD3]
                        return stages
                    pipe.push(mk())
            pipe.drain()
        return ao

    def diff_attention(i, j):
        lam_init = 0.8 - 0.6 * math.exp(-0.3 * i)
        qT = scr[:, 0:4096].rearrange("p (c t) -> p c t", t=T)
        kT = scr[:, 4096:8192].rearrange("p (c t) -> p c t", t=T)
        v = scr[:, 8192:8192 + 4128].rearrange("p (t h d) -> p t h d", h=2, d=129)
        ao = scr[:, 12352:12352 + 16384].rearrange("p (c t) -> p c t", t=T)
        expBp = scr[:, 28736:28736 + 512].rearrange("p (h q) -> p h q", q=256)
        f32z = scr[:, 30272:30272 + 1600].bitcast(F32)
        gC = f32z[:, 0:128]
        lamv = f32z[:, 128:384].rearrange("p (a d) -> p a d", d=64)
        t1 = f32z[:, 384:512]
        dtm3 = scr[:, 29248:29248 + 768].bitcast(F32).rearrange("p (a d) -> p a d", d=128)
        dt_ring = Ring(3)
        ltmp = f32z[:, 640:768]
        sc = f32z[:, 768:784]
        cc = f32z[:, 784:792]
        ncc = f32z[:, 792:800]
        gb_loaded = [False]
        dma("sp", gC, cgn_d, [], ["gC"], "cst2_%d" % i)
        dma("sp", lamv, clam_d, [], ["lamv"], "cst2_%d" % i)
        dma("sp", cc, ccc_d, [], ["cc"], "cst2_%d" % i)
        ts("dve", ncc, cc, -1.0, ALU.mult, ["cc"], ["ncc"])
        mix_guard = {S.last("pe")} if S.last("pe") is not None else None
        v_init = [False]
        for m in range(2):
            tt("dve", ltmp[:, m * 64:(m + 1) * 64], lamv[:, 2 * m, :], lamv[:, 2 * m + 1, :], ALU.mult, ["lamv"], [("ltmp", m)])
            S.add("dve", lambda q, m=m: q.reduce_sum(out=sc[:, m:m + 1], in_=ltmp[:, m * 64:(m + 1) * 64],
                                                     axis=mybir.AxisListType.X), r=[("ltmp", m)], w=[("sc", m)])
            act(sc[:, 2 + m:3 + m], sc[:, m:m + 1], AF.Exp, [("sc", m)], [("sc", 2 + m)])
        tt("dve", sc[:, 4:5], sc[:, 3:4], sc[:, 2:3], ALU.subtract, [("sc", 2), ("sc", 3)], [("sc", 4)])
        ts("dve", sc[:, 5:6], sc[:, 4:5], -float(lam_init), ALU.add, [("sc", 4)], ["nlam"])
        nlam = sc[:, 5:6]
        e_ring = Ring(3)
        s_ring = Ring(4)
        on_ring = Ring(2)
        ccn = float(1.0 - lam_init) ** 2
        for g in range(4):
            S.tag = "L%d.g%d.proj" % (i, g)
            pqk = W.get((i, "qk", g))
            pv = W.get((i, "v", g))

            def build_expb(hl):
                hd = 2 * g + hl
                dma("pool", bstage[:, 0:256], cbias_d[hd][:, 0:256], [], ["bstage", ("hb", 0), ("hb", 1)], "bias")
                act(expBp[:, hl, :], bstage[:, 0:256], AF.Exp, ["bstage", ("hb", 0), ("hb", 1), "ncc"], [("expB", hl)],
                    bias=ncc[:, hd:hd + 1])

            for c in range(2):
                proj_fm(lambda tq, c=c: qT[:, c, tq * 512:(tq + 1) * 512], lambda tq, c=c: ("qT", c, tq), pqk, 0, c * 128, 0.125)
                ln_drain()
                if not gb_loaded[0]:
                    gb_loaded[0] = True
                    load_gb(i, 0)
                build_expb(c)
                proj_fm(lambda tq, c=c: kT[:, c, tq * 512:(tq + 1) * 512], lambda tq, c=c: ("kT", c, tq), pqk, 1, c * 128, None)
            if not v_init[0]:
                v_init[0] = True
                S.guard = mix_guard
                S.add("dve", lambda q: q.memset(scr[:, 8192:8192 + 4128], 1.0), r=[], w=[("v", t) for t in range(NT)])
                S.guard = None
            for t in range(NT):
                b = proj_tm(t, pv, 0, 0, 256)
                cp("act" if t % 2 else "dve", v[:, t, :, 0:128], bank(b, 0, 256).rearrange("p (h d) -> p h d", d=128),
                   pskeys(b), [("v", t)])
            W.done((i, "qk", g))
            W.done((i, "v", g))

            pipe = Pipe()
            shared = {}
            for qb in range(NT):
                for hl in range(2):
                    for m in range(2):
                        nk4 = qb // 4 + 1
                        for k4 in range(nk4):
                            def mk(qb=qb, hl=hl, m=m, k4=k4, last=(k4 == nk4 - 1)):
                                hd = 2 * g + hl
                                po = m * 64
                                nb = min(4, qb + 1 - 4 * k4)
                                ob = 4 + hl
                                loc = {}
                                tagb = "L%d.g%d.q%d.h%d.m%d.k%d." % (i, g, qb, hl, m, k4)

                                def sA():
                                    S.tag = tagb + "A"
                                    sbk = s_ring.next()
                                    loc["s"] = sbk
                                    for jj in range(nb):
                                        kb = qb - (4 * k4 + jj)
                                        mm(bank(sbk, jj * 128, 128), kT[po:po + 64, hl, kb * 128:(kb + 1) * 128],
                                           qT[po:po + 64, hl, qb * 128:(qb + 1) * 128], True, True,
                                           [("kT", hl, kb // 4), ("qT", hl, qb // 4)], pskeys(sbk))
                                    if "warm" in DBG:
                                        mm(bank(7, 0, 512), ident[:], hT[:, 0, 0:512], True, True, ["ident"], pskeys(7))

                                def sB():
                                    S.tag = tagb + "B"
                                    sbk = loc["s"]
                                    e = e_ring.next()
                                    loc["e"] = e
                                    act(ebuf[:, e, 0:nb * 128], bank(sbk, 0, nb * 128), AF.Exp, pskeys(sbk) + ["cc"], [("E", e)],
                                        bias=cc[:, hd:hd + 1])

                                def sB2():
                                    S.tag = tagb + "B2"
                                    e = loc["e"]
                                    if k4 == 0:
                                        nmul = min(nb, 2)
                                        tt("dve", ebuf[:, e, 0:nmul * 128], ebuf[:, e, 0:nmul * 128], expBp[:, hl, 0:nmul * 128],
                                           ALU.mult, [("E", e), ("expB", hl)], [("E", e)])

                                def sC():
                                    S.tag = tagb + "C"
                                    e = loc["e"]
                                    for jj in range(nb):
                                        kb = qb - (4 * k4 + jj)
                                        mm(bank(ob, m * 129, 129), ebuf[:, e, jj * 128:(jj + 1) * 128], v[:, kb, hl, :],
                                           k4 == 0 and jj == 0, last and jj == nb - 1, [("E", e), ("v", kb)], pskeys(ob))

                                o3 = bank(ob, 0, 258).rearrange("p (h d) -> p h d", d=129)

                                def sD1a():
                                    S.tag = tagb + "D1a"
                                    si = st2_ring.next()
                                    loc["si"] = si
                                    di = dt_ring.next()
                                    loc["di"] = di
                                    dtm = dtm3[:, di, :]
                                    rec = stat2[:, si, 0:2]
                                    S.add("dve", lambda q: q.reciprocal(out=rec, in_=o3[:, :, 128]), r=pskeys(ob), w=[("rec", si)])
                                    tt("dve", stat2[:, si, 2:3], stat2[:, si, 1:2], nlam, ALU.mult, [("rec", si), "nlam"], [("rec2", si)])
                                    ts("dve", t1, o3[:, 0, 0:128], stat2[:, si, 0:1], ALU.mult, pskeys(ob) + [("rec", si)], ["t1"])
                                    stt("dve", dtm, o3[:, 1, 0:128], stat2[:, si, 2:3], t1, ALU.mult, ALU.add,
                                        pskeys(ob) + [("rec2", si), "t1"], [("dtm", di)])

                                def sD1b():
                                    S.tag = tagb + "D1b"
                                    si = loc["si"]
                                    di = loc["di"]
                                    act(ltmp, dtm3[:, di, :], AF.Square, [("dtm", di)], ["sqj", ("ss", si)], accum=stat2[:, si, 3:4])

                                def sD1c():
                                    S.tag = tagb + "D1c"
                                    si = loc["si"]
                                    ts("dve", stat2[:, si, 4:5], stat2[:, si, 3:4], 1.0 / (128.0 * ccn), ALU.mult, [("ss", si)],
                                       [("ms", si)], s2=RMS_EPS / ccn, op1=ALU.add)
                                    tt("pool", stat2[:, si, 5:6], stat2[:, si, 4:5], cst[:, 0:1], ALU.pow, [("ms", si), "cst"],
                                       [("rs", si)])

                                def sD1d():
                                    S.tag = tagb + "D1d"
                                    si = loc["si"]
                                    if hl == 0:
                                        shared[qb] = on_ring.next()
                                    r = shared[qb]
                                    di = loc["di"]
                                    stt("dve", onb[:, r, hl * 128:(hl + 1) * 128], dtm3[:, di, :], stat2[:, si, 5:6], gC, ALU.mult, ALU.mult,
                                        [("dtm", di), ("rs", si), "gC"], [("on", r, hl)])

                                def sD2():
                                    S.tag = tagb + "D2"
                                    r = shared[qb]
                                    tb = 6 if "warm" in DBG else 6 + tb_ring.next()
                                    loc["tb"] = tb
                                    for cc_ in range(2):
                                        tp(psb[:, tb * 1024 + cc_ * 128: tb * 1024 + (cc_ + 1) * 128],
                                           onb[:, r, cc_ * 128:(cc_ + 1) * 128], ident[:],
                                           [("on", r, 0), ("on", r, 1), "ident"], pskeys(tb))

                                def sD3():
                                    S.tag = tagb + "D3"
                                    tb = loc["tb"]
                                    srcp = psb[:, tb * 1024: tb * 1024 + 256].rearrange("p (c t) -> p c t", t=128)
                                    cp("act", ao[:, 2 * g:2 * g + 2, qb * 128:(qb + 1) * 128], srcp, pskeys(tb),
                                       [("ao", 2 * g, qb), ("ao", 2 * g + 1, qb)])

                                stages = [sA, sB, sB2, sC]
                                if last and m == 1:
                                    stages += [sD1a, sD1b, sD1c, sD1d]
                                    if hl == 1:
                                        stages += [sD2, sD3]
                                return stages
                            pipe.push(mk())
            pipe.drain()
        return ao

    def gla(i):
        qT2 = scr[:, 0:4096].rearrange("p (a t) -> p a t", t=T)
        g1T = scr[:, 4096:6144]
        w2 = scr[:, 6144:6656]
        fz = scr[:, 6656:6656 + 5456].bitcast(F32)
        gB = fz[:, 0:256]
        U = fz[:, 256:384]
        ones2 = fz[:, 384:386]

        def hbuf(k, off, n):
            base = 392 + k * 1168 + off
            return fz[:, base:base + n]
        bz = scr[:, 28736:28736 + 2048]
        ao = scr[:, 12352:12352 + 16384].rearrange("p (c t) -> p c t", t=T)
        S.guard = {S.last("pe")} if S.last("pe") is not None else None
        dma("sp", gB, bgn_d, [], ["gB"], "cst2_%d" % i)
        dma("sp", U, cU_d, [], ["U"], "cst2_%d" % i)
        dma("pool", w2[0:17, :], bg2_d, [], ["w2"], "bias")
        S.add("dve", lambda q: q.memset(ones2, 0.0), r=[], w=["ones2"])
        S.add("dve", lambda q: q.memset(ones2[0:64, 0:1], 1.0), r=["ones2"], w=["ones2"])
        S.add("dve", lambda q: q.memset(ones2[64:128, 1:2], 1.0), r=["ones2"], w=["ones2"])
        S.add("dve", lambda q: q.memset(g1T[0:32, :], 1.0), r=[], w=["g1T"])
        S.add("dve", lambda q: q.memset(bz[:, 0:512], 0.0), r=[], w=[("kdec", 0), ("kdec", 1)])
        S.guard = None
        pg1 = W.get((i, "g1"))
        for tq in range(4):
            need_hT(range(4 * tq, 4 * tq + 4))
            b = pj_ring.next()
            hk = [("hT", 4 * tq + u) for u in range(4)]
            for kc in range(KC):
                mm(ps[0:16, b * 512:(b + 1) * 512], pg1.ap(0, kc, 0, 16), hT[:, kc, tq * 512:(tq + 1) * 512],
                   kc == 0, kc == KC - 1, hk + [pg1.key(0)], pskeys(b))
            cp("dve", g1T[0:16, tq * 512:(tq + 1) * 512], ps[0:16, b * 512:(b + 1) * 512], pskeys(b) + ["g1T"], ["g1T"])
        W.done((i, "g1"))
        ln_drain()
        load_gb(i, 0)

        def head_gen(hd, k):
            P1, P2, P3 = 3 * k, 3 * k + 1, 3 * k + 2
            spb = hbuf(k, 0, 128)
            wdec = hbuf(k, 128, 128)
            decay = hbuf(k, 256, 8)
            state = hbuf(k, 264, 256)
            er = hbuf(k, 520, 256)
            onf = hbuf(k, 776, 256)
            e1 = hbuf(k, 1032, 128)
            ssq = hbuf(k, 1160, 8)
            kdec2 = bz[:, k * 256:(k + 1) * 256].rearrange("p (a d) -> p a d", d=128)
            vbf = bz[:, 512 + k * 256:512 + (k + 1) * 256]
            stbf = bz[:, 1024 + k * 512:1024 + (k + 1) * 512].rearrange("p (c d) -> p c d", d=256)
            K_ = lambda name: (name, k)
            pA = W.get((i, "qkv", hd))
            pR = W.get((i, "r", hd))
            proj_fm(lambda tq: qT2[:, k, tq * 512:(tq + 1) * 512], lambda tq: ("qT", k, tq), pA, 0, 0, 128.0 ** -0.5)
            S.add("dve", lambda q: q.memset(state, 0.0), r=[], w=[K_("state")])
            yield
            def proj_kgv(t):
                tok = slice(t * 128, (t + 1) * 128)
                need_hT([t])
                for kc in range(KC):
                    mm(bank(P1, 0, 128), hT[:, kc, tok], pA.ap(1, kc, 0, 128), kc == 0, kc == KC - 1,
                       [("hT", t), pA.key(1)], pskeys(P1))
                mm(bank(P1, 128, 128), g1T[0:17, tok], w2[0:17, hd * 128:(hd + 1) * 128], True, True, ["g1T", "w2"], pskeys(P1))
                for kc in range(KC):
                    mm(bank(P2, 0, 256), hT[:, kc, tok], pA.ap(2, kc, 0, 256), kc == 0, kc == KC - 1,
                       [("hT", t), pA.key(2)], pskeys(P2))

            P4 = 6 + k
            proj_kgv(0)
            for t in range(NT):
                S.tag = "L%d.gla.h%d.t%d" % (i, hd, t)
                tok = slice(t * 128, (t + 1) * 128)
                for kc in range(KC):
                    mm(bank(P3, 0, 256), hT[:, kc, tok], pR.ap(0, kc, 0, 256), kc == 0, kc == KC - 1,
                       [("hT", t), pR.key(0)], pskeys(P3))
                yield
                act(e1, bank(P1, 128, 128), AF.Exp, pskeys(P1), [K_("e1")], scale=-1.0)
                cp("act", vbf, bank(P2, 0, 256), pskeys(P2), [K_("vbf")])
                act(er, bank(P3, 0, 256), AF.Exp, pskeys(P3), [K_("er")], scale=-1.0)
                yield
                act(spb, e1, AF.Ln, [K_("e1")], [K_("spb")], bias=1.0)
                act(er, er, AF.Ln, [K_("er")], [K_("er")], bias=1.0)
                yield
                mm(bank(P1, 256, 128), U, spb, True, True, ["U", K_("spb")], pskeys(P1))
                mm(bank(P1, 384, 2), spb, ones2, True, True, ["ones2", K_("spb")], pskeys(P1))
                act(er, er, AF.Exp, [K_("er")], [K_("er")], scale=-1.0)
                yield
                act(hbuf(k, 128, 130), bank(P1, 256, 130), AF.Exp, pskeys(P1), [K_("wdec"), K_("decay")], scale=-1.0 / 16.0)
                yield
                for c in range(2):
                    lo, hi = c * 64, (c + 1) * 64
                    tt("dve", kdec2[lo:hi, c, :], ps[lo:hi, P1 * 512:P1 * 512 + 128], wdec[lo:hi, :], ALU.mult,
                       pskeys(P1) + [K_("wdec")], [("kdec", k)])
                yield
                mm(bank(P1, 0, 256), kdec2[:, 0, :], vbf, True, True, [("kdec", k), K_("vbf")], pskeys(P1))
                mm(bank(P2, 256, 256), kdec2[:, 1, :], vbf, True, True, [("kdec", k), K_("vbf")], pskeys(P2))
                yield
                kvs = [bank(P1, 0, 256), bank(P2, 256, 256)]
                kvk = [pskeys(P1), pskeys(P2)]
                for c in range(2):
                    stt("dve", state, state, decay[:, c:c + 1], kvs[c], ALU.mult, ALU.add,
                        [K_("state"), K_("decay")] + kvk[c], [K_("state")])
                    cp("dve", stbf[:, c, :], state, [K_("state")], [("stbf", k, c)])
                yield
                mm(bank(P4, 0, 256), qT2[:, k, tok], stbf[:, 0, :], True, True, [("qT", k, t // 4), ("stbf", k, 0)], pskeys(P4))
                mm(bank(P4, 256, 256), qT2[:, k, tok], stbf[:, 1, :], True, True, [("qT", k, t // 4), ("stbf", k, 1)], pskeys(P4))
                if t + 1 < NT:
                    proj_kgv(t + 1)
                yield
                ocol = [0, 256]
                for c in range(2):
                    lo, hi = c * 64, (c + 1) * 64
                    act(onf[lo:hi, :], ps[lo:hi, P4 * 512 + ocol[c]:P4 * 512 + ocol[c] + 256], AF.Square, pskeys(P4),
                        [("onf", k, c), ("ssq", k, c)], accum=ssq[lo:hi, 0:1])
                yield
                ts("dve", ssq[:, 1:2], ssq[:, 0:1], 1.0 / 256.0, ALU.mult, [("ssq", k, 0), ("ssq", k, 1)], [K_("rs1")],
                   s2=RMS_EPS, op1=ALU.add)
                tt("pool", ssq[:, 2:3], ssq[:, 1:2], cst[:, 0:1], ALU.pow, [K_("rs1"), "cst"], [K_("rs2")])
                yield
                for c in range(2):
                    lo, hi = c * 64, (c + 1) * 64
                    stt("dve", onf[lo:hi, :], ps[lo:hi, P4 * 512 + ocol[c]:P4 * 512 + ocol[c] + 256], ssq[lo:hi, 2:3],
                        gB[lo:hi, :], ALU.mult, ALU.mult, pskeys(P4) + [K_("rs2"), "gB", ("onf", k, c)], [("onf", k, c)])
                tt("dve", er, er, bank(P3, 0, 256), ALU.mult, [K_("er")] + pskeys(P3), [K_("er")])
                tt("dve", onb[:, k, :], onf, er, ALU.mult, [("onf", k, 0), ("onf", k, 1), K_("er")], [("on", k)])
                yield
                for cc_ in range(2):
                    tp(psb[:, P3 * 1024 + 512 + cc_ * 128: P3 * 1024 + 512 + (cc_ + 1) * 128], onb[:, k, cc_ * 128:(cc_ + 1) * 128],
                       ident[:], [("on", k), "ident"], pskeys(P3))
                yield
                srcp = psb[:, P3 * 1024 + 512: P3 * 1024 + 768].rearrange("p (c t) -> p c t", t=128)
                cp("act", ao[:, 2 * hd:2 * hd + 2, t * 128:(t + 1) * 128], srcp, pskeys(P3),
                   [("ao", 2 * hd, t), ("ao", 2 * hd + 1, t)])
                yield
            W.done((i, "qkv", hd))
            W.done((i, "r", hd))

        for pair in ((0, 1), (2, 3)):
            gens = [head_gen(hd, k) for k, hd in enumerate(pair)]
            alive = list(gens)
            for gen in gens:
                next(gen)
            for _ in range(4):
                next(gens[0])
            while alive:
                for gi, gen in enumerate(list(alive)):
                    try:
                        next(gen)
                    except StopIteration:
                        alive.remove(gen)
        return ao

    for i in range(n_layers):
        kind = i % 3
        j = i // 3
        last_layer = (i == n_layers - 1)
        if kind == 0:
            ao = chunk_attention(i, j)
        elif kind == 1:
            ao = gla(i)
        else:
            ao = diff_attention(i, j)
        if ao is None:
            for t in range(NT):
                dma("sp", out_d[t * 128:(t + 1) * 128, :], h[:, t, :], [("h", t)], [("out", t)], "out")
            break
        out_proj_ln(i, ao, final=(last_layer and stop_mid))
        if not (last_layer and stop_mid):
            ffn(i, final=last_layer)

    ln_drain()
    S.fence("sp", [("out", t) for t in range(NT)])
    S.resolve()
    sems = {k: es.enter_context(nc.semaphore("s_%s_%s" % k)) for k in S.sem_keys()}
    with nc.Block() as block:
        S.emit(block, sems)
    es.close()
    return nc


def _t5_bucket_np(rel):
    nb = 16
    max_exact = 8
    ret = (rel > 0).astype(np.int32) * nb
    n = np.abs(rel)
    is_small = n < max_exact
    n_f = np.maximum(n, 1).astype(np.float32)
    large = max_exact + (np.log(n_f / np.float32(max_exact)) / np.float32(math.log(128 / max_exact))
                         * np.float32(nb - max_exact)).astype(np.int32)
    large = np.minimum(large, nb - 1)
    return ret + np.where(is_small, n, large)


def _prep_shared(inp):
    f = np.float32
    k_in = np.arange(128)[:, None]
    q_in = np.arange(128)[None, :]
    a_rel = np.asarray(inp["a_rel_bias"], f)
    a_bias = np.empty((2, 16, 128, 640), f)
    for jj in range(5):
        rel = q_in - k_in + (4 - jj) * 128
        idx = np.clip(rel, -128, 128) + 128
        vals = a_rel[:, idx, :]
        vals = np.transpose(vals, (0, 3, 1, 2))
        if jj == 4:
            mask = (k_in >= 64) & (q_in < 64)
        elif jj == 0:
            mask = (k_in < 64) & (q_in >= 64)
        else:
            mask = np.zeros((128, 128), bool)
        a_bias[:, :, :, jj * 128:(jj + 1) * 128] = np.where(mask[None, None], f(NEG), vals)
    t5 = np.asarray(inp["t5_table"], f)
    c_bias = np.empty((8, 128, 768), f)
    for bi, jd in enumerate([0, 1, 2, 2, 2, 2]):
        rel = (k_in - q_in - 128 * jd).astype(np.int32)
        vals = np.transpose(t5[_t5_bucket_np(rel)], (2, 0, 1))
        if jd == 0:
            vals = np.where(((k_in >= 64) & (q_in < 64))[None], f(NEG), vals)
        c_bias[:, :, bi * 128:(bi + 1) * 128] = vals
    assert (_t5_bucket_np(np.arange(-2048, -128)) == 15).all()
    lngb = np.stack([inp["ln1_g"], inp["ln1_b"], inp["ln2_g"], inp["ln2_b"]], axis=1).astype(f)
    lngb = np.ascontiguousarray(np.broadcast_to(lngb[:, :, None, :], (DEPTH, 4, 128, D)))
    lam = np.stack([inp["c_lam_q1"][0], inp["c_lam_k1"][0], inp["c_lam_q2"][0], inp["c_lam_k2"][0]], 0).astype(f)
    s = np.arange(128)
    U = ((s[:, None] // 64 == s[None, :] // 64) & (s[:, None] > s[None, :])).astype(f)
    return {
        "lngb": lngb,
        "ffn_gu": np.ascontiguousarray(inp["ffn_w_gate_up"], f),
        "ffn_d": np.ascontiguousarray(inp["ffn_w_down"], f),
        "a_qkv": np.ascontiguousarray(inp["a_w_qkv"], f),
        "a_o": np.ascontiguousarray(inp["a_w_o"], f),
        "a_bias": a_bias,
        "b_in": np.ascontiguousarray(inp["b_w_in"][0], f),
        "b_g1": np.ascontiguousarray(inp["b_w_g1"][0], f),
        "b_g2aug": np.ascontiguousarray(np.concatenate([inp["b_w_g2"][0], inp["b_b_g"][0][None, :]], 0), f),
        "b_gn": np.ascontiguousarray(np.broadcast_to(np.asarray(inp["b_g_norm"][0], f)[None, :], (128, 256))),
        "b_o": np.ascontiguousarray(inp["b_w_o"][0], f),
        "c_qkv": np.ascontiguousarray(inp["c_w_qkv"][0], f),
        "c_o": np.ascontiguousarray(inp["c_w_o"][0], f),
        "c_bias": c_bias,
        "c_cc": np.ascontiguousarray(np.broadcast_to(t5[15, :][None, :], (128, 8))),
        "c_gn": np.ascontiguousarray(np.broadcast_to(np.asarray(inp["c_g_norm"][0], f)[None, :], (128, 128))),
        "c_lam": np.ascontiguousarray(np.broadcast_to(lam[None], (128, 4, 64))),
        "k_ident": np.eye(128, dtype=f),
        "k_U": U,
    }


_NC_CACHE = {}


def kernel(_n_layers=DEPTH, _stop_mid=False, _cores=8, **inputs):
    inp = {k: np.asarray(v) for k, v in inputs.items()}
    shared = _prep_shared(inp)
    key = (_n_layers, _stop_mid)
    if key not in _NC_CACHE:
        _NC_CACHE[key] = build_program(_n_layers, _stop_mid)
    nc = _NC_CACHE[key]
    x = np.asarray(inp["x"], np.float32)
    in_maps = []
    for b in range(_cores):
        m = dict(shared)
        m["x"] = np.ascontiguousarray(x[b])
        in_maps.append(m)
    res = run_bass_kernel_spmd(nc, in_maps, core_ids=list(range(_cores)))
    return np.stack([res.results[b]["out"] for b in range(_cores)], axis=0)
```
